# Optimizing a Trainium2 kernel written in Bass

```python
import math
import jax
import jax.numpy as jnp
from jax import lax
import numpy as np

D_MODEL = 1024
BATCH = 16
SEQ = 256
DEPTH = 2
DEC_BATCH = 8
DEC_SEQ = 4096
PAST_LEN = 512

GRID_W = 64
N_EVEN = (DEPTH + 1) // 2
N_ODD = DEPTH // 2
EPS = 1e-6
NEG_INF = -1e30
QBLK = 128
ROPE_THETA = 10000.0

D_SSD = D_MODEL
SSD_HEADDIM = 64
SSD_HEADS = D_SSD // SSD_HEADDIM
SSD_GROUPS = 4
SSD_HPG = SSD_HEADS // SSD_GROUPS
SSD_STATE = 128
SSD_CHUNK = 128
D_CONV = 5
CONV_CH = D_SSD + 2 * SSD_GROUPS * SSD_STATE
D_POOL = D_MODEL
POOL_WINDOWS = (2, 4, 8, 16)
POOL_GROUP = D_POOL // len(POOL_WINDOWS)
IN_EVEN = D_SSD + CONV_CH + 2 * SSD_HEADS + 2 * D_POOL
DIFF_HEADS = 8
DIFF_HD = 64
W_DIFF = DIFF_HEADS * 2 * DIFF_HD
WIN_HEADS = 16
WIN_KV = 4
WIN_GROUP = WIN_HEADS // WIN_KV
WIN_HD = 64
WINDOW = 128
WIN_BLK = 128
W_WIN = WIN_HEADS * WIN_HD
IN_ODD = 4 * W_DIFF + 2 * W_WIN + 2 * WIN_KV * WIN_HD

kernel_name = 'hybrid_dit_ssd_pool_diffattn_swa_step'


def rmsnorm(x, g=None):
    xf = x.astype(jnp.float32)
    y = (xf * lax.rsqrt(jnp.mean(xf * xf, axis=-1, keepdims=True) + EPS)).astype(x.dtype)
    return y if g is None else y * g


def split_cols(u, sizes):
    offs = np.cumsum(sizes)[:-1].tolist()
    return jnp.split(u, offs, axis=-1)


def split_blocks(a, axis):
    t = a.shape[axis]
    a = a.reshape(a.shape[:axis] + (t // QBLK, QBLK) + a.shape[axis + 1:])
    return jnp.moveaxis(a, axis, 0)


def merge_blocks(a, axis):
    a = jnp.moveaxis(a, 0, axis)
    return a.reshape(a.shape[:axis] + (a.shape[axis] * a.shape[axis + 1],) + a.shape[axis + 2:])


def ada_mod(cvec, w, b):
    m = (jax.nn.silu(cvec) @ w + b)[:, None, :]
    return jnp.split(m, 3, axis=-1)


def modulated_norm(x, shift, scale):
    return rmsnorm(x) * (1 + scale) + shift


def axial_rope_tables(t_len, d):
    n_rows = t_len // GRID_W
    row = jnp.repeat(jnp.arange(n_rows), GRID_W).astype(jnp.float32)
    col = jnp.tile(jnp.arange(GRID_W), n_rows).astype(jnp.float32)
    nf = d // 4
    inv = ROPE_THETA ** (-jnp.arange(nf, dtype=jnp.float32) / nf)
    ang = jnp.stack([row[:, None] * inv, col[:, None] * inv], axis=1)
    return jnp.cos(ang), jnp.sin(ang)


def apply_axial_rope(x, cos, sin):
    nf = cos.shape[-1]
    shp = x.shape
    xr = x.reshape(shp[:-1] + (2, 2, nf))
    x1, x2 = xr[..., 0, :], xr[..., 1, :]
    c, s = cos.astype(x.dtype), sin.astype(x.dtype)
    out = jnp.stack([x1 * c - x2 * s, x2 * c + x1 * s], axis=-2)
    return out.reshape(shp)


def centred_dwconv(x, w, b):
    pad = D_CONV // 2
    y = lax.conv_general_dilated(x, w[:, None, :], window_strides=(1,), padding=[(pad, pad)],
                                 dimension_numbers=('NWC', 'WIO', 'NWC'),
                                 feature_group_count=x.shape[-1])
    return y + b


def ssd_scan(x, dt, A, Bm, Cm, h0):
    bsz, t_len = x.shape[:2]
    nc = t_len // SSD_CHUNK

    def chunk(a):
        return a.reshape((bsz, nc, SSD_CHUNK) + a.shape[2:])

    a = chunk(dt.astype(jnp.float32) * A.astype(jnp.float32))
    xdt = chunk(x * dt[..., None])
    Bc, Cc = chunk(Bm), chunk(Cm)
    a_cs = jnp.cumsum(a, axis=2)
    a_cs_t = jnp.moveaxis(a_cs, 2, -1)
    seg = a_cs_t[..., :, None] - a_cs_t[..., None, :]
    lower = jnp.tril(jnp.ones((SSD_CHUNK, SSD_CHUNK), dtype=bool))
    l_mat = jnp.exp(jnp.where(lower, seg, NEG_INF))
    cb = jnp.einsum('bclgn,bcsgn->bcgls', Cc, Bc)
    y_diag = jnp.einsum('bcgrls,bcsgrp->bclgrp', cb[:, :, :, None] * l_mat, xdt)
    decay_to_end = jnp.exp(a_cs[:, :, -1:] - a_cs)
    states = jnp.einsum('bclgn,bclgr,bclgrp->bcgrpn', Bc, decay_to_end, xdt)
    chunk_decay = jnp.exp(a_cs[:, :, -1])

    def step(h, inp):
        st, dec = inp
        return h * dec[..., None, None] + st, h

    h_final, h_prev = lax.scan(step, h0.astype(states.dtype),
                               (jnp.moveaxis(states, 1, 0), jnp.moveaxis(chunk_decay, 1, 0)))
    h_prev = jnp.moveaxis(h_prev, 0, 1)
    y_off = jnp.einsum('bclgn,bcgrpn,bclgr->bclgrp', Cc, h_prev, jnp.exp(a_cs))
    y = (y_diag + y_off).reshape(x.shape)
    return y.astype(x.dtype), h_final.astype(x.dtype)


def pool_mixer(xb, pool_w, pool_scale):
    bsz, t_len, _ = xb.shape
    ng = len(POOL_WINDOWS)
    xg = xb.reshape(bsz, t_len, ng, POOL_GROUP).astype(jnp.float32)
    cs = jnp.concatenate([jnp.zeros((bsz, 1, ng, POOL_GROUP), jnp.float32), jnp.cumsum(xg, axis=1)], axis=1)
    win = np.array(POOL_WINDOWS)
    left = win // 2
    right = win - 1 - left
    t = jnp.arange(t_len)
    lo = jnp.clip(t[None, :] - left[:, None], 0, t_len)
    hi = jnp.clip(t[None, :] + right[:, None] + 1, 0, t_len)
    gi = jnp.arange(ng)[:, None]
    s = cs[:, hi, gi] - cs[:, lo, gi]
    mean = s / (hi - lo).astype(jnp.float32)[None, :, :, None]
    pooled = (jnp.moveaxis(mean, 1, 2) - xg).astype(xb.dtype)
    y = jnp.einsum('btgc,gcd->btgd', pooled, pool_w).reshape(bsz, t_len, D_POOL)
    return y * pool_scale


def even_mixer(h, w_in, conv_w, conv_b, A_log, dt_bias, D_skip, norm_g, pool_w, pool_scale, w_out, h0_fwd, h0_bwd):
    bsz, t_len, _ = h.shape
    u = h @ w_in
    z_a, xbc, dt_raw, z_b, x_b = split_cols(u, [D_SSD, CONV_CH, 2 * SSD_HEADS, D_POOL, D_POOL])
    xbc = jax.nn.silu(centred_dwconv(xbc, conv_w, conv_b))
    xs, Bm, Cm = split_cols(xbc, [D_SSD, SSD_GROUPS * SSD_STATE, SSD_GROUPS * SSD_STATE])
    xs = xs.reshape(bsz, t_len, SSD_GROUPS, SSD_HPG, SSD_HEADDIM)
    Bm = Bm.reshape(bsz, t_len, SSD_GROUPS, SSD_STATE)
    Cm = Cm.reshape(bsz, t_len, SSD_GROUPS, SSD_STATE)
    dt = jax.nn.softplus(dt_raw.reshape(bsz, t_len, 2, SSD_GROUPS, SSD_HPG)
                         + dt_bias.reshape(2, SSD_GROUPS, SSD_HPG))
    A = -jnp.exp(A_log.reshape(2, SSD_GROUPS, SSD_HPG))

    def to_grp(s):
        return s.reshape(bsz, SSD_GROUPS, SSD_HPG, SSD_HEADDIM, SSD_STATE)

    def flip(a):
        return jnp.flip(a, axis=1)

    y_f, h_f = ssd_scan(xs, dt[:, :, 0], A[0], Bm, Cm, to_grp(h0_fwd))
    y_b, h_b = ssd_scan(flip(xs), flip(dt[:, :, 1]), A[1], flip(Bm), flip(Cm), to_grp(h0_bwd))
    y = y_f + flip(y_b) + xs * D_skip.reshape(SSD_GROUPS, SSD_HPG, 1)
    y_ssd = rmsnorm(y.reshape(bsz, t_len, D_SSD) * jax.nn.silu(z_a), norm_g)
    y_pool = pool_mixer(x_b, pool_w, pool_scale) * jax.nn.silu(z_b)
    out = jnp.concatenate([y_ssd, y_pool], axis=-1) @ w_out
    return (out,
            h_f.reshape(bsz, SSD_HEADS, SSD_HEADDIM, SSD_STATE),
            h_b.reshape(bsz, SSD_HEADS, SSD_HEADDIM, SSD_STATE))


def odd_project(h, w_in):
    bsz, t_len, _ = h.shape
    u = h @ w_in
    qc, kc, vc, zc, qd, kd, vd, zd = split_cols(
        u, [W_DIFF, W_DIFF, W_DIFF, W_DIFF, W_WIN, WIN_KV * WIN_HD, WIN_KV * WIN_HD, W_WIN])
    qc = qc.reshape(bsz, t_len, DIFF_HEADS, 2, DIFF_HD).transpose(0, 2, 3, 1, 4)
    kc = kc.reshape(bsz, t_len, DIFF_HEADS, 2, DIFF_HD).transpose(0, 2, 3, 1, 4)
    vc = vc.reshape(bsz, t_len, DIFF_HEADS, 2 * DIFF_HD).transpose(0, 2, 1, 3)
    qd = qd.reshape(bsz, t_len, WIN_KV, WIN_GROUP, WIN_HD).transpose(0, 2, 3, 1, 4)
    kd = kd.reshape(bsz, t_len, WIN_KV, WIN_HD).transpose(0, 2, 1, 3)
    vd = vd.reshape(bsz, t_len, WIN_KV, WIN_HD).transpose(0, 2, 1, 3)
    return qc, kc, vc, zc, qd, kd, vd, zd


def diff_lambda(lam_params, layer_idx):
    lam_init = 0.8 - 0.6 * math.exp(-0.3 * layer_idx)
    lq1, lk1, lq2, lk2 = lam_params[0], lam_params[1], lam_params[2], lam_params[3]
    lam = (jnp.exp(jnp.sum(lq1 * lk1).astype(jnp.float32))
           - jnp.exp(jnp.sum(lq2 * lk2).astype(jnp.float32)) + lam_init)
    return lam, lam_init


def diff_attention(q, k, v, lam):
    scale = DIFF_HD ** -0.5

    def block(qb):
        s = jnp.einsum('bhiqd,bhikd->bhiqk', qb, k).astype(jnp.float32) * scale
        p = jax.nn.softmax(s, axis=-1)
        attn = p[:, :, 0] - lam * p[:, :, 1]
        return jnp.einsum('bhqk,bhkv->bhqv', attn.astype(v.dtype), v)

    out = lax.map(block, split_blocks(q, axis=3))
    return merge_blocks(out, axis=2)


def sink_attention(q, k, v, sink):
    scale = WIN_HD ** -0.5
    sk = sink.astype(jnp.float32)[None, :, :, None, None]

    def block(qb):
        s = jnp.einsum('bkgqd,bkld->bkgql', qb, k).astype(jnp.float32) * scale
        logits = jnp.concatenate([s, jnp.broadcast_to(sk, s.shape[:-1] + (1,))], axis=-1)
        p = jax.nn.softmax(logits, axis=-1)[..., :-1]
        return jnp.einsum('bkgql,bkld->bkgqd', p.astype(v.dtype), v)

    out = lax.map(block, split_blocks(q, axis=3))
    return merge_blocks(out, axis=3)


def window_sink_attention(q, k, v, k_ctx, v_ctx, sink):
    t_len = q.shape[3]
    nb = t_len // WIN_BLK
    n_ctx = k_ctx.shape[2]
    scale = WIN_HD ** -0.5
    pad = ((0, 0), (0, 0), (WIN_BLK, WIN_BLK), (0, 0))
    k_pad = jnp.pad(k, pad)
    v_pad = jnp.pad(v, pad)
    sk = sink.astype(jnp.float32)[None, :, :, None, None]

    def block(i):
        start = i * WIN_BLK
        qb = lax.dynamic_slice_in_dim(q, start, WIN_BLK, axis=3)
        kw = lax.dynamic_slice_in_dim(k_pad, start, 3 * WIN_BLK, axis=2)
        vw = lax.dynamic_slice_in_dim(v_pad, start, 3 * WIN_BLK, axis=2)
        qpos = start + jnp.arange(WIN_BLK)
        kpos = start - WIN_BLK + jnp.arange(3 * WIN_BLK)
        valid = ((kpos[None, :] >= 0) & (kpos[None, :] < t_len)
                 & (jnp.abs(qpos[:, None] - kpos[None, :]) <= WINDOW))
        s_w = jnp.einsum('bkgqd,bkwd->bkgqw', qb, kw).astype(jnp.float32) * scale
        s_w = jnp.where(valid, s_w, NEG_INF)
        s_c = jnp.einsum('bkgqd,bkld->bkgql', qb, k_ctx).astype(jnp.float32) * scale
        logits = jnp.concatenate([s_c, s_w, jnp.broadcast_to(sk, s_c.shape[:-1] + (1,))], axis=-1)
        p = jax.nn.softmax(logits, axis=-1).astype(v.dtype)
        return (jnp.einsum('bkgql,bkld->bkgqd', p[..., :n_ctx], v_ctx)
                + jnp.einsum('bkgqw,bkwd->bkgqd', p[..., n_ctx:n_ctx + 3 * WIN_BLK], vw))

    out = lax.map(block, jnp.arange(nb))
    return merge_blocks(out, axis=3)


def odd_output(oc, od, zc, zd, subln_g, lam_init, w_out):
    bsz, t_len = zc.shape[:2]
    oc = rmsnorm(oc, subln_g) * (1 - lam_init)
    oc = oc.transpose(0, 2, 1, 3).reshape(bsz, t_len, W_DIFF)
    od = od.transpose(0, 3, 1, 2, 4).reshape(bsz, t_len, W_WIN)
    return jnp.concatenate([oc * jax.nn.silu(zc), od * jax.nn.silu(zd)], axis=-1) @ w_out


def odd_mixer_context(h, w_in, lam_params, subln_g, sink, w_out, layer_idx):
    bsz, t_len, _ = h.shape
    qc, kc, vc, zc, qd, kd, vd, zd = odd_project(h, w_in)
    lam, lam_init = diff_lambda(lam_params, layer_idx)
    oc = diff_attention(qc, kc, vc, lam)
    od = sink_attention(qd, kd, vd, sink.reshape(WIN_KV, WIN_GROUP))
    out = odd_output(oc, od, zc, zd, subln_g, lam_init, w_out)
    k_cache = kc.transpose(0, 1, 3, 2, 4).reshape(bsz, DIFF_HEADS, t_len, 2 * DIFF_HD)
    return out, k_cache, vc, kd, vd


def odd_mixer_latent(h, w_in, lam_params, subln_g, sink, w_out, ck, cv, wk, wv, layer_idx):
    bsz, t_len, _ = h.shape
    qc, kc, vc, zc, qd, kd, vd, zd = odd_project(h, w_in)
    cos_c, sin_c = axial_rope_tables(t_len, DIFF_HD)
    qc = apply_axial_rope(qc, cos_c, sin_c)
    kc = apply_axial_rope(kc, cos_c, sin_c)
    cos_w, sin_w = axial_rope_tables(t_len, WIN_HD)
    qd = apply_axial_rope(qd, cos_w, sin_w)
    kd = apply_axial_rope(kd, cos_w, sin_w)
    n_ctx = ck.shape[2]
    ck = ck.reshape(bsz, DIFF_HEADS, n_ctx, 2, DIFF_HD).transpose(0, 1, 3, 2, 4)
    k_all = jnp.concatenate([ck, kc], axis=3)
    v_all = jnp.concatenate([cv, vc], axis=2)
    lam, lam_init = diff_lambda(lam_params, layer_idx)
    oc = diff_attention(qc, k_all, v_all, lam)
    od = window_sink_attention(qd, kd, vd, wk, wv, sink.reshape(WIN_KV, WIN_GROUP))
    return odd_output(oc, od, zc, zd, subln_g, lam_init, w_out)


def setup_inputs(seed: int = 0) -> dict:
    key = jax.random.key(seed)
    ks = jax.random.split(key, 32)
    f32 = jnp.float32

    def nrm(k, shape, s):
        return jax.random.normal(k, shape, f32) * s

    inp = {}
    inp['x_prompt'] = nrm(ks[0], (BATCH, SEQ, D_MODEL), 1.0)
    inp['x_sample'] = nrm(ks[1], (DEC_BATCH, DEC_SEQ, D_MODEL), 1.0)
    inp['state_ssd_fwd'] = nrm(ks[2], (DEC_BATCH, N_EVEN, SSD_HEADS, SSD_HEADDIM, SSD_STATE), 0.5)
    inp['state_ssd_bwd'] = nrm(ks[3], (DEC_BATCH, N_EVEN, SSD_HEADS, SSD_HEADDIM, SSD_STATE), 0.5)
    inp['cache_diff_k'] = nrm(ks[4], (DEC_BATCH, N_ODD, DIFF_HEADS, PAST_LEN, 2 * DIFF_HD), 1.0)
    inp['cache_diff_v'] = nrm(ks[5], (DEC_BATCH, N_ODD, DIFF_HEADS, PAST_LEN, 2 * DIFF_HD), 1.0)
    inp['cache_win_k'] = nrm(ks[6], (DEC_BATCH, N_ODD, WIN_KV, PAST_LEN, WIN_HD), 1.0)
    inp['cache_win_v'] = nrm(ks[7], (DEC_BATCH, N_ODD, WIN_KV, PAST_LEN, WIN_HD), 1.0)
    inp['c'] = nrm(ks[8], (DEC_BATCH, D_MODEL), 1.0)
    inp['c_ctx'] = nrm(ks[9], (D_MODEL,), 1.0)
    inp['w_ada'] = nrm(ks[10], (DEPTH, D_MODEL, 3 * D_MODEL), 0.5 * D_MODEL ** -0.5)
    inp['b_ada'] = nrm(ks[11], (DEPTH, 3 * D_MODEL), 0.02)
    inp['ev_w_in'] = nrm(ks[12], (N_EVEN, D_MODEL, IN_EVEN), D_MODEL ** -0.5)
    inp['ev_conv_w'] = nrm(ks[13], (N_EVEN, D_CONV, CONV_CH), D_CONV ** -0.5)
    inp['ev_conv_b'] = nrm(ks[14], (N_EVEN, CONV_CH), 0.02)
    inp['ev_A_log'] = jnp.log(jax.random.uniform(ks[15], (N_EVEN, 2, SSD_HEADS), f32, 1.0, 16.0))
    dt0 = jnp.exp(jax.random.uniform(ks[16], (N_EVEN, 2, SSD_HEADS), f32, math.log(1e-3), math.log(1e-1)))
    inp['ev_dt_bias'] = dt0 + jnp.log(-jnp.expm1(-dt0))
    inp['ev_D'] = 1.0 + nrm(ks[17], (N_EVEN, SSD_HEADS), 0.1)
    inp['ev_norm_g'] = 1.0 + nrm(ks[18], (N_EVEN, D_SSD), 0.05)
    inp['ev_pool_w'] = nrm(ks[19], (N_EVEN, len(POOL_WINDOWS), POOL_GROUP, POOL_GROUP), POOL_GROUP ** -0.5)
    inp['ev_pool_scale'] = 1.0 + nrm(ks[20], (N_EVEN, D_POOL), 0.1)
    inp['ev_w_out'] = nrm(ks[21], (N_EVEN, D_SSD + D_POOL, D_MODEL), (D_SSD + D_POOL) ** -0.5)
    inp['od_w_in'] = nrm(ks[22], (N_ODD, D_MODEL, IN_ODD), D_MODEL ** -0.5)
    inp['od_lambda'] = nrm(ks[23], (N_ODD, 4, DIFF_HD), 0.1)
    inp['od_subln_g'] = 1.0 + nrm(ks[24], (N_ODD, 2 * DIFF_HD), 0.05)
    inp['od_sink'] = nrm(ks[25], (N_ODD, WIN_HEADS), 0.5)
    inp['od_w_out'] = nrm(ks[26], (N_ODD, W_DIFF + W_WIN, D_MODEL), (W_DIFF + W_WIN) ** -0.5)
    inp['final_norm_g'] = 1.0 + nrm(ks[27], (D_MODEL,), 0.05)
    return inp


def reference(x_prompt, x_sample, state_ssd_fwd, state_ssd_bwd, cache_diff_k, cache_diff_v,
              cache_win_k, cache_win_v, c, c_ctx, w_ada, b_ada,
              ev_w_in, ev_conv_w, ev_conv_b, ev_A_log, ev_dt_bias, ev_D, ev_norm_g,
              ev_pool_w, ev_pool_scale, ev_w_out,
              od_w_in, od_lambda, od_subln_g, od_sink, od_w_out, final_norm_g):
    x = x_prompt
    n_req = x.shape[0]
    ssd_f, ssd_b, diff_k, diff_v, win_k, win_v = [], [], [], [], [], []
    for l in range(DEPTH):
        j = l // 2
        shift, scale, gate = ada_mod(c_ctx[None, :], w_ada[l], b_ada[l])
        h = modulated_norm(x, shift, scale)
        if l % 2 == 0:
            zero_state = jnp.zeros((n_req, SSD_HEADS, SSD_HEADDIM, SSD_STATE), x.dtype)
            y, h_f, h_b = even_mixer(h, ev_w_in[j], ev_conv_w[j], ev_conv_b[j], ev_A_log[j], ev_dt_bias[j],
                                     ev_D[j], ev_norm_g[j], ev_pool_w[j], ev_pool_scale[j], ev_w_out[j],
                                     zero_state, zero_state)
            ssd_f.append(h_f)
            ssd_b.append(h_b)
        else:
            y, k_d, v_d, k_w, v_w = odd_mixer_context(h, od_w_in[j], od_lambda[j], od_subln_g[j],
                                                      od_sink[j], od_w_out[j], l)
            diff_k.append(k_d)
            diff_v.append(v_d)
            win_k.append(k_w)
            win_v.append(v_w)
        x = x + gate * y
    y_prompt = rmsnorm(x, final_norm_g)

    x = x_sample
    for l in range(DEPTH):
        j = l // 2
        shift, scale, gate = ada_mod(c, w_ada[l], b_ada[l])
        h = modulated_norm(x, shift, scale)
        if l % 2 == 0:
            y, _, _ = even_mixer(h, ev_w_in[j], ev_conv_w[j], ev_conv_b[j], ev_A_log[j], ev_dt_bias[j],
                                 ev_D[j], ev_norm_g[j], ev_pool_w[j], ev_pool_scale[j], ev_w_out[j],
                                 state_ssd_fwd[:, j], state_ssd_bwd[:, j])
        else:
            y = odd_mixer_latent(h, od_w_in[j], od_lambda[j], od_subln_g[j], od_sink[j], od_w_out[j],
                                 cache_diff_k[:, j], cache_diff_v[:, j], cache_win_k[:, j], cache_win_v[:, j], l)
        x = x + gate * y
    y_sample = rmsnorm(x, final_norm_g)

    new_ssd_fwd = jnp.stack(ssd_f, axis=1)
    new_ssd_bwd = jnp.stack(ssd_b, axis=1)
    new_diff_k = jnp.stack(diff_k, axis=1)
    new_diff_v = jnp.stack(diff_v, axis=1)
    new_win_k = jnp.stack(win_k, axis=1)
    new_win_v = jnp.stack(win_v, axis=1)
    return (y_prompt, y_sample, new_ssd_fwd, new_ssd_bwd, new_diff_k, new_diff_v, new_win_k, new_win_v)
```

```python
import numpy as np
import concourse.bass as bass
import concourse.mybir as mybir

F32 = mybir.dt.float32
BF16 = mybir.dt.bfloat16
AF = mybir.ActivationFunctionType
ALU = mybir.AluOpType

ENGS = ("pe", "act", "dve", "pool", "sp")


class Op:
    __slots__ = ("eng", "fn", "deps", "signal", "sigval", "is_dma", "sem", "seq", "is_barrier")

    def __init__(self, eng, fn, is_dma=False):
        self.eng = eng
        self.fn = fn
        self.deps = set()
        self.signal = False
        self.sigval = 0
        self.is_dma = is_dma
        self.sem = None
        self.seq = 0
        self.is_barrier = False


class Res:
    _uid = 0

    def __init__(self, name):
        self.name = name
        self.lastw = {}
        self.readers = {}
        Res._uid += 1
        self.uid = Res._uid

    def _writers(self, key):
        if key is None:
            return list(self.lastw.values())
        out = []
        if key in self.lastw:
            out.append(self.lastw[key])
        if None in self.lastw:
            out.append(self.lastw[None])
        return out

    def read(self, op, key):
        for w in self._writers(key):
            op.deps.add(w)
        self.readers.setdefault(key, []).append(op)

    @staticmethod
    def _add_readers(op, lst):
        last = {}
        for r in lst:
            if r.is_dma:
                op.deps.add(r)
            else:
                p = last.get(r.eng)
                if p is None or r.seq > p.seq:
                    last[r.eng] = r
        for r in last.values():
            op.deps.add(r)

    def write(self, op, key):
        for w in self._writers(key):
            op.deps.add(w)
        if key is None:
            for lst in self.readers.values():
                self._add_readers(op, lst)
            self.lastw = {None: op}
            self.readers = {}
        else:
            self._add_readers(op, self.readers.get(key, ()))
            self._add_readers(op, self.readers.get(None, ()))
            self.lastw[key] = op
            self.readers[key] = []


class V:
    __slots__ = ("res", "key", "ap")

    def __init__(self, res, key, ap):
        self.res = res
        self.key = key
        self.ap = ap

    def __getitem__(self, idx):
        return V(self.res, self.key, self.ap[idx])

    def m(self, f):
        return V(self.res, self.key, f(self.ap))


class T:
    def __init__(self, handle, name, space="sb"):
        self.h = handle
        self.res = Res(name)
        self.res.space = space
        self.name = name

    def k(self, key=None):
        return V(self.res, key, self.h.ap())

    def __getitem__(self, idx):
        return V(self.res, None, self.h.ap()[idx])


class KB:
    def __init__(self):
        self.nc = bass.Bass("TRN2", target_bir_lowering=False)
        self.ops = []
        self.sb_lo = 16512
        self.sb_hi = 229344
        self.sb_ptr = self.sb_lo
        self.sb_stack = []
        self.nid = 0
        self.dma_sems = {}
        self.dma_last = {}
        self.dma_cnt = {}
        self.final_deps = []
        self.last_op = {}

    def sb(self, name, shape, dtype):
        nbytes = int(np.prod(shape[1:])) * (4 if dtype == F32 else 2)
        off = (self.sb_ptr + 31) // 32 * 32
        assert off + nbytes <= self.sb_hi, f"SBUF overflow allocating {name}: {off}+{nbytes}"
        self.sb_ptr = off + nbytes
        self.nid += 1
        h = self.nc.alloc_sbuf_tensor_at(f"{name}_{self.nid}", list(shape), dtype, offset=off)
        return T(h, name)

    def mark(self):
        self.sb_stack.append(self.sb_ptr)

    def release(self):
        self.barrier()
        self.sb_ptr = self.sb_stack.pop()

    def psum(self, name, shape, dtype=F32):
        h = self.nc.alloc_psum_tensor(name, list(shape), dtype)
        return T(h, name, "ps")

    def dram(self, name, shape, dtype, kind=None):
        if kind is None:
            h = self.nc.dram_tensor(name, list(shape), dtype)
        else:
            h = self.nc.dram_tensor(name, list(shape), dtype, kind=kind)
        return T(h, name, "dram")

    def _reg(self, op, reads, writes):
        for v in reads:
            v.res.read(op, v.key)
        for v in writes:
            v.res.write(op, v.key)
        op.deps.discard(op)
        op.seq = len(self.ops)
        self.ops.append(op)
        self.last_op[op.eng] = op

    def I(self, eng, meth, wr=("out",), extra_r=(), extra_w=(), **kw):
        reads, writes = list(extra_r), list(extra_w)
        args = {}
        for k_, v in kw.items():
            if isinstance(v, V):
                args[k_] = v.ap
                if k_ in wr or k_ == "accum_out":
                    writes.append(v)
                else:
                    reads.append(v)
            else:
                args[k_] = v

        def fn(e, meth=meth, args=args):
            return getattr(e, meth)(**args)

        op = Op(eng, fn)
        self._reg(op, reads, writes)
        return op

    def mm(self, out, lhsT, rhs, start=True, stop=True, **kw):
        args = dict(start=start, stop=stop, **kw)

        def fn(e, o=out.ap, l=lhsT.ap, r=rhs.ap, args=args):
            return e.matmul(o, l, r, **args)

        op = Op("pe", fn)
        self._reg(op, [lhsT, rhs], [out])
        return op

    def tr(self, out, in_, ident):
        def fn(e, o=out.ap, i=in_.ap, d=ident.ap):
            return e.transpose(o, i, d)

        op = Op("pe", fn)
        self._reg(op, [in_, ident], [out])
        return op

    def dma(self, out, in_, eng="sp", semkey=None, final=False):
        def fn(e, o=out.ap, i=in_.ap):
            return e.dma_start(out=o, in_=i)

        op = Op(eng, fn, is_dma=True)
        if semkey is None:
            side = out if out.res.space != "dram" else in_
            semkey = ("t", side.res.uid, side.key)
        op.sem = semkey
        prev = self.dma_last.get(semkey)
        if prev is not None:
            op.deps.add(prev)
        self.dma_last[semkey] = op
        self._reg(op, [in_], [out])
        if final:
            self.final_deps.append(op)
        return op

    def barrier(self):
        lasts = [o for o in self.last_op.values()] + list(self.dma_last.values())
        for e in ENGS:
            op = Op(e, None)
            op.is_barrier = True
            for l in lasts:
                op.deps.add(l)
            op.seq = len(self.ops)
            self.ops.append(op)
            self.last_op[e] = op

    def finish(self, same_eng_sync=True):
        nc = self.nc
        fin = Op("sp", None)
        for d in self.final_deps:
            fin.deps.add(d)
        fin.seq = len(self.ops)
        self.ops.append(fin)
        def needs_sig(op, d):
            if d.fn is None:
                return False
            if d.is_dma:
                return True
            if d.eng == op.eng and not op.is_dma and (op.eng == "pe" or not same_eng_sync):
                return False
            return True
        for op in self.ops:
            op.deps.discard(op)
            for d in op.deps:
                if needs_sig(op, d):
                    d.signal = True
        def expand(op):
            out = set()
            stack = list(op.deps)
            while stack:
                d = stack.pop()
                if d.fn is None:
                    stack.extend(d.deps)
                else:
                    out.add(d)
            return out
        for op in self.ops:
            if any(d.fn is None for d in op.deps):
                op.deps = expand(op)
                for d in op.deps:
                    if needs_sig(op, d):
                        d.signal = True
        esem = {e: nc.alloc_semaphore(f"s_{e}") for e in ("pe", "act", "dve", "pool")}
        ecnt = {e: 0 for e in esem}
        dsem = {}
        active = {}
        free = {False: [], True: []}
        nslots = [0]
        for op in self.ops:
            if op.fn is None:
                if op.is_barrier and op.eng == "pe":
                    for key, slot in active.items():
                        free[slot[2]].append(slot)
                    active = {}
                continue
            if op.is_dma:
                sw = (op.eng == "pool")
                if op.sem not in active:
                    if free[sw]:
                        slot = free[sw].pop()
                    else:
                        slot = [nc.alloc_semaphore(f"d{nslots[0]}"), 0, sw]
                        nslots[0] += 1
                    active[op.sem] = slot
                slot = active[op.sem]
                assert slot[2] == sw, f"semkey {op.sem} mixes SW and HW DGE"
                slot[1] += 16
                op.sigval = slot[1]
                op.signal = True
                op.sem = ("slot", id(slot), op.seq)
                dsem[op.sem] = slot[0]
            elif op.signal:
                ecnt[op.eng] += 1
                op.sigval = ecnt[op.eng]
        self._slots_keepalive = (active, free)
        self.sem_stats = (dict(ecnt), nslots[0], max([sl[1] for sl in free[False] + free[True] + list(active.values())] + [0]))
        self.n_dma_sems = nslots[0]
        streams = {e: [o for o in self.ops if o.eng == e] for e in ENGS}
        engobj = {"pe": "tensor", "act": "scalar", "dve": "vector", "pool": "gpsimd", "sp": "sync"}

        def emit(e, eng):
            known = {}
            for op in streams[e]:
                for d in sorted(op.deps, key=lambda o: o.seq):
                    if d.is_dma:
                        sem, val = dsem[d.sem], d.sigval
                    else:
                        if d.eng == e and not op.is_dma:
                            if e == "pe" or not same_eng_sync:
                                continue
                        sem, val = esem[d.eng], d.sigval
                    kk = id(sem)
                    if known.get(kk, 0) >= val:
                        continue
                    known[kk] = val
                    eng.wait_ge(sem, val)
                if op.fn is None:
                    continue
                inst = op.fn(eng)
                if op.is_dma:
                    inst.then_inc(dsem[op.sem], 16)
                elif op.signal:
                    inst.then_inc(esem[op.eng], 1)

        with nc.Block() as block:
            @block.tensor
            def _(eng):
                emit("pe", eng)

            @block.scalar
            def _(eng):
                emit("act", eng)

            @block.vector
            def _(eng):
                emit("dve", eng)

            @block.gpsimd
            def _(eng):
                emit("pool", eng)

            @block.sync
            def _(eng):
                emit("sp", eng)
        return nc
from concourse.bass_utils import run_bass_kernel_spmd
import math
import ml_dtypes

NTOK = 4608
NT = 36
D = 1024
EPS = 1e-6
SEQS = [(0, 4096, 0), (4096, 256, 1), (4352, 256, 1)]
SEGS = [(b * 512, 512) for b in range(8)] + [(4096, 256), (4352, 256)]
LAM_INIT = 0.8 - 0.6 * math.exp(-0.3 * 1)


def seq_of(tok):
    return 0 if tok < 4096 else (1 if tok < 4352 else 2)


def scol(tok):
    return tok + 2 + 4 * seq_of(tok)


def pcol(tok):
    return tok + 8 + 16 * seq_of(tok)


class Builder:
    def __init__(self, debug=False, stop_after=None):
        self.k = KB()
        self.debug = debug
        self.stop_after = stop_after
        self.dbg_outs = []
        self.build()

    def din(self, name, shape, dtype=F32):
        return self.k.dram(name, shape, dtype, kind="ExternalInput")

    def dout(self, name, shape, dtype=F32):
        return self.k.dram(name, shape, dtype, kind="ExternalOutput")

    def scratch(self, name, shape, dtype):
        if self.debug:
            self.dbg_outs.append(name)
            return self.k.dram(name, shape, dtype, kind="ExternalOutput")
        return self.k.dram(name, shape, dtype)

    def dump(self, name, view, shape, dtype=F32):
        if not self.debug:
            return
        t = self.k.dram("dbg_" + name, list(shape), dtype, kind="ExternalOutput")
        self.dbg_outs.append("dbg_" + name)
        self.k.dma(t.k(), view, eng="sp", semkey=("dbg", name))

    def next_ps(self):
        self.ps_i = (self.ps_i + 1) % len(self.PS)
        return self.PS[self.ps_i]

    def build(self):
        k = self.k
        self.x_all = self.din("x_all", [NTOK, D])
        self.cc = self.din("cc", [128, 16])
        self.w_ada = self.din("w_ada", [2, 128, 8, 3072])
        self.b_ada_col = self.din("b_ada_col", [2, 128, 24])
        self.b_ada_row = self.din("b_ada_row", [2, 3072])
        self.ev_w_in = self.din("ev_w_in", [128, 8, 5152])
        self.conv_w = self.din("conv_w", [128, 16, 5])
        self.conv_b = self.din("conv_b", [128, 16])
        self.a_log = self.din("a_log", [1, 32])
        self.dt_bias = self.din("dt_bias", [1, 32])
        self.d_skip = self.din("d_skip", [1, 16])
        self.norm_g = self.din("norm_g", [128, 8])
        self.pool_scale = self.din("pool_scale", [128, 8])
        self.pool_w = self.din("pool_w", [4, 128, 2, 256])
        self.ev_w_out = self.din("ev_w_out", [128, 16, 1024])
        self.od_w_in = self.din("od_w_in", [128, 8, 6656])
        self.od_w_out = self.din("od_w_out", [128, 16, 1024])
        self.od_lambda = self.din("od_lambda", [1, 256])
        self.subln_g = self.din("subln_g", [128, 1])
        self.sink = self.din("sink", [1, 16])
        self.fnorm_g = self.din("fnorm_g", [1, 1024])
        self.hf0 = self.din("hf0", [128, 1024])
        self.hb0 = self.din("hb0", [128, 1024])
        self.dk_T = self.din("dk_T", [8, 128, 512])
        self.dv = self.din("dv", [8, 512, 128])
        self.wk_T = self.din("wk_T", [4, 64, 512])
        self.wv = self.din("wv", [4, 512, 64])
        self.c_ident = self.din("c_ident", [128, 128])
        self.c_triu = self.din("c_triu", [128, 128])
        self.c_tril = self.din("c_tril", [128, 128])
        self.c_nmf = self.din("c_nmf", [128, 128])
        self.c_nmb = self.din("c_nmb", [128, 128])
        self.c_sel = self.din("c_sel", [64, 32, 128])
        self.c_pm = self.din("c_pm", [128, 128])
        self.c_cos = self.din("c_cos", [128, 4096])
        self.c_sin = self.din("c_sin", [128, 4096])
        self.c_pedge = self.din("c_pedge", [128, 4, 16])
        self.y_all = self.dout("y_all", [NTOK, D])
        self.o_ssdf = self.dout("o_ssdf", [2, 1024, 128])
        self.o_ssdb = self.dout("o_ssdb", [2, 1024, 128])
        self.o_dk = self.dout("o_dk", [512, 1024])
        self.o_dv = self.dout("o_dv", [512, 1024])
        self.o_wk = self.dout("o_wk", [512, 256])
        self.o_wv = self.dout("o_wv", [512, 256])
        self.sza = self.scratch("sza", [NTOK, 1024], F32)
        self.xcT = self.scratch("xcT", [2048, NTOK], BF16)
        self.ypT = self.scratch("ypT", [1024, NTOK], BF16)
        self.x1 = self.scratch("x1", [NTOK, D], F32)
        self.yloc = self.scratch("yloc", [NTOK, 1024], F32)
        self.QcT = self.scratch("QcT", [1024, NTOK], BF16)
        self.KcT = self.scratch("KcT", [1024, NTOK], BF16)
        self.Vc = self.scratch("Vc", [NTOK, 1024], BF16)
        self.szcT = self.scratch("szcT", [1024, NTOK], F32)
        self.QdT = self.scratch("QdT", [1024, NTOK], BF16)
        self.KdT = self.scratch("KdT", [256, NTOK], BF16)
        self.Vd = self.scratch("Vd", [NTOK, 256], BF16)
        self.szdT = self.scratch("szdT", [1024, NTOK], F32)
        self.ocT = self.scratch("ocT", [1024, NTOK], BF16)
        self.odT = self.scratch("odT", [1024, NTOK], BF16)
        self.yoff = [self.scratch(f"yoff{d}", [NTOK, 1024], F32) for d in range(2)]
        self.xw_s = [self.scratch(f"xw{d}", [NTOK, 1024], BF16) for d in range(2)]
        self.btok_s = self.scratch("btok", [NTOK, 512], BF16)

        self.PS = [k.psum(f"ps{i}", [128, 512], F32) for i in range(8)]
        self.ps_i = -1

        self.ident = k.sb("ident", [128, 128], F32)
        self.identb = k.sb("identb", [128, 128], BF16)
        self.epsc = k.sb("epsc", [128, 1], F32)
        self.modT = [k.sb(f"modT{l}", [128, 24, 2], F32) for l in range(2)]
        self.gate_bc = [[k.sb(f"gate{l}{c}", [128, 1024], F32) for c in range(2)] for l in range(2)]
        self.dt_all = k.sb("dt_all", [128, NT, 32], F32)
        self.e_all = k.sb("e_all", [128, NT, 32], F32)
        self.dec_all = k.sb("dec_all", [128, NT, 32], F32)
        k.dma(self.ident.k(), self.c_ident.k())
        k.dma(self.identb.k(), self.c_ident.k(), eng="pool")
        k.I("dve", "memset", wr=("ap",), ap=self.epsc.k(), constant=EPS)

        self.phase0()
        for l in range(2):
            self.dump(f"modT{l}", self.modT[l].k(), [128, 24, 2])
            for c in range(2):
                self.dump(f"gate{l}{c}", self.gate_bc[l][c].k(), [128, 1024])
        if self.stop_after == "p0":
            return self.finish_debug()
        k.mark()
        self.hT = k.sb("hT", [128, 8, NTOK], BF16)
        self.build_hT(self.x_all, 0)
        self.dump("hT", self.hT.k(), [128, 8, NTOK], BF16)
        self.passA1()
        self.dump("dt_all", self.dt_all.k(), [128, NT, 32])
        if self.stop_after == "A1":
            return self.finish_debug()
        self.passA2()
        k.release()
        if self.stop_after == "A2":
            return self.finish_debug()
        self.passB()
        if self.stop_after == "B":
            return self.finish_debug()
        self.passC()
        if self.stop_after == "C":
            return self.finish_debug()
        self.passD()
        if self.stop_after == "D":
            return self.finish_debug()
        if self.stop_after == "L1only":
            pass
        k.mark()
        self.hT = k.sb("hT", [128, 8, NTOK], BF16)
        self.build_hT(self.x1, 1)
        if self.stop_after == "hT1":
            return self.finish_debug()
        self.passE()
        k.release()
        if self.stop_after in ("E", "E00", "E0", "E0a", "E0b", "E1", "E2"):
            return self.finish_debug()
        self.passF()
        if self.stop_after == "F":
            return self.finish_debug()
        self.passG()
        if self.stop_after in ("G",):
            return self.finish_debug()
        self.passH()
        self.nc = k.finish()

    def finish_debug(self):
        k = self.k
        self.nc = k.finish()

    def phase0(self):
        k = self.k
        k.mark()
        cc_sb = k.sb("cc_sb", [128, 16], F32)
        sc = k.sb("sc", [128, 16], F32)
        scb = k.sb("scb", [128, 16, 128], F32)
        wada = k.sb("wada", [128, 8, 3072], F32)
        bcol = k.sb("bcol", [128, 24], F32)
        brow = k.sb("brow", [128, 1024], F32)
        k.dma(cc_sb.k(), self.cc.k())
        k.I("act", "activation", out=sc.k(), in_=cc_sb.k(), func=AF.Silu)
        k.I("dve", "tensor_copy", out=scb.k(), in_=sc.k().m(lambda a: a.unsqueeze(2).to_broadcast([128, 16, 128])))
        for l in range(2):
            for q in range(6):
                k.dma(wada.k(("q", q))[:, :, q * 512:(q + 1) * 512],
                      self.w_ada.k()[l, :, :, q * 512:(q + 1) * 512])
            k.dma(bcol.k(), self.b_ada_col.k()[l])
            k.dma(brow.k(), self.b_ada_row.k()[l:l + 1, 2048:3072].m(lambda a: a.partition_broadcast(128)))
            ps = self.next_ps()
            for j in range(24):
                q = j // 4
                for kk in range(8):
                    k.mm(ps.k()[:, 2 * j:2 * j + 2], wada.k(("q", q))[:, kk, j * 128:(j + 1) * 128],
                         sc.k()[:, kk:16:8], start=(kk == 0), stop=(kk == 7))
            k.I("dve", "tensor_tensor", out=self.modT[l].k(),
                in0=ps.k()[:, 0:48].m(lambda a: a.rearrange("p (j c) -> p j c", c=2)),
                in1=bcol.k().m(lambda a: a.unsqueeze(2).to_broadcast([128, 24, 2])), op=ALU.add)
            k.I("dve", "tensor_scalar", out=self.modT[l].k()[:, 8:16, :], in0=self.modT[l].k()[:, 8:16, :],
                scalar1=1.0, scalar2=None, op0=ALU.add)
            for cond in range(2):
                for n in range(2):
                    ps2 = self.next_ps()
                    for kk in range(8):
                        k.mm(ps2.k(), scb.k()[:, cond * 8 + kk, :],
                             wada.k(("q", 4 + n))[:, kk, 2048 + n * 512:2048 + (n + 1) * 512],
                             start=(kk == 0), stop=(kk == 7))
                    k.I("dve", "tensor_tensor", out=self.gate_bc[l][cond].k()[:, n * 512:(n + 1) * 512],
                        in0=ps2.k(), in1=brow.k()[:, n * 512:(n + 1) * 512], op=ALU.add)
        k.release()

    def build_hT(self, xsrc, l):
        k = self.k
        k.mark()
        xr = [k.sb(f"xr{i}", [128, 1024], F32) for i in range(2)]
        xn = [k.sb(f"xn{i}", [128, 1024], F32) for i in range(2)]
        sq = k.sb("sq", [128, 1024], F32)
        ss = [k.sb(f"ss{i}", [128, 1], F32) for i in range(2)]
        rs = [k.sb(f"rs{i}", [128, 1], F32) for i in range(2)]
        rstd = [k.sb(f"rstd{i}", [128, 1], F32) for i in range(2)]
        xr.append(k.sb("xr2", [128, 1024], F32))
        pss = {}

        def stage_a(i):
            b = i % 2
            b3 = i % 3
            k.dma(xr[b3].k(), xsrc.k()[i * 128:(i + 1) * 128, :])
            k.I("act", "activation", out=sq.k(), in_=xr[b3].k(), func=AF.Square, accum_out=ss[b].k())
            k.I("act", "activation", out=rs[b].k(), in_=ss[b].k(), func=AF.Sqrt, scale=1.0 / D, bias=self.epsc.k())
            k.I("dve", "reciprocal", out=rstd[b].k(), in_=rs[b].k())
            k.I("dve", "tensor_scalar", out=xn[b].k(), in0=xr[b3].k(), scalar1=rstd[b].k(), scalar2=None, op0=ALU.mult)
            lst = []
            for half in range(2):
                ps = self.next_ps()
                for q in range(4):
                    c = half * 4 + q
                    k.tr(ps.k()[:, q * 128:(q + 1) * 128], xn[b].k()[:, c * 128:(c + 1) * 128], self.ident.k())
                lst.append(ps)
            pss[i] = lst

        def stage_b(i):
            cond = 0 if i < 32 else 1
            for half in range(2):
                ps = pss[i][half]
                for q in range(4):
                    c = half * 4 + q
                    o = self.hT.k(("t", i))[:, c, i * 128:(i + 1) * 128]
                    sc_ = self.modT[l].k()[:, 8 + c, cond:cond + 1]
                    sh_ = self.modT[l].k()[:, c, cond:cond + 1]
                    if half == 0:
                        k.I("act", "activation", out=o, in_=ps.k()[:, q * 128:(q + 1) * 128],
                            func=AF.Identity, scale=sc_, bias=sh_)
                    else:
                        k.I("dve", "tensor_scalar", out=o, in0=ps.k()[:, q * 128:(q + 1) * 128],
                            scalar1=sc_, scalar2=sh_, op0=ALU.mult, op1=ALU.add)
            del pss[i]

        stage_a(0)
        for i in range(NT):
            if i + 1 < NT:
                stage_a(i + 1)
            stage_b(i)
        k.release()

    def hT_cols(self, t0, n):
        return [self.hT.k(("t", i)) for i in range(t0 // 128, (t0 + n + 127) // 128)]

    def passA1(self):
        k = self.k
        k.mark()
        wfm = [k.sb(f"wfm{i}", [128, 8, 128], BF16) for i in range(4)]
        wtm = [k.sb(f"wtm{i}", [128, 8, 512], BF16) for i in range(2)]
        wdt = k.sb("wdt", [128, 8, 32], BF16)
        strips = [k.sb(f"strip{i}", [128, 4620], BF16) for i in range(2)]
        dg = k.sb("dg", [128, 16, 5, 128], BF16)
        cw = k.sb("cw", [128, 16, 5], F32)
        cb = k.sb("cb", [128, 16], F32)
        stg = [k.sb(f"stg{i}", [128, 512], F32) for i in range(3)]
        stgc = [k.sb(f"stgc{i}", [128, 512], BF16) for i in range(3)]
        dtraw = k.sb("dtraw", [128, NT, 32], F32)
        dtb = k.sb("dtb", [128, 32], F32)
        hT = self.hT

        k.dma(cw.k(), self.conv_w.k())
        k.dma(cb.k(), self.conv_b.k())
        k.dma(dtb.k(), self.dt_bias.k().m(lambda a: a.partition_broadcast(128)))
        for j in range(16):
            k.I("dve", "tensor_tensor", out=dg.k(("j", j))[:, j, :, :],
                in0=self.ident.k().m(lambda a: a.unsqueeze(1).to_broadcast([128, 5, 128])),
                in1=cw.k()[:, j, :].m(lambda a: a.unsqueeze(2).to_broadcast([128, 5, 128])), op=ALU.mult)
        for s in strips:
            k.I("pool", "memset", wr=("ap",), ap=s.k(), constant=0.0)

        k.dma(wdt.k(), self.ev_w_in.k()[:, :, 3072:3104], eng="pool")
        for i in range(NT):
            ps = self.next_ps()
            for kk in range(8):
                k.mm(ps.k()[:, 0:32], hT.k(("t", i))[:, kk, i * 128:(i + 1) * 128], wdt.k()[:, kk, :],
                     start=(kk == 0), stop=(kk == 7))
            k.I("dve", "tensor_tensor", out=dtraw.k(("t", i))[:, i, :], in0=ps.k()[:, 0:32], in1=dtb.k(), op=ALU.add)
        k.I("dve", "tensor_scalar", out=dtraw.k(), in0=dtraw.k(), scalar1=30.0, scalar2=None, op0=ALU.min)
        k.I("act", "activation", out=dtraw.k(), in_=dtraw.k(), func=AF.Exp)
        k.I("act", "activation", out=self.dt_all.k(), in_=dtraw.k(), func=AF.Ln, bias=1.0)

        n_st = 0
        for fb in range(2):
            W = wtm[fb % 2]
            k.dma(W.k(), self.ev_w_in.k()[:, :, fb * 512:(fb + 1) * 512], eng="pool")
            for i in range(NT):
                ps = self.next_ps()
                for kk in range(8):
                    k.mm(ps.k(), hT.k(("t", i))[:, kk, i * 128:(i + 1) * 128], W.k()[:, kk, :],
                         start=(kk == 0), stop=(kk == 7))
                st = stg[n_st % 3]
                n_st += 1
                k.I("act", "activation", out=st.k(), in_=ps.k(), func=AF.Silu)
                k.dma(self.sza.k(("t", i, fb))[i * 128:(i + 1) * 128, fb * 512:(fb + 1) * 512], st.k())

        n_sc = 0
        n_ev = 0
        for j in range(16):
            W = wfm[j % 4]
            k.dma(W.k(), self.ev_w_in.k()[:, :, 1024 + j * 128:1024 + (j + 1) * 128], eng="pool")
            strip = strips[j % 2]
            for si, (t0, n) in enumerate(SEGS):
                ps = self.next_ps()
                for kk in range(8):
                    k.mm(ps.k()[:, 0:n], W.k()[:, kk, :], hT.k()[:, kk, t0:t0 + n],
                         start=(kk == 0), stop=(kk == 7))
                c0 = scol(t0)
                if n_ev % 2 == 0:
                    k.I("act", "activation", out=strip.k(("s", si))[:, c0:c0 + n], in_=ps.k()[:, 0:n], func=AF.Copy)
                else:
                    k.I("dve", "tensor_copy", out=strip.k(("s", si))[:, c0:c0 + n], in_=ps.k()[:, 0:n])
                n_ev += 1
            for si, (t0, n) in enumerate(SEGS):
                ps = self.next_ps()
                c0 = scol(t0)
                nb = [strip.k(("s", s2)) for s2 in (si - 1, si + 1) if 0 <= s2 < len(SEGS)]
                for tap in range(5):
                    op = k.mm(ps.k()[:, 0:n], dg.k(("j", j))[:, j, tap, :],
                              strip.k(("s", si))[:, c0 + tap - 2:c0 + tap - 2 + n],
                              start=(tap == 0), stop=(tap == 4))
                    for v in nb:
                        v.res.read(op, v.key)
                st = stgc[n_sc % 3]
                n_sc += 1
                k.I("act", "activation", out=st.k()[:, 0:n], in_=ps.k()[:, 0:n], func=AF.Silu, bias=cb.k()[:, j:j + 1])
                k.dma(self.xcT.k(("j", j, si))[j * 128:(j + 1) * 128, t0:t0 + n], st.k()[:, 0:n])
        k.release()

    def passA2(self):
        k = self.k
        hT = self.hT
        k.mark()
        PW = NTOK + 48
        wfm = [k.sb(f"wfm{i}", [128, 8, 128], BF16) for i in range(4)]
        xb = [k.sb(f"xb{i}", [128, PW], F32) for i in range(2)]
        pl = [k.sb(f"pl{i}", [128, NTOK], BF16) for i in range(2)]
        szb = k.sb("szb", [128, NTOK], F32)
        tA = [k.sb(f"tA{i}", [128, 528], F32) for i in range(2)]
        tB = [k.sb(f"tB{i}", [128, 528], F32) for i in range(2)]
        te = [k.sb(f"te{i}", [128, 8], F32) for i in range(2)]
        pw = k.sb("pw", [128, 4, 2, 256], BF16)
        psc = k.sb("psc", [128, 8], F32)
        pedge = k.sb("pedge", [128, 4, 16], F32)
        stg = [k.sb(f"stgp{i}", [128, 512], BF16) for i in range(3)]
        k.dma(pw.k(), self.pool_w.k().m(lambda a: a.rearrange("g p c d -> p g c d")), eng="pool")
        k.dma(psc.k(), self.pool_scale.k())
        k.dma(pedge.k(), self.c_pedge.k())
        for t in xb:
            k.I("pool", "memset", wr=("ap",), ap=t.k(), constant=0.0)
        nw = 0
        nev = 0
        nst = 0
        seq_starts = {t0 for (t0, T, c) in SEQS}
        seq_ends = {t0 + T for (t0, T, c) in SEQS}
        for g in range(4):
            w = (2, 4, 8, 16)[g]
            levels = g + 1
            for cc in range(2):
                W = wfm[nw % 4]
                nw += 1
                f0 = 4128 + g * 256 + cc * 128
                k.dma(W.k(), self.ev_w_in.k()[:, :, f0:f0 + 128], eng="pool")
                for si, (t0, n) in enumerate(SEGS):
                    ps = self.next_ps()
                    for kk in range(8):
                        k.mm(ps.k()[:, 0:n], W.k()[:, kk, :], hT.k()[:, kk, t0:t0 + n], start=(kk == 0), stop=(kk == 7))
                    c0 = pcol(t0)
                    if nev % 2 == 0:
                        k.I("act", "activation", out=xb[cc].k(("s", si))[:, c0:c0 + n], in_=ps.k()[:, 0:n], func=AF.Copy)
                    else:
                        k.I("dve", "tensor_copy", out=xb[cc].k(("s", si))[:, c0:c0 + n], in_=ps.k()[:, 0:n])
                    nev += 1
            for cc in range(2):
                X = xb[cc]
                for si, (t0, n) in enumerate(SEGS):
                    par = si % 2
                    eng = "dve" if par == 0 else "pool"
                    c0 = pcol(t0)
                    base = c0 - 8
                    nbr = [X.k(("s", s2)) for s2 in (si - 1, si + 1) if 0 <= s2 < len(SEGS)]
                    Xv = X.k(("s", si))
                    cur, oth = tA[par], tB[par]
                    lo, hi = c0 - 7, c0 + n + 7
                    k.I(eng, "tensor_tensor", out=cur.k()[:, lo - base:hi - base], in0=Xv[:, lo - 1:hi - 1],
                        in1=Xv[:, lo:hi], op=ALU.add, extra_r=nbr)
                    for lv, (m, sh) in enumerate(((6, 1), (4, 2), (0, 4))):
                        if levels < lv + 2:
                            break
                        lo, hi = c0 - m, c0 + n + m
                        k.I(eng, "tensor_tensor", out=oth.k()[:, lo - base:hi - base],
                            in0=cur.k()[:, lo - sh - base:hi - sh - base], in1=cur.k()[:, lo + sh - base:hi + sh - base],
                            op=ALU.add)
                        cur, oth = oth, cur
                    k.I("dve", "scalar_tensor_tensor", out=pl[cc].k(("s", si))[:, t0:t0 + n], in0=cur.k()[:, 8:8 + n],
                        scalar=1.0 / w, in1=Xv[:, c0:c0 + n], op0=ALU.mult, op1=ALU.subtract)
                    if t0 in seq_starts:
                        k.I("dve", "tensor_tensor", out=te[par].k(), in0=cur.k()[:, 8:16], in1=pedge.k()[:, g, 0:8], op=ALU.mult)
                        k.I("dve", "tensor_tensor", out=pl[cc].k(("s", si))[:, t0:t0 + 8], in0=te[par].k(),
                            in1=Xv[:, c0:c0 + 8], op=ALU.subtract)
                    if t0 + n in seq_ends:
                        k.I("dve", "tensor_tensor", out=te[par].k(), in0=cur.k()[:, n:n + 8], in1=pedge.k()[:, g, 8:16], op=ALU.mult)
                        k.I("dve", "tensor_tensor", out=pl[cc].k(("s", si))[:, t0 + n - 8:t0 + n], in0=te[par].k(),
                            in1=Xv[:, c0 + n - 8:c0 + n], op=ALU.subtract)
            for dd in range(2):
                ft = g * 2 + dd
                W = wfm[nw % 4]
                nw += 1
                f0 = 3104 + ft * 128
                k.dma(W.k(), self.ev_w_in.k()[:, :, f0:f0 + 128], eng="pool")
                for si, (t0, n) in enumerate(SEGS):
                    ps = self.next_ps()
                    for kk in range(8):
                        k.mm(ps.k()[:, 0:n], W.k()[:, kk, :], hT.k()[:, kk, t0:t0 + n], start=(kk == 0), stop=(kk == 7))
                    k.I("act", "activation", out=szb.k(("s", si))[:, t0:t0 + n], in_=ps.k()[:, 0:n], func=AF.Silu)
                for si, (t0, n) in enumerate(SEGS):
                    ps = self.next_ps()
                    for cc in range(2):
                        k.mm(ps.k()[:, 0:n], pw.k()[:, g, cc, dd * 128:(dd + 1) * 128], pl[cc].k(("s", si))[:, t0:t0 + n],
                             start=(cc == 0), stop=(cc == 1))
                    st = stg[nst % 3]
                    nst += 1
                    k.I("dve", "scalar_tensor_tensor", out=st.k()[:, 0:n], in0=ps.k()[:, 0:n], scalar=psc.k()[:, ft:ft + 1],
                        in1=szb.k(("s", si))[:, t0:t0 + n], op0=ALU.mult, op1=ALU.mult)
                    k.dma(self.ypT.k(("f", ft, si))[ft * 128:(ft + 1) * 128, t0:t0 + n], st.k()[:, 0:n])
        k.release()

    def psbf(self, ps):
        return V(ps.res, None, ps.h.bitcast(BF16).ap())

    def passB(self):
        k = self.k
        k.mark()
        PS = self.PS
        triu = k.sb("triu", [128, 128], BF16)
        tril = k.sb("tril", [128, 128], BF16)
        onesb = k.sb("onesb", [128, 128], BF16)
        nmf = k.sb("nmf", [128, 128], BF16)
        nmb = k.sb("nmb", [128, 128], BF16)
        A_bc = k.sb("A_bc", [128, 32], F32)
        D_bc = k.sb("D_bc", [128, 16], F32)
        k.dma(triu.k(), self.c_triu.k(), eng="pool")
        k.dma(tril.k(), self.c_tril.k(), eng="pool")
        k.dma(nmf.k(), self.c_nmf.k(), eng="pool")
        k.dma(nmb.k(), self.c_nmb.k(), eng="pool")
        k.I("dve", "memset", wr=("ap",), ap=onesb.k(), constant=1.0)
        k.dma(A_bc.k(), self.a_log.k().m(lambda a: a.partition_broadcast(128)))
        k.dma(D_bc.k(), self.d_skip.k().m(lambda a: a.partition_broadcast(128)))
        k.I("act", "activation", out=A_bc.k(), in_=A_bc.k(), func=AF.Exp)
        k.I("dve", "tensor_scalar", out=A_bc.k(), in0=A_bc.k(), scalar1=-1.0, scalar2=None, op0=ALU.mult)
        tri_d = (triu, tril)
        nm_d = (nmf, nmb)

        def mk(name, shape, dt_):
            return [k.sb(f"{name}{i}", shape, dt_) for i in range(2)]
        xc = mk("xc", [128, 16, 128], BF16)
        a32 = mk("a32", [128, 32], F32)
        ahl = mk("ahl", [128, 64], BF16)
        cst = mk("cst", [128, 64], F32)
        ncs = mk("ncs", [128, 32], F32)
        dte = mk("dte", [128, 32], F32)
        wgt = mk("wgt", [128, 32], F32)
        cbT = mk("cbT", [128, 512], F32)
        E = [k.sb(f"E{i}", [128, 512], F32) for i in range(4)]
        M = [[k.sb(f"M{d}{i}", [128, 512], BF16) for i in range(2)] for d in range(2)]
        xs_sb = mk("xs_sb", [128, 1024], BF16)
        xdt = [mk(f"xdt{d}", [128, 1024], BF16) for d in range(2)]
        xw = [mk(f"xwl{d}", [128, 1024], BF16) for d in range(2)]
        xsD = mk("xsD", [128, 1024], F32)
        btk = mk("btk", [128, 512], BF16)
        yst = mk("yst", [128, 1024], F32)
        xcv = self.xcT.k().m(lambda a: a.rearrange("(j p) t -> p j t", p=128))
        nE = 0
        k.dma(xc[0].k(), V(self.xcT.res, None, xcv.ap[:, :, slice(0, 128)]))
        for i in range(NT):
            b = i % 2
            sl = slice(i * 128, (i + 1) * 128)
            if i + 1 < NT:
                k.dma(xc[(i + 1) % 2].k(), V(self.xcT.res, None, xcv.ap[:, :, slice((i + 1) * 128, (i + 2) * 128)]))
            dt = self.dt_all.k(("t", i))[:, i, :]
            k.I("dve", "tensor_tensor", out=a32[b].k(), in0=dt, in1=A_bc.k(), op=ALU.mult)
            k.I("dve", "tensor_copy", out=ahl[b].k()[:, 0:32], in_=a32[b].k())
            k.I("dve", "tensor_tensor", out=ahl[b].k()[:, 32:64], in0=a32[b].k(), in1=ahl[b].k()[:, 0:32], op=ALU.subtract)
            pc = PS[6]
            k.mm(pc.k()[:, 0:16], triu.k(), ahl[b].k()[:, 0:16], start=True, stop=False)
            k.mm(pc.k()[:, 0:16], triu.k(), ahl[b].k()[:, 32:48], start=False, stop=True)
            k.mm(pc.k()[:, 16:32], tril.k(), ahl[b].k()[:, 16:32], start=True, stop=False)
            k.mm(pc.k()[:, 16:32], tril.k(), ahl[b].k()[:, 48:64], start=False, stop=True)
            k.mm(pc.k()[:, 32:64], onesb.k(), ahl[b].k()[:, 0:32], start=True, stop=False)
            k.mm(pc.k()[:, 32:64], onesb.k(), ahl[b].k()[:, 32:64], start=False, stop=True)
            k.I("dve", "tensor_copy", out=cst[b].k(), in_=pc.k()[:, 0:64])
            k.I("dve", "tensor_scalar", out=ncs[b].k(), in0=cst[b].k()[:, 0:32], scalar1=-1.0, scalar2=None, op0=ALU.mult)
            k.I("act", "activation", out=self.e_all.k(("t", i))[:, i, :], in_=cst[b].k()[:, 0:32], func=AF.Exp)
            k.I("act", "activation", out=self.dec_all.k(("t", i))[:, i, :], in_=cst[b].k()[:, 32:64], func=AF.Exp)
            k.I("dve", "tensor_tensor", out=dte[b].k(), in0=cst[b].k()[:, 32:64], in1=cst[b].k()[:, 0:32], op=ALU.subtract)
            k.I("act", "activation", out=dte[b].k(), in_=dte[b].k(), func=AF.Exp)
            k.I("dve", "tensor_tensor", out=wgt[b].k(), in0=dt, in1=dte[b].k(), op=ALU.mult)
            pcb = PS[6]
            for g in range(4):
                k.mm(pcb.k()[:, g * 128:(g + 1) * 128], xc[b].k()[:, 8 + g, :], xc[b].k()[:, 12 + g, :])
            k.I("act", "activation", out=cbT[b].k(), in_=pcb.k(), func=AF.Copy)
            px = self.psbf(PS[7])
            for j in range(8):
                k.tr(px[:, j * 128:(j + 1) * 128], xc[b].k()[:, j, :], self.identb.k())
            k.I("act", "activation", out=xs_sb[b].k(), in_=px, func=AF.Copy)
            pb = self.psbf(PS[7])
            for g in range(4):
                k.tr(pb[:, g * 128:(g + 1) * 128], xc[b].k()[:, 8 + g, :], self.identb.k())
            k.I("dve", "tensor_copy", out=btk[b].k(), in_=pb[:, 0:512])
            x3 = xs_sb[b].k().m(lambda a: a.rearrange("p (r q) -> p r q", q=64))

            def bc16(v):
                return v.m(lambda a: a.unsqueeze(2).to_broadcast([128, 16, 64]))
            for d in range(2):
                k.I("dve", "tensor_tensor", out=xdt[d][b].k().m(lambda a: a.rearrange("p (r q) -> p r q", q=64)),
                    in0=x3, in1=bc16(self.dt_all.k(("t", i))[:, i, d * 16:(d + 1) * 16]), op=ALU.mult)
                k.I("pool", "tensor_tensor", out=xw[d][b].k().m(lambda a: a.rearrange("p (r q) -> p r q", q=64)),
                    in0=x3, in1=bc16(wgt[b].k()[:, d * 16:(d + 1) * 16]), op=ALU.mult)
            k.I("pool", "tensor_tensor", out=xsD[b].k().m(lambda a: a.rearrange("p (r q) -> p r q", q=64)),
                in0=x3, in1=bc16(D_bc.k()), op=ALU.mult)
            py = (PS[0], PS[1])

            def emitE(g):
                nonlocal nE
                for d in range(2):
                    pE = PS[2 + (nE % 4)]
                    Et = E[nE % 4]
                    nE += 1
                    for rr in range(4):
                        col = d * 16 + g * 4 + rr
                        o = pE.k()[:, rr * 128:(rr + 1) * 128]
                        k.mm(o, ahl[b].k()[:, col:col + 1].m(lambda a: a.to_broadcast([128, 128])), tri_d[d].k(),
                             start=True, stop=False)
                        k.mm(o, ahl[b].k()[:, 32 + col:33 + col].m(lambda a: a.to_broadcast([128, 128])), tri_d[d].k(),
                             start=False, stop=False)
                        k.mm(o, self.identb.k(), nm_d[d].k(), start=False, stop=True)
                    for rr in range(4):
                        col = d * 16 + g * 4 + rr
                        k.I("act", "activation", out=Et.k()[:, rr * 128:(rr + 1) * 128], in_=pE.k()[:, rr * 128:(rr + 1) * 128],
                            func=AF.Exp, bias=ncs[b].k()[:, col:col + 1])
                    k.I("dve", "tensor_tensor", out=M[d][g % 2].k().m(lambda a: a.rearrange("p (r q) -> p r q", q=128)),
                        in0=Et.k().m(lambda a: a.rearrange("p (r q) -> p r q", q=128)),
                        in1=cbT[b].k()[:, g * 128:(g + 1) * 128].m(lambda a: a.unsqueeze(1).to_broadcast([128, 4, 128])),
                        op=ALU.mult)

            def emitY(g):
                for rr in range(4):
                    r = g * 4 + rr
                    o = py[r // 8].k()[:, (r % 8) * 64:(r % 8 + 1) * 64]
                    k.mm(o, M[0][g % 2].k()[:, rr * 128:(rr + 1) * 128], xdt[0][b].k()[:, r * 64:(r + 1) * 64], start=True, stop=False)
                    k.mm(o, M[1][g % 2].k()[:, rr * 128:(rr + 1) * 128], xdt[1][b].k()[:, r * 64:(r + 1) * 64], start=False, stop=True)

            emitE(0)
            for g in range(4):
                if g + 1 < 4:
                    emitE(g + 1)
                emitY(g)
            for hh in range(2):
                k.I("dve", "tensor_tensor", out=yst[b].k()[:, hh * 512:(hh + 1) * 512], in0=py[hh].k(),
                    in1=xsD[b].k()[:, hh * 512:(hh + 1) * 512], op=ALU.add)
            k.dma(self.yloc.k(("t", i))[sl, :], yst[b].k())
            for d in range(2):
                k.dma(self.xw_s[d].k(("t", i))[sl, :], xw[d][b].k())
            k.dma(self.btok_s.k(("t", i))[sl, :], btk[b].k())
        k.release()

    def passC(self):
        k = self.k
        k.mark()
        PS = self.PS
        H = [k.sb(f"H{d}", [128, 1024], F32) for d in range(2)]
        Hb = [k.sb(f"Hb{d}", [128, 1024], BF16) for d in range(2)]
        tmpH = [k.sb(f"tmpH{d}", [128, 1024], F32) for d in range(2)]
        CT = [[k.sb(f"CT{d}{i}", [128, 4, 128], BF16) for i in range(2)] for d in range(2)]
        Bt = [[k.sb(f"Bt{d}{i}", [128, 512], BF16) for i in range(2)] for d in range(2)]
        xwt = [[k.sb(f"xwt{d}{i}", [128, 1024], BF16) for i in range(2)] for d in range(2)]
        yo = [[k.sb(f"yo{d}{i}", [128, 1024], F32) for i in range(2)] for d in range(2)]
        hout = k.sb("hout", [128, 8, 128], F32)
        h0src = (self.hf0, self.hb0)
        oss = (self.o_ssdf, self.o_ssdb)
        xcv = self.xcT.k().m(lambda a: a.rearrange("(j p) t -> p j t", p=128))

        def bc16(v):
            return v.m(lambda a: a.unsqueeze(2).to_broadcast([128, 16, 64]))

        def v3(v):
            return v.m(lambda a: a.rearrange("p (r q) -> p r q", q=64))
        for sidx, (t0, T, cond) in enumerate(SEQS):
            nch = T // 128
            cb_ = t0 // 128
            for d in range(2):
                if sidx == 0:
                    k.dma(H[d].k(), h0src[d].k())
                else:
                    k.I("pool", "memset", wr=("ap",), ap=H[d].k(), constant=0.0)
                k.I("act", "activation", out=Hb[d].k(), in_=H[d].k(), func=AF.Copy)
            def c_loads(st):
                b = st % 2
                for d in range(2):
                    c = cb_ + (st if d == 0 else nch - 1 - st)
                    sl = slice(c * 128, (c + 1) * 128)
                    k.dma(CT[d][b].k(), V(self.xcT.res, None, xcv.ap[:, 12:16, sl]))
                    k.dma(Bt[d][b].k(), self.btok_s.k(("t", c))[sl, :])
                    k.dma(xwt[d][b].k(), self.xw_s[d].k(("t", c))[sl, :])
            c_loads(0)
            for st in range(nch):
                b = st % 2
                if st + 1 < nch:
                    c_loads(st + 1)
                for d in range(2):
                    c = cb_ + (st if d == 0 else nch - 1 - st)
                    sl = slice(c * 128, (c + 1) * 128)
                    pY = (PS[4 * d], PS[4 * d + 1])
                    pS = (PS[4 * d + 2], PS[4 * d + 3])
                    for g in range(4):
                        k.mm(pY[g // 2].k()[:, (g % 2) * 256:(g % 2 + 1) * 256], CT[d][b].k()[:, g, :],
                             Hb[d].k()[:, g * 256:(g + 1) * 256])
                    for hh in range(2):
                        k.I("dve", "tensor_tensor", out=v3(yo[d][b].k()[:, hh * 512:(hh + 1) * 512]), in0=v3(pY[hh].k()),
                            in1=self.e_all.k(("t", c))[:, c, d * 16 + hh * 8:d * 16 + hh * 8 + 8].m(
                                lambda a: a.unsqueeze(2).to_broadcast([128, 8, 64])), op=ALU.mult)
                    k.dma(self.yoff[d].k(("t", c))[sl, :], yo[d][b].k())
                    for g in range(4):
                        k.mm(pS[g // 2].k()[:, (g % 2) * 256:(g % 2 + 1) * 256], Bt[d][b].k()[:, g * 128:(g + 1) * 128],
                             xwt[d][b].k()[:, g * 256:(g + 1) * 256])
                    k.I("pool", "tensor_tensor", out=v3(tmpH[d].k()), in0=v3(H[d].k()),
                        in1=bc16(self.dec_all.k(("t", c))[:, c, d * 16:(d + 1) * 16]), op=ALU.mult)
                    for hh in range(2):
                        k.I("dve", "tensor_tensor", out=H[d].k()[:, hh * 512:(hh + 1) * 512], in0=pS[hh].k(),
                            in1=tmpH[d].k()[:, hh * 512:(hh + 1) * 512], op=ALU.add)
                    k.I("act", "activation", out=Hb[d].k(), in_=H[d].k(), func=AF.Copy)
            if sidx > 0:
                pi = sidx - 1
                for d in range(2):
                    for half in range(2):
                        ps = PS[4 * d + half]
                        for q in range(4):
                            j = half * 4 + q
                            k.tr(ps.k()[:, q * 128:(q + 1) * 128], H[d].k()[:, j * 128:(j + 1) * 128], self.ident.k())
                        k.I("act", "activation", out=hout.k()[:, half * 4:half * 4 + 4, :].m(
                            lambda a: a.rearrange("p j n -> p (j n)")), in_=ps.k(), func=AF.Copy)
                    k.dma(oss[d].k(("p", pi))[pi].m(lambda a: a.rearrange("(j p) n -> p j n", p=128)), hout.k(), final=True)
        k.release()

    def passD(self):
        k = self.k
        k.mark()
        PS = self.PS
        wout = k.sb("wout", [128, 16, 1024], BF16)
        ng = k.sb("ng", [128, 8], F32)
        for q in range(4):
            k.dma(wout.k(("q", q))[:, q * 4:(q + 1) * 4, :], self.ev_w_out.k()[:, q * 4:(q + 1) * 4, :], eng="pool")
        k.dma(ng.k(), self.norm_g.k())

        def mk(name, shape, dt_):
            return [k.sb(f"{name}{i}", shape, dt_) for i in range(2)]
        def mk3(name, shape, dt_):
            return [k.sb(f"{name}{i}", shape, dt_) for i in range(3)]
        y0 = mk3("dy0", [128, 1024], F32)
        y1 = mk3("dy1", [128, 1024], F32)
        y2 = mk3("dy2", [128, 1024], F32)
        za = mk3("dza", [128, 1024], F32)
        xr = mk3("dxr", [128, 1024], F32)
        sq = k.sb("dsq", [128, 1024], F32)
        gnb = mk("dgn", [128, 1024], BF16)
        ysT = mk("dysT", [128, 8, 128], BF16)
        ypt = mk3("dyp", [128, 8, 128], BF16)
        ss = mk("dss", [128, 1], F32)
        rs = mk("drs", [128, 1], F32)
        rstd = mk("drstd", [128, 1], F32)
        xo = mk("dxo", [128, 1024], F32)
        ypv = self.ypT.k().m(lambda a: a.rearrange("(j p) t -> p j t", p=128))

        def loads(i):
            b = i % 3
            sl = slice(i * 128, (i + 1) * 128)
            k.dma(y0[b].k(), self.yloc.k(("t", i))[sl, :])
            k.dma(y1[b].k(), self.yoff[0].k(("t", i))[sl, :])
            k.dma(y2[b].k(), self.yoff[1].k(("t", i))[sl, :])
            k.dma(za[b].k(), self.sza.k()[sl, :])
            k.dma(xr[b].k(), self.x_all.k()[sl, :])
            k.dma(ypt[b].k(), V(self.ypT.res, None, ypv.ap[:, :, sl]))

        def stage1(i):
            b = i % 2
            b3 = i % 3
            k.I("pool", "tensor_tensor", out=y1[b3].k(), in0=y1[b3].k(), in1=y2[b3].k(), op=ALU.add)
            k.I("dve", "tensor_tensor", out=y0[b3].k(), in0=y0[b3].k(), in1=y1[b3].k(), op=ALU.add)
            k.I("dve", "tensor_tensor", out=y0[b3].k(), in0=y0[b3].k(), in1=za[b3].k(), op=ALU.mult)
            k.I("act", "activation", out=sq.k(), in_=y0[b3].k(), func=AF.Square, accum_out=ss[b].k())
            k.I("act", "activation", out=rs[b].k(), in_=ss[b].k(), func=AF.Sqrt, scale=1.0 / 1024, bias=self.epsc.k())
            k.I("dve", "reciprocal", out=rstd[b].k(), in_=rs[b].k())
            k.I("dve", "tensor_scalar", out=gnb[b].k(), in0=y0[b3].k(), scalar1=rstd[b].k(), scalar2=None, op0=ALU.mult)
            pt = self.psbf(PS[4 + b])
            for j in range(8):
                k.tr(pt[:, j * 128:(j + 1) * 128], gnb[b].k()[:, j * 128:(j + 1) * 128], self.identb.k())
            k.I("dve", "tensor_tensor", out=ysT[b].k(), in0=pt.m(lambda a: a.rearrange("p (j t) -> p j t", t=128)),
                in1=ng.k().m(lambda a: a.unsqueeze(2).to_broadcast([128, 8, 128])), op=ALU.mult)

        def stage2(i):
            b = i % 2
            b3 = i % 3
            cond = 0 if i < 32 else 1
            sl = slice(i * 128, (i + 1) * 128)
            for nb in range(2):
                po = PS[2 * b + nb]
                for kk in range(16):
                    lhs = ysT[b].k()[:, kk, :] if kk < 8 else ypt[b3].k()[:, kk - 8, :]
                    k.mm(po.k(), lhs, wout.k()[:, kk, nb * 512:(nb + 1) * 512], start=(kk == 0), stop=(kk == 15))
                k.I("dve", "tensor_tensor", out=xo[b].k()[:, nb * 512:(nb + 1) * 512], in0=po.k(),
                    in1=self.gate_bc[0][cond].k()[:, nb * 512:(nb + 1) * 512], op=ALU.mult)
            k.I("pool", "tensor_tensor", out=xo[b].k(), in0=xo[b].k(), in1=xr[b3].k(), op=ALU.add)
            k.dma(self.x1.k(("t", i))[sl, :], xo[b].k())

        loads(0)
        loads(1)
        stage1(0)
        for i in range(NT):
            if i + 2 < NT:
                loads(i + 2)
            if i + 1 < NT:
                stage1(i + 1)
            stage2(i)
        k.release()

    def passE(self):
        k = self.k
        hT = self.hT
        k.mark()
        cosT = k.sb("cosT", [128, 4096], F32)
        sinT = k.sb("sinT", [128, 4096], F32)
        for q in range(4):
            k.dma(cosT.k(("q", q))[:, q * 1024:(q + 1) * 1024], self.c_cos.k()[:, q * 1024:(q + 1) * 1024])
            k.dma(sinT.k(("q", q))[:, q * 1024:(q + 1) * 1024], self.c_sin.k()[:, q * 1024:(q + 1) * 1024])
        wfm = [k.sb(f"wfm{i}", [128, 8, 128], BF16) for i in range(3)]
        qbf = [k.sb(f"eqb{i}", [128, 512], BF16) for i in range(2)]
        pmb = k.sb("pmb", [128, 128], BF16)
        k.dma(pmb.k(), self.c_pm.k(), eng="pool")
        wtm = [k.sb(f"wtm{i}", [128, 8, 512], BF16) for i in range(2)]
        t1 = [k.sb(f"et1{i}", [128, 512], F32) for i in range(2)]
        t2 = [k.sb(f"et2{i}", [128, 512], F32) for i in range(2)]
        sb16 = [k.sb(f"esb{i}", [128, 512], BF16) for i in range(3)]
        sf32 = [k.sb(f"esf{i}", [128, 512], F32) for i in range(3)]
        cnt = {"w": 0, "b": 0, "f": 0, "t": 0, "e": 0}

        pend = []

        def fm_rope(col0, rcol0, ntiles, dst):
            for j in range(ntiles):
                W = wfm[cnt["w"] % 3]
                cnt["w"] += 1
                k.dma(W.k(), self.od_w_in.k()[:, :, col0 + j * 128:col0 + (j + 1) * 128], eng="pool")
                for si, (t0, n) in enumerate(SEGS):
                    pa = self.next_ps()
                    for kk in range(8):
                        k.mm(pa.k()[:, 0:n], W.k()[:, kk, :], hT.k()[:, kk, t0:t0 + n], start=(kk == 0), stop=(kk == 7))
                    while pend:
                        pend.pop(0)()
                    st = sb16[cnt["b"] % 3]
                    cnt["b"] += 1
                    if si < 8:
                        qb16 = qbf[cnt["t"] % 2]
                        a = t1[cnt["t"] % 2]
                        b_ = t2[cnt["t"] % 2]
                        cnt["t"] += 1
                        k.I("act", "activation", out=qb16.k()[:, 0:n], in_=pa.k()[:, 0:n], func=AF.Copy)
                        k.I("dve", "tensor_tensor", out=a.k()[:, 0:n], in0=pa.k()[:, 0:n], in1=cosT.k()[:, t0:t0 + n], op=ALU.mult,
                            extra_r=[qb16.k()])

                        def tail(qb16=qb16, a=a, b_=b_, st=st, t0=t0, n=n, j=j, si=si):
                            pb = self.next_ps()
                            k.mm(pb.k()[:, 0:n], pmb.k(), qb16.k()[:, 0:n])
                            k.I("dve", "tensor_tensor", out=b_.k()[:, 0:n], in0=pb.k()[:, 0:n], in1=sinT.k()[:, t0:t0 + n], op=ALU.mult)
                            k.I("pool", "tensor_tensor", out=st.k()[:, 0:n], in0=a.k()[:, 0:n], in1=b_.k()[:, 0:n], op=ALU.add)
                            k.dma(dst.k(("f", j, si))[j * 128:(j + 1) * 128, t0:t0 + n], st.k()[:, 0:n])
                        pend.append(tail)
                    else:
                        k.I("act", "activation", out=st.k()[:, 0:n], in_=pa.k()[:, 0:n], func=AF.Copy)
                        k.dma(dst.k(("f", j, si))[j * 128:(j + 1) * 128, t0:t0 + n], st.k()[:, 0:n])
            while pend:
                pend.pop(0)()

        def fm_silu(col0, ntiles, dst):
            for j in range(ntiles):
                W = wfm[cnt["w"] % 3]
                cnt["w"] += 1
                k.dma(W.k(), self.od_w_in.k()[:, :, col0 + j * 128:col0 + (j + 1) * 128], eng="pool")
                for si, (t0, n) in enumerate(SEGS):
                    pa = self.next_ps()
                    for kk in range(8):
                        k.mm(pa.k()[:, 0:n], W.k()[:, kk, :], hT.k()[:, kk, t0:t0 + n], start=(kk == 0), stop=(kk == 7))
                    st = sf32[cnt["f"] % 3]
                    cnt["f"] += 1
                    k.I("act", "activation", out=st.k()[:, 0:n], in_=pa.k()[:, 0:n], func=AF.Silu)
                    k.dma(dst.k(("f", j, si))[j * 128:(j + 1) * 128, t0:t0 + n], st.k()[:, 0:n])

        def tm(col0, width, tiles, sinks):
            W = wtm[cnt["e"] % 2]
            cnt["e"] += 1
            k.dma(W.k()[:, :, 0:width], self.od_w_in.k()[:, :, col0:col0 + width], eng="pool")
            for i in tiles:
                pa = self.next_ps()
                for kk in range(8):
                    k.mm(pa.k()[:, 0:width], hT.k(("t", i))[:, kk, i * 128:(i + 1) * 128], W.k()[:, kk, 0:width],
                         start=(kk == 0), stop=(kk == 7))
                for (c0, c1, d16, dcol, d32, fcol, frow) in sinks:
                    if d16 is not None and self.stop_after != "E0b":
                        st = sb16[cnt["b"] % 3]
                        cnt["b"] += 1
                        k.I("act", "activation", out=st.k()[:, 0:c1 - c0], in_=pa.k()[:, c0:c1], func=AF.Copy)
                        k.dma(d16.k(("t", i, dcol))[i * 128:(i + 1) * 128, dcol:dcol + c1 - c0], st.k()[:, 0:c1 - c0])
                    if d32 is not None and i >= 32 and self.stop_after != "E0a":
                        st = sf32[cnt["f"] % 3]
                        cnt["f"] += 1
                        k.I("act", "activation", out=st.k()[:, 0:c1 - c0], in_=pa.k()[:, c0:c1], func=AF.Copy)
                        r0 = (i - 32) * 128
                        k.dma(d32.k(("t", i, fcol))[r0:r0 + 128, fcol:fcol + c1 - c0], st.k()[:, 0:c1 - c0], final=True)

        allt = list(range(NT))
        pt = list(range(32, NT))
        if self.stop_after == "E00":
            k.release()
            return
        for fb in range(2):
            tm(2048 + fb * 512, 512, allt, [(0, 512, self.Vc, fb * 512, self.o_dv, fb * 512, 0)])
        if self.stop_after in ("E0", "E0a", "E0b"):
            k.release()
            return
        tm(5376, 256, allt, [(0, 256, self.Vd, 0, self.o_wv, 0, 0)])
        for fb in range(2):
            tm(1024 + fb * 512, 512, pt, [(0, 512, None, 0, self.o_dk, fb * 512, 0)])
        tm(5120, 256, pt, [(0, 256, None, 0, self.o_wk, 0, 0)])
        if self.stop_after == "E1":
            k.release()
            return
        fm_rope(0, 0, 8, self.QcT)
        if self.stop_after == "E2":
            k.release()
            return
        fm_rope(1024, 1024, 8, self.KcT)
        fm_rope(4096, 2048, 8, self.QdT)
        fm_rope(5120, 3072, 2, self.KdT)
        fm_silu(3072, 8, self.szcT)
        fm_silu(5632, 8, self.szdT)
        k.release()

    def passF(self):
        k = self.k
        k.mark()
        PS = self.PS
        SC = 64 ** -0.5
        onesb = k.sb("f_ones", [128, 128], BF16)
        k.I("dve", "memset", wr=("ap",), ap=onesb.k(), constant=1.0)
        lam4 = k.sb("lam4", [128, 4, 64], F32)
        lp = k.sb("lamp", [128, 2, 64], F32)
        ls = k.sb("lams", [128, 2], F32)
        nlam = k.sb("nlam", [128, 1], F32)
        gsub = k.sb("gsub", [128, 1], F32)
        k.dma(lam4.k(), self.od_lambda.k().m(lambda a: a.partition_broadcast(128)).m(
            lambda a: a.rearrange("p o (a b) -> p (o a) b", b=64)))
        k.I("dve", "tensor_tensor", out=lp.k()[:, 0, :], in0=lam4.k()[:, 0, :], in1=lam4.k()[:, 1, :], op=ALU.mult)
        k.I("dve", "tensor_tensor", out=lp.k()[:, 1, :], in0=lam4.k()[:, 2, :], in1=lam4.k()[:, 3, :], op=ALU.mult)
        k.I("dve", "tensor_reduce", out=ls.k(), in_=lp.k(), axis=mybir.AxisListType.X, op=ALU.add)
        k.I("act", "activation", out=ls.k(), in_=ls.k(), func=AF.Exp)
        k.I("dve", "tensor_tensor", out=nlam.k(), in0=ls.k()[:, 1:2], in1=ls.k()[:, 0:1], op=ALU.subtract)
        k.I("dve", "tensor_scalar", out=nlam.k(), in0=nlam.k(), scalar1=-LAM_INIT, scalar2=None, op0=ALU.add)
        k.dma(gsub.k(), self.subln_g.k())
        k.I("dve", "tensor_scalar", out=gsub.k(), in0=gsub.k(), scalar1=1.0 - LAM_INIT, scalar2=None, op0=ALU.mult)

        KT = [k.sb(f"fKT{i}", [128, 4608], BF16) for i in range(2)]
        Vh = [k.sb(f"fVh{i}", [128, 36, 128], BF16) for i in range(2)]
        QT = [k.sb(f"fQT{i}", [128, 512], BF16) for i in range(2)]
        szc = [k.sb(f"fszc{i}", [128, 512], F32) for i in range(3)]
        P = [k.sb(f"fP{i}", [128, 512], BF16) for i in range(8)]
        ones32 = k.sb("f_ones32", [128, 32], BF16)
        k.I("dve", "memset", wr=("ap",), ap=ones32.k(), constant=1.0)
        c32 = k.sb("f_c32", [64, 128], F32)
        k.I("pool", "memset", wr=("ap",), ap=c32.k(), constant=1.0 / 32)
        zsb = k.sb("fzsb", [64, 512], F32)
        osb = [k.sb(f"fosb{i}", [128, 512], F32) for i in range(2)]
        rz = [k.sb(f"frz{i}", [128, 512], F32) for i in range(2)]
        ta = k.sb("fta", [128, 512], F32)
        tb = k.sb("ftb", [128, 512], F32)
        oc = k.sb("foc", [128, 512], F32)
        sq = k.sb("fsq", [128, 512], BF16)
        rs = k.sb("frs", [128, 512], F32)
        ost = [k.sb(f"fost{i}", [128, 512], BF16) for i in range(2)]
        cnt = {"P": 0, "S": 0}
        nq_ = 0
        nh = 0
        pO = (PS[0], PS[1])
        pZ = PS[2]
        pX = PS[3]
        heads = []
        qitems = []
        for sidx, (t0, T, cond) in enumerate(SEQS):
            nctx = 4 if sidx == 0 else 0
            qblocks = [(t0 + q * 512, 512) for q in range(T // 512)] if T >= 512 else [(t0, T)]
            for h in range(8):
                heads.append((sidx, t0, T, nctx, h))
                for qi, (q0, nq) in enumerate(qblocks):
                    qitems.append((len(heads) - 1, q0, nq, qi == 0))

        def head_loads(hidx):
            sidx, t0, T, nctx, h = heads[hidx]
            kt_, vh_ = KT[hidx % 2], Vh[hidx % 2]
            if nctx:
                k.dma(kt_.k(("c",))[:, 0:512], self.dk_T.k()[h], eng="pool")
                k.dma(vh_.k(("c",))[:, 0:4, :], self.dv.k()[h].m(lambda a: a.rearrange("(j p) v -> p j v", p=128)), eng="pool")
            for c0 in range(0, T, 2048):
                cw_ = min(2048, T - c0)
                k.dma(kt_.k(("o", c0))[:, nctx * 128 + c0:nctx * 128 + c0 + cw_],
                      self.KcT.k()[h * 128:(h + 1) * 128, t0 + c0:t0 + c0 + cw_])
            for j0 in range(0, T // 128, 8):
                jn = min(8, T // 128 - j0)
                k.dma(vh_.k(("o", j0))[:, nctx + j0:nctx + j0 + jn, :],
                      self.Vc.k()[t0 + j0 * 128:t0 + (j0 + jn) * 128, h * 128:(h + 1) * 128].m(
                          lambda a: a.rearrange("(j p) v -> p j v", p=128)))

        def q_loads(qidx):
            hidx, q0, nq, first = qitems[qidx]
            h = heads[hidx][4]
            k.dma(QT[qidx % 2].k()[:, 0:nq], self.QcT.k()[h * 128:(h + 1) * 128, q0:q0 + nq])
            k.dma(szc[qidx % 3].k()[:, 0:nq], self.szcT.k()[h * 128:(h + 1) * 128, q0:q0 + nq])

        head_loads(0)
        q_loads(0)
        deferred = []
        PsAll = {}
        if True:
            if True:
                for qidx, (hidx, q0, nq, first) in enumerate(qitems):
                    sidx, t0, T, nctx, h = heads[hidx]
                    nkt = nctx + T // 128
                    kt_, vh_ = KT[hidx % 2], Vh[hidx % 2]
                    qb = qidx % 2
                    if first and hidx + 1 < len(heads):
                        head_loads(hidx + 1)
                    if qidx + 1 < len(qitems):
                        q_loads(qidx + 1)
                    def emit_qk(qi2, kt):
                        hidx2, q02, nq2, first2 = qitems[qi2]
                        nctx2 = heads[hidx2][3]
                        ktile2 = KT[hidx2 % 2]
                        kkey = ("c",) if kt < nctx2 else ("o", ((kt - nctx2) * 128) // 2048 * 2048)
                        lst = []
                        for i in range(2):
                            pS = PS[4 + cnt["S"] % 4]
                            cnt["S"] += 1
                            k.mm(pS.k()[:, 0:nq2], ktile2.k(kkey)[i * 64:(i + 1) * 64, kt * 128:(kt + 1) * 128],
                                 QT[qi2 % 2].k()[i * 64:(i + 1) * 64, 0:nq2])
                            Pt = P[cnt["P"] % 8]
                            cnt["P"] += 1
                            k.I("act", "activation", out=Pt.k()[:, 0:nq2], in_=pS.k()[:, 0:nq2], func=AF.Exp, scale=SC)
                            lst.append(Pt)
                        PsAll[(qi2, kt)] = lst
                    if qidx == 0:
                        emit_qk(0, 0)
                    Ps = {}
                    for kt in range(nkt):
                        if kt + 1 < nkt:
                            emit_qk(qidx, kt + 1)
                        elif qidx + 1 < len(qitems):
                            emit_qk(qidx + 1, 0)
                        Ps[kt] = PsAll.pop((qidx, kt))
                        vkey = ("c",) if kt < nctx else ("o", (kt - nctx) // 8 * 8)
                        for i in range(2):
                            k.mm(pO[i].k()[:, 0:nq], vh_.k(vkey)[:, kt, :], Ps[kt][i].k()[:, 0:nq],
                                 start=(kt == 0), stop=(kt == nkt - 1))
                        for i in range(2):
                            k.mm(pZ.k()[32 * i:32 * i + 32, 0:nq], ones32.k(), Ps[kt][i].k()[:, 0:nq],
                                 start=(kt == 0), stop=(kt == nkt - 1), tile_position=(0, 32 * i))
                        del Ps[kt]
                        while deferred and deferred[0][0] <= kt:
                            deferred.pop(0)[1]()
                    while deferred:
                        deferred.pop(0)[1]()
                    k.I("dve", "tensor_copy", out=osb[0].k()[:, 0:nq], in_=pO[0].k()[:, 0:nq])
                    k.I("dve", "tensor_copy", out=osb[1].k()[:, 0:nq], in_=pO[1].k()[:, 0:nq])
                    k.I("dve", "tensor_copy", out=zsb.k()[:, 0:nq], in_=pZ.k()[0:64, 0:nq])

                    def mk_tail(h=h, q0=q0, nq=nq, qidx=qidx):
                        szc_ = szc[qidx % 3]
                        ot = ost[qidx % 2]

                        def d1():
                            k.I("dve", "reciprocal", out=zsb.k()[:, 0:nq], in_=zsb.k()[:, 0:nq])

                        def d2():
                            k.mm(pX.k()[:, 0:nq], c32.k()[0:32, :], zsb.k()[0:32, 0:nq])
                            k.I("dve", "tensor_tensor", out=ta.k()[:, 0:nq], in0=osb[0].k()[:, 0:nq], in1=pX.k()[:, 0:nq], op=ALU.mult)

                        def d3():
                            k.mm(pX.k()[:, 0:nq], c32.k()[32:64, :], zsb.k()[32:64, 0:nq])
                            k.I("dve", "tensor_tensor", out=tb.k()[:, 0:nq], in0=osb[1].k()[:, 0:nq], in1=pX.k()[:, 0:nq], op=ALU.mult)
                            k.I("dve", "scalar_tensor_tensor", out=oc.k()[:, 0:nq], in0=tb.k()[:, 0:nq], scalar=nlam.k(),
                                in1=ta.k()[:, 0:nq], op0=ALU.mult, op1=ALU.add)

                        def d4():
                            k.I("act", "activation", out=sq.k()[:, 0:nq], in_=oc.k()[:, 0:nq], func=AF.Square)

                        def d5():
                            k.mm(pX.k()[:, 0:nq], onesb.k(), sq.k()[:, 0:nq])

                        def d6():
                            k.I("act", "activation", out=rs.k()[:, 0:nq], in_=pX.k()[:, 0:nq], func=AF.Sqrt, scale=1.0 / 128,
                                bias=self.epsc.k())
                            k.I("dve", "reciprocal", out=rs.k()[:, 0:nq], in_=rs.k()[:, 0:nq])
                            k.I("dve", "tensor_tensor", out=oc.k()[:, 0:nq], in0=oc.k()[:, 0:nq], in1=rs.k()[:, 0:nq], op=ALU.mult)
                            k.I("dve", "scalar_tensor_tensor", out=ot.k()[:, 0:nq], in0=oc.k()[:, 0:nq], scalar=gsub.k(),
                                in1=szc_.k()[:, 0:nq], op0=ALU.mult, op1=ALU.mult)
                            k.dma(self.ocT.k(("h", h, q0))[h * 128:(h + 1) * 128, q0:q0 + nq], ot.k()[:, 0:nq])
                        return [(1, d1), (3, d2), (6, d3), (9, d4), (12, d5), (15, d6)]
                    deferred.extend(mk_tail())
        while deferred:
            deferred.pop(0)[1]()
        k.release()

    def passG(self):
        k = self.k
        k.mark()
        PS = self.PS
        SC = 64 ** -0.5
        onesb = k.sb("g_ones", [128, 64], BF16)
        k.I("dve", "memset", wr=("ap",), ap=onesb.k(), constant=1.0)
        nmp = k.sb("g_nmp", [128, 128], BF16)
        nmn = k.sb("g_nmn", [128, 128], BF16)
        k.dma(nmp.k(), self.c_nmb.k(), eng="pool")
        k.dma(nmn.k(), self.c_nmf.k(), eng="pool")
        esink = k.sb("esink", [128, 16], F32)
        k.dma(esink.k(), self.sink.k().m(lambda a: a.partition_broadcast(128)))
        k.I("act", "activation", out=esink.k(), in_=esink.k(), func=AF.Exp)
        KT = [k.sb(f"gKT{i}", [128, 4608], BF16) for i in range(2)]
        Vh = [k.sb(f"gVh{i}", [128, 36, 128], BF16) for i in range(2)]
        for t in Vh:
            k.I("pool", "memset", wr=("ap",), ap=t.k(), constant=1.0)
        c64 = k.sb("g_c64", [128, 64], F32)
        k.I("pool", "memset", wr=("ap",), ap=c64.k(), constant=1.0 / 64)
        QA = [k.sb(f"gQA{i}", [128, 4096], BF16) for i in range(2)]
        QB = [k.sb(f"gQB{i}", [128, 4096], BF16) for i in range(2)]
        P = [k.sb(f"gP{i}", [128, 512], BF16) for i in range(4)]
        szd = [k.sb(f"gszd{i}", [64, 4, 128], F32) for i in range(3)]
        osb = k.sb("gosb", [128, 512], F32)
        zmv = k.sb("gzmv", [64, 512], F32)
        zt = k.sb("gzt", [64, 512], F32)
        od = k.sb("god", [64, 512], F32)
        ost = [k.sb(f"gost{i}", [64, 4, 128], BF16) for i in range(2)]
        gcnt = {"P": 0, "S": 0}

        def pv(v):
            return v.m(lambda a: a.rearrange("p (r h q) -> p r h q", r=2, h=2))

        def gv(v):
            return v.m(lambda a: a.rearrange("p (h r) q -> p r h q", r=2))

        heads = []
        items = []
        for sidx, (t0, T, cond) in enumerate(SEQS):
            nctx = 4 if sidx == 0 else 0
            nblk = T // 128
            for kv in range(4):
                heads.append((sidx, t0, T, nctx, nblk, kv))
                for blk in range(nblk):
                    items.append((len(heads) - 1, blk))

        def head_loads(hidx):
            sidx, t0, T, nctx, nblk, kv = heads[hidx]
            hb = hidx % 2
            kt_, vh_, qa, qb_ = KT[hb], Vh[hb], QA[hb], QB[hb]
            for half in range(2):
                rows = slice(half * 64, (half + 1) * 64)
                if nctx:
                    k.dma(kt_.k(("c", half))[rows, 0:512], self.wk_T.k()[kv], eng="pool")
                for c0 in range(0, T, 2048):
                    cw_ = min(2048, T - c0)
                    k.dma(kt_.k(("o", half, c0))[rows, nctx * 128 + c0:nctx * 128 + c0 + cw_],
                          self.KdT.k()[kv * 64:(kv + 1) * 64, t0 + c0:t0 + c0 + cw_])
            if nctx:
                k.dma(vh_.k(("c",))[:, 0:4, 0:64], self.wv.k()[kv].m(lambda a: a.rearrange("(j p) v -> p j v", p=128)), eng="pool")
            for j0 in range(0, nblk, 8):
                jn = min(8, nblk - j0)
                k.dma(vh_.k(("o", j0))[:, nctx + j0:nctx + j0 + jn, 0:64],
                      self.Vd.k()[t0 + j0 * 128:t0 + (j0 + jn) * 128, kv * 64:(kv + 1) * 64].m(
                          lambda a: a.rearrange("(j p) v -> p j v", p=128)))
            for c0 in range(0, T, 2048):
                cw_ = min(2048, T - c0)
                k.dma(qa.k(("q", c0))[:, c0:c0 + cw_], self.QdT.k()[kv * 256:kv * 256 + 128, t0 + c0:t0 + c0 + cw_])
                k.dma(qb_.k(("q", c0))[:, c0:c0 + cw_], self.QdT.k()[kv * 256 + 128:kv * 256 + 256, t0 + c0:t0 + c0 + cw_])

        def blk_loads(idx):
            hidx, blk = items[idx]
            sidx, t0, T, nctx, nblk, kv = heads[hidx]
            q0 = t0 + blk * 128
            k.dma(szd[idx % 3].k(), self.szdT.k()[kv * 256:(kv + 1) * 256, q0:q0 + 128].m(
                lambda a: a.rearrange("(g d) q -> d g q", d=64)))

        head_loads(0)
        blk_loads(0)
        deferred = []
        ctxs = []
        for idx, (hidx, blk) in enumerate(items):
            sidx, t0, T, nctx, nblk, kv = heads[hidx]
            if sidx == 0:
                kts = [(c, ("c",), None) for c in range(4)]
                for off, msk in ((-1, nmp), (0, None), (1, nmn)):
                    if 0 <= blk + off < nblk:
                        kts.append((nctx + blk + off, ("o",), msk))
            else:
                kts = [(c, ("o",), None) for c in range(nblk)]
            ctxs.append(kts)
        steps = [(idx, n_) for idx in range(len(items)) for n_ in range(len(ctxs[idx]))]
        Pq = {}

        def emit_qk(si):
            idx, n_ = steps[si]
            hidx, blk = items[idx]
            sidx, t0, T, nctx, nblk, kv = heads[hidx]
            hb = hidx % 2
            kt_, qtile = KT[hb], (QA[hb], QB[hb])
            kt, key, msk = ctxs[idx][n_]
            pSA = PS[4 + 2 * (gcnt["S"] % 2)]
            pSB = PS[5 + 2 * (gcnt["S"] % 2)]
            gcnt["S"] += 1
            Pt = P[gcnt["P"] % 4]
            gcnt["P"] += 1
            for g in range(4):
                rows = slice((g % 2) * 64, (g % 2 + 1) * 64)
                pS = pSA if g % 2 == 0 else pSB
                o = pS.k()[:, (g // 2) * 128:(g // 2 + 1) * 128]
                kk_ = ("c", g % 2) if key[0] == "c" else ("o", g % 2, ((kt - nctx) * 128) // 2048 * 2048)
                if msk is not None:
                    k.mm(o, self.identb.k(), msk.k(), start=True, stop=False)
                k.mm(o, kt_.k(kk_)[rows, kt * 128:(kt + 1) * 128],
                     qtile[g // 2].k(("q", (blk * 128) // 2048 * 2048))[rows, blk * 128:(blk + 1) * 128],
                     start=(msk is None), stop=True)
            k.I("act", "activation", out=Pt.k()[:, 0:256], in_=pSA.k()[:, 0:256], func=AF.Exp, scale=SC)
            k.I("act", "activation", out=Pt.k()[:, 256:512], in_=pSB.k()[:, 0:256], func=AF.Exp, scale=SC)
            Pq[si] = Pt

        emit_qk(0)
        for si, (idx, n_) in enumerate(steps):
            hidx, blk = items[idx]
            sidx, t0, T, nctx, nblk, kv = heads[hidx]
            hb = hidx % 2
            vh_ = Vh[hb]
            kts = ctxs[idx]
            if n_ == 0:
                if blk == 0 and hidx + 1 < len(heads):
                    head_loads(hidx + 1)
                if idx + 1 < len(items):
                    blk_loads(idx + 1)
            q0 = t0 + blk * 128
            frow = slice(kv * 256, (kv + 1) * 256)
            pO = PS[idx % 2]
            if si + 1 < len(steps):
                emit_qk(si + 1)
            kt, key, msk = kts[n_]
            Pt = Pq.pop(si)
            vkey = ("c",) if key[0] == "c" else ("o", (kt - nctx) // 8 * 8)
            k.mm(pO.k(), vh_.k(vkey)[:, kt, :], Pt.k(), start=(n_ == 0), stop=(n_ == len(kts) - 1))
            while deferred and deferred[0][0] <= n_:
                deferred.pop(0)[1]()
            if n_ < len(kts) - 1:
                continue
            while deferred:
                deferred.pop(0)[1]()

            def mk_tail(pO=pO, kv=kv, q0=q0, frow=frow, idx=idx):
                szd_ = szd[idx % 3]
                ost_ = ost[idx % 2]

                def d0():
                    k.I("dve", "tensor_copy", out=osb.k(), in_=pO.k())
                    k.dma(zmv.k(), osb.k()[64:128, :], semkey=("zmv",))

                def d1():
                    k.I("dve", "tensor_tensor", out=pv(zt.k()), in0=pv(zmv.k()),
                        in1=esink.k()[0:64, kv * 4:kv * 4 + 4].m(
                            lambda a: a.rearrange("p (h r) -> p r h", r=2).unsqueeze(3).to_broadcast([64, 2, 2, 128])), op=ALU.add)
                    k.I("dve", "reciprocal", out=zt.k(), in_=zt.k())
                    k.I("dve", "tensor_tensor", out=od.k(), in0=osb.k()[0:64, :], in1=zt.k(), op=ALU.mult)

                def d2():
                    k.I("pool", "tensor_tensor", out=gv(ost_.k()), in0=pv(od.k()), in1=gv(szd_.k()), op=ALU.mult)
                    k.dma(self.odT.k(("kv", kv, q0))[frow, q0:q0 + 128].m(lambda a: a.rearrange("(g d) q -> d g q", d=64)), ost_.k())
                return [(-1, d0), (1, d1), (3, d2)]
            tl = mk_tail()
            tl.pop(0)[1]()
            deferred.extend(tl)
        while deferred:
            deferred.pop(0)[1]()
        k.release()

    def passH(self):
        k = self.k
        k.mark()
        PS = self.PS
        wout = k.sb("wout1", [128, 16, 1024], BF16)
        for q in range(4):
            k.dma(wout.k(("q", q))[:, q * 4:(q + 1) * 4, :], self.od_w_out.k()[:, q * 4:(q + 1) * 4, :], eng="pool")
        fg = k.sb("fg", [128, 1024], F32)
        k.dma(fg.k(), self.fnorm_g.k().m(lambda a: a.partition_broadcast(128)))

        def mk(name, shape, dt_):
            return [k.sb(f"{name}{i}", shape, dt_) for i in range(2)]
        oct_ = [k.sb(f"hoc{i}", [128, 8, 128], BF16) for i in range(3)]
        odt_ = [k.sb(f"hod{i}", [128, 8, 128], BF16) for i in range(3)]
        xr = [k.sb(f"hxr{i}", [128, 1024], F32) for i in range(3)]
        xo = mk("hxo", [128, 1024], F32)
        sq = k.sb("hsq", [128, 1024], F32)
        ss = mk("hss", [128, 1], F32)
        rs = mk("hrs", [128, 1], F32)
        rstd = mk("hrstd", [128, 1], F32)
        yo = mk("hyo", [128, 1024], F32)
        ocv = self.ocT.k().m(lambda a: a.rearrange("(j p) t -> p j t", p=128))
        odv = self.odT.k().m(lambda a: a.rearrange("(j p) t -> p j t", p=128))
        def loads(i):
            b = i % 3
            sl = slice(i * 128, (i + 1) * 128)
            k.dma(oct_[b].k(), V(self.ocT.res, None, ocv.ap[:, :, sl]))
            k.dma(odt_[b].k(), V(self.odT.res, None, odv.ap[:, :, sl]))
            k.dma(xr[b].k(), self.x1.k()[sl, :])

        def compute(i):
            b3 = i % 3
            b = i % 2
            cond = 0 if i < 32 else 1
            sl = slice(i * 128, (i + 1) * 128)
            for nb in range(2):
                po = PS[2 * b + nb]
                for kk in range(16):
                    lhs = oct_[b3].k()[:, kk, :] if kk < 8 else odt_[b3].k()[:, kk - 8, :]
                    k.mm(po.k(), lhs, wout.k()[:, kk, nb * 512:(nb + 1) * 512], start=(kk == 0), stop=(kk == 15))
                k.I("dve", "tensor_tensor", out=xo[b].k()[:, nb * 512:(nb + 1) * 512], in0=po.k(),
                    in1=self.gate_bc[1][cond].k()[:, nb * 512:(nb + 1) * 512], op=ALU.mult)
            k.I("pool", "tensor_tensor", out=xo[b].k(), in0=xo[b].k(), in1=xr[b3].k(), op=ALU.add)
            k.I("act", "activation", out=sq.k(), in_=xo[b].k(), func=AF.Square, accum_out=ss[b].k())
            k.I("act", "activation", out=rs[b].k(), in_=ss[b].k(), func=AF.Sqrt, scale=1.0 / D, bias=self.epsc.k())
            k.I("dve", "reciprocal", out=rstd[b].k(), in_=rs[b].k())
            k.I("dve", "scalar_tensor_tensor", out=yo[b].k(), in0=xo[b].k(), scalar=rstd[b].k(), in1=fg.k(), op0=ALU.mult, op1=ALU.mult)
            k.dma(self.y_all.k(("t", i))[sl, :], yo[b].k(), final=True)

        loads(0)
        loads(1)
        for i in range(NT):
            if i + 2 < NT:
                loads(i + 2)
            compute(i)
        k.release()

def make_consts():
    c = {}
    c["c_ident"] = np.eye(128, dtype=np.float32)
    t = np.arange(128)
    c["c_triu"] = (t[:, None] <= t[None, :]).astype(np.float32)
    c["c_tril"] = (t[:, None] >= t[None, :]).astype(np.float32)
    c["c_nmf"] = np.where(t[None, :] < t[:, None], -30000.0, 0.0).astype(np.float32)
    c["c_nmb"] = np.where(t[None, :] > t[:, None], -30000.0, 0.0).astype(np.float32)
    sel = np.zeros((64, 32, 128), np.float32)
    c["c_sel"] = sel
    T = 4096
    row = np.repeat(np.arange(T // 64), 64).astype(np.float32)
    col = np.tile(np.arange(64), T // 64).astype(np.float32)
    nf = 16
    inv = (10000.0 ** (-np.arange(nf, dtype=np.float32) / nf)).astype(np.float32)
    ang = np.stack([row[:, None] * inv, col[:, None] * inv], axis=1)
    cos = np.cos(ang).astype(np.float32)
    sin = np.sin(ang).astype(np.float32)
    ct = np.zeros((64, T), np.float32)
    st = np.zeros((64, T), np.float32)
    for ax in range(2):
        for half in range(2):
            for f in range(nf):
                d = ax * 32 + half * 16 + f
                ct[d] = cos[:, ax, f]
                st[d] = sin[:, ax, f] * (-1.0 if half == 0 else 1.0)
    pm = np.zeros((128, 128), np.float32)
    for fo in range(128):
        d = fo % 32
        fi = fo + 16 if d < 16 else fo - 16
        pm[fi, fo] = 1.0
    c["c_pm"] = pm
    c["c_cos"] = np.concatenate([ct, ct], 0)
    c["c_sin"] = np.concatenate([st, st], 0)
    pe = np.zeros((128, 4, 16), np.float32)
    for g, w in enumerate((2, 4, 8, 16)):
        left = w // 2
        right = w - 1 - left
        for j in range(8):
            pe[:, g, j] = 1.0 / ((j + right + 1) - max(j - left, 0))
            pe[:, g, 8 + j] = 1.0 / (min(right + 1, 8 - j) + left)
    c["c_pedge"] = pe
    return c


def prep_core_inputs(inp, core, consts):
    f = np.float32
    m = {}
    xs = np.asarray(inp["x_sample"][core], f)
    xp = np.asarray(inp["x_prompt"][2 * core:2 * core + 2], f).reshape(512, D)
    m["x_all"] = np.ascontiguousarray(np.concatenate([xs, xp], 0))
    cc = np.concatenate([np.asarray(inp["c"][core], f).reshape(8, 128).T,
                         np.asarray(inp["c_ctx"], f).reshape(8, 128).T], 1)
    m["cc"] = np.ascontiguousarray(cc)
    m.update(consts)
    return m


def prep_shared_inputs(inp):
    f = np.float32
    m = {}
    wa = np.asarray(inp["w_ada"], f)
    m["w_ada"] = np.ascontiguousarray(wa.reshape(2, 8, 128, 3072).transpose(0, 2, 1, 3))
    ba = np.asarray(inp["b_ada"], f)
    m["b_ada_col"] = np.ascontiguousarray(ba.reshape(2, 24, 128).transpose(0, 2, 1))
    m["b_ada_row"] = np.ascontiguousarray(ba)
    m["ev_w_in"] = np.ascontiguousarray(np.asarray(inp["ev_w_in"], f)[0].reshape(8, 128, 5152).transpose(1, 0, 2))
    cw = np.asarray(inp["ev_conv_w"], f)[0]
    m["conv_w"] = np.ascontiguousarray(cw.reshape(5, 16, 128).transpose(2, 1, 0))
    m["conv_b"] = np.ascontiguousarray(np.asarray(inp["ev_conv_b"], f)[0].reshape(16, 128).T)
    m["a_log"] = np.ascontiguousarray(np.asarray(inp["ev_A_log"], f)[0].reshape(1, 32))
    m["dt_bias"] = np.ascontiguousarray(np.asarray(inp["ev_dt_bias"], f)[0].reshape(1, 32))
    m["d_skip"] = np.ascontiguousarray(np.asarray(inp["ev_D"], f).reshape(1, 16))
    m["norm_g"] = np.ascontiguousarray(np.asarray(inp["ev_norm_g"], f)[0].reshape(8, 128).T)
    m["pool_scale"] = np.ascontiguousarray(np.asarray(inp["ev_pool_scale"], f)[0].reshape(8, 128).T)
    pw = np.asarray(inp["ev_pool_w"], f)[0]
    m["pool_w"] = np.ascontiguousarray(pw.reshape(4, 2, 128, 256).transpose(0, 2, 1, 3))
    m["ev_w_out"] = np.ascontiguousarray(np.asarray(inp["ev_w_out"], f)[0].reshape(16, 128, 1024).transpose(1, 0, 2))
    m["od_w_in"] = np.ascontiguousarray(np.asarray(inp["od_w_in"], f)[0].reshape(8, 128, 6656).transpose(1, 0, 2))
    m["od_w_out"] = np.ascontiguousarray(np.asarray(inp["od_w_out"], f)[0].reshape(16, 128, 1024).transpose(1, 0, 2))
    m["od_lambda"] = np.ascontiguousarray(np.asarray(inp["od_lambda"], f)[0].reshape(1, 256))
    m["subln_g"] = np.ascontiguousarray(np.asarray(inp["od_subln_g"], f)[0].reshape(128, 1))
    m["sink"] = np.ascontiguousarray(np.asarray(inp["od_sink"], f)[0].reshape(1, 16))
    m["fnorm_g"] = np.ascontiguousarray(np.asarray(inp["final_norm_g"], f).reshape(1, 1024))
    return m


def prep_core_caches(inp, core, m):
    f = np.float32
    m["hf0"] = np.ascontiguousarray(np.asarray(inp["state_ssd_fwd"], f)[core, 0].reshape(1024, 128).T)
    m["hb0"] = np.ascontiguousarray(np.asarray(inp["state_ssd_bwd"], f)[core, 0].reshape(1024, 128).T)
    m["dk_T"] = np.ascontiguousarray(np.asarray(inp["cache_diff_k"], f)[core, 0].transpose(0, 2, 1))
    m["dv"] = np.ascontiguousarray(np.asarray(inp["cache_diff_v"], f)[core, 0])
    m["wk_T"] = np.ascontiguousarray(np.asarray(inp["cache_win_k"], f)[core, 0].transpose(0, 2, 1))
    m["wv"] = np.ascontiguousarray(np.asarray(inp["cache_win_v"], f)[core, 0])
    return m


_PROGRAM = {}


def _get_program():
    if "nc" not in _PROGRAM:
        b = Builder(debug=False)
        _PROGRAM["nc"] = b.nc
    return _PROGRAM["nc"]


def kernel(**inputs):
    n = 8
    nc = _get_program()
    consts = make_consts()
    shared = prep_shared_inputs(inputs)
    in_maps = []
    for core in range(n):
        m = prep_core_inputs(inputs, core, consts)
        m.update(shared)
        prep_core_caches(inputs, core, m)
        in_maps.append(m)
    res = run_bass_kernel_spmd(nc, in_maps, core_ids=list(range(n)))
    R = res.results
    f = np.float32
    y_prompt = np.zeros((16, 256, D), f)
    y_sample = np.zeros((8, 4096, D), f)
    ssd_f = np.zeros((16, 1, 16, 64, 128), f)
    ssd_b = np.zeros((16, 1, 16, 64, 128), f)
    dk = np.zeros((16, 1, 8, 256, 128), f)
    dv = np.zeros((16, 1, 8, 256, 128), f)
    wk = np.zeros((16, 1, 4, 256, 64), f)
    wv = np.zeros((16, 1, 4, 256, 64), f)
    for c in range(n):
        r = R[c]
        ya = np.asarray(r["y_all"], f)
        y_sample[c] = ya[0:4096]
        y_prompt[2 * c:2 * c + 2] = ya[4096:4608].reshape(2, 256, D)
        ssd_f[2 * c:2 * c + 2, 0] = np.asarray(r["o_ssdf"], f).reshape(2, 16, 64, 128)
        ssd_b[2 * c:2 * c + 2, 0] = np.asarray(r["o_ssdb"], f).reshape(2, 16, 64, 128)
        dk[2 * c:2 * c + 2, 0] = np.asarray(r["o_dk"], f).reshape(2, 256, 8, 128).transpose(0, 2, 1, 3)
        dv[2 * c:2 * c + 2, 0] = np.asarray(r["o_dv"], f).reshape(2, 256, 8, 128).transpose(0, 2, 1, 3)
        wk[2 * c:2 * c + 2, 0] = np.asarray(r["o_wk"], f).reshape(2, 256, 4, 64).transpose(0, 2, 1, 3)
        wv[2 * c:2 * c + 2, 0] = np.asarray(r["o_wv"], f).reshape(2, 256, 4, 64).transpose(0, 2, 1, 3)
    return (y_prompt, y_sample, ssd_f, ssd_b, dk, dv, wk, wv)
```

```python
import numpy as np
import concourse.bass as bass
import concourse.mybir as mybir

F32 = mybir.dt.float32
BF16 = mybir.dt.bfloat16
AF = mybir.ActivationFunctionType
ALU = mybir.AluOpType

ENGS = ("pe", "act", "dve", "pool", "sp")


class Op:
    __slots__ = ("eng", "fn", "deps", "signal", "sigval", "is_dma", "sem", "seq", "is_barrier")

    def __init__(self, eng, fn, is_dma=False):
        self.eng = eng
        self.fn = fn
        self.deps = set()
        self.signal = False
        self.sigval = 0
        self.is_dma = is_dma
        self.sem = None
        self.seq = 0
        self.is_barrier = False


class Res:
    _uid = 0

    def __init__(self, name):
        self.name = name
        self.lastw = {}
        self.readers = {}
        Res._uid += 1
        self.uid = Res._uid

    def _writers(self, key):
        if key is None:
            return list(self.lastw.values())
        out = []
        if key in self.lastw:
            out.append(self.lastw[key])
        if None in self.lastw:
            out.append(self.lastw[None])
        return out

    def read(self, op, key):
        for w in self._writers(key):
            op.deps.add(w)
        self.readers.setdefault(key, []).append(op)

    @staticmethod
    def _add_readers(op, lst):
        last = {}
        for r in lst:
            if r.is_dma:
                op.deps.add(r)
            else:
                p = last.get(r.eng)
                if p is None or r.seq > p.seq:
                    last[r.eng] = r
        for r in last.values():
            op.deps.add(r)

    def write(self, op, key):
        for w in self._writers(key):
            op.deps.add(w)
        if key is None:
            for lst in self.readers.values():
                self._add_readers(op, lst)
            self.lastw = {None: op}
            self.readers = {}
        else:
            self._add_readers(op, self.readers.get(key, ()))
            self._add_readers(op, self.readers.get(None, ()))
            self.lastw[key] = op
            self.readers[key] = []


class V:
    __slots__ = ("res", "key", "ap")

    def __init__(self, res, key, ap):
        self.res = res
        self.key = key
        self.ap = ap

    def __getitem__(self, idx):
        return V(self.res, self.key, self.ap[idx])

    def m(self, f):
        return V(self.res, self.key, f(self.ap))


class T:
    def __init__(self, handle, name, space="sb"):
        self.h = handle
        self.res = Res(name)
        self.res.space = space
        self.name = name

    def k(self, key=None):
        return V(self.res, key, self.h.ap())

    def __getitem__(self, idx):
        return V(self.res, None, self.h.ap()[idx])


class KB:
    def __init__(self):
        self.nc = bass.Bass("TRN2", target_bir_lowering=False)
        self.ops = []
        self.sb_lo = 16512
        self.sb_hi = 229344
        self.sb_ptr = self.sb_lo
        self.sb_stack = []
        self.nid = 0
        self.dma_sems = {}
        self.dma_last = {}
        self.dma_cnt = {}
        self.final_deps = []
        self.last_op = {}

    def sb(self, name, shape, dtype):
        nbytes = int(np.prod(shape[1:])) * (4 if dtype == F32 else 2)
        off = (self.sb_ptr + 31) // 32 * 32
        assert off + nbytes <= self.sb_hi, f"SBUF overflow allocating {name}: {off}+{nbytes}"
        self.sb_ptr = off + nbytes
        self.nid += 1
        h = self.nc.alloc_sbuf_tensor_at(f"{name}_{self.nid}", list(shape), dtype, offset=off)
        return T(h, name)

    def mark(self):
        self.sb_stack.append(self.sb_ptr)

    def release(self):
        self.barrier()
        self.sb_ptr = self.sb_stack.pop()

    def psum(self, name, shape, dtype=F32):
        h = self.nc.alloc_psum_tensor(name, list(shape), dtype)
        return T(h, name, "ps")

    def dram(self, name, shape, dtype, kind=None):
        if kind is None:
            h = self.nc.dram_tensor(name, list(shape), dtype)
        else:
            h = self.nc.dram_tensor(name, list(shape), dtype, kind=kind)
        return T(h, name, "dram")

    def _reg(self, op, reads, writes):
        for v in reads:
            v.res.read(op, v.key)
        for v in writes:
            v.res.write(op, v.key)
        op.deps.discard(op)
        op.seq = len(self.ops)
        self.ops.append(op)
        self.last_op[op.eng] = op

    def I(self, eng, meth, wr=("out",), extra_r=(), extra_w=(), **kw):
        reads, writes = list(extra_r), list(extra_w)
        args = {}
        for k_, v in kw.items():
            if isinstance(v, V):
                args[k_] = v.ap
                if k_ in wr or k_ == "accum_out":
                    writes.append(v)
                else:
                    reads.append(v)
            else:
                args[k_] = v

        def fn(e, meth=meth, args=args):
            return getattr(e, meth)(**args)

        op = Op(eng, fn)
        self._reg(op, reads, writes)
        return op

    def mm(self, out, lhsT, rhs, start=True, stop=True, **kw):
        args = dict(start=start, stop=stop, **kw)

        def fn(e, o=out.ap, l=lhsT.ap, r=rhs.ap, args=args):
            return e.matmul(o, l, r, **args)

        op = Op("pe", fn)
        self._reg(op, [lhsT, rhs], [out])
        return op

    def tr(self, out, in_, ident):
        def fn(e, o=out.ap, i=in_.ap, d=ident.ap):
            return e.transpose(o, i, d)

        op = Op("pe", fn)
        self._reg(op, [in_, ident], [out])
        return op

    def dma(self, out, in_, eng="sp", semkey=None, final=False):
        def fn(e, o=out.ap, i=in_.ap):
            return e.dma_start(out=o, in_=i)

        op = Op(eng, fn, is_dma=True)
        if semkey is None:
            side = out if out.res.space != "dram" else in_
            semkey = ("t", side.res.uid, side.key)
        op.sem = semkey
        prev = self.dma_last.get(semkey)
        if prev is not None:
            op.deps.add(prev)
        self.dma_last[semkey] = op
        self._reg(op, [in_], [out])
        if final:
            self.final_deps.append(op)
        return op

    def barrier(self):
        lasts = [o for o in self.last_op.values()] + list(self.dma_last.values())
        for e in ENGS:
            op = Op(e, None)
            op.is_barrier = True
            for l in lasts:
                op.deps.add(l)
            op.seq = len(self.ops)
            self.ops.append(op)
            self.last_op[e] = op

    def finish(self, same_eng_sync=True):
        nc = self.nc
        fin = Op("sp", None)
        for d in self.final_deps:
            fin.deps.add(d)
        fin.seq = len(self.ops)
        self.ops.append(fin)
        def needs_sig(op, d):
            if d.fn is None:
                return False
            if d.is_dma:
                return True
            if d.eng == op.eng and not op.is_dma and (op.eng == "pe" or not same_eng_sync):
                return False
            return True
        for op in self.ops:
            op.deps.discard(op)
            for d in op.deps:
                if needs_sig(op, d):
                    d.signal = True
        def expand(op):
            out = set()
            stack = list(op.deps)
            while stack:
                d = stack.pop()
                if d.fn is None:
                    stack.extend(d.deps)
                else:
                    out.add(d)
            return out
        for op in self.ops:
            if any(d.fn is None for d in op.deps):
                op.deps = expand(op)
                for d in op.deps:
                    if needs_sig(op, d):
                        d.signal = True
        esem = {e: nc.alloc_semaphore(f"s_{e}") for e in ("pe", "act", "dve", "pool")}
        ecnt = {e: 0 for e in esem}
        dsem = {}
        active = {}
        free = {False: [], True: []}
        nslots = [0]
        for op in self.ops:
            if op.fn is None:
                if op.is_barrier and op.eng == "pe":
                    for key, slot in active.items():
                        free[slot[2]].append(slot)
                    active = {}
                continue
            if op.is_dma:
                sw = (op.eng == "pool")
                if op.sem not in active:
                    if free[sw]:
                        slot = free[sw].pop()
                    else:
                        slot = [nc.alloc_semaphore(f"d{nslots[0]}"), 0, sw]
                        nslots[0] += 1
                    active[op.sem] = slot
                slot = active[op.sem]
                assert slot[2] == sw, f"semkey {op.sem} mixes SW and HW DGE"
                slot[1] += 16
                op.sigval = slot[1]
                op.signal = True
                op.sem = ("slot", id(slot), op.seq)
                dsem[op.sem] = slot[0]
            elif op.signal:
                ecnt[op.eng] += 1
                op.sigval = ecnt[op.eng]
        self._slots_keepalive = (active, free)
        self.sem_stats = (dict(ecnt), nslots[0], max([sl[1] for sl in free[False] + free[True] + list(active.values())] + [0]))
        self.n_dma_sems = nslots[0]
        streams = {e: [o for o in self.ops if o.eng == e] for e in ENGS}
        engobj = {"pe": "tensor", "act": "scalar", "dve": "vector", "pool": "gpsimd", "sp": "sync"}

        def emit(e, eng):
            known = {}
            for op in streams[e]:
                for d in sorted(op.deps, key=lambda o: o.seq):
                    if d.is_dma:
                        sem, val = dsem[d.sem], d.sigval
                    else:
                        if d.eng == e and not op.is_dma:
                            if e == "pe" or not same_eng_sync:
                                continue
                        sem, val = esem[d.eng], d.sigval
                    kk = id(sem)
                    if known.get(kk, 0) >= val:
                        continue
                    known[kk] = val
                    eng.wait_ge(sem, val)
                if op.fn is None:
                    continue
                inst = op.fn(eng)
                if op.is_dma:
                    inst.then_inc(dsem[op.sem], 16)
                elif op.signal:
                    inst.then_inc(esem[op.eng], 1)

        with nc.Block() as block:
            @block.tensor
            def _(eng):
                emit("pe", eng)

            @block.scalar
            def _(eng):
                emit("act", eng)

            @block.vector
            def _(eng):
                emit("dve", eng)

            @block.gpsimd
            def _(eng):
                emit("pool", eng)

            @block.sync
            def _(eng):
                emit("sp", eng)
        return nc
from concourse.bass_utils import run_bass_kernel_spmd
import math
import ml_dtypes

NTOK = 4608
NT = 36
D = 1024
EPS = 1e-6
SEQS = [(0, 4096, 0), (4096, 256, 1), (4352, 256, 1)]
SEGS = [(b * 512, 512) for b in range(8)] + [(4096, 256), (4352, 256)]
LAM_INIT = 0.8 - 0.6 * math.exp(-0.3 * 1)


def seq_of(tok):
    return 0 if tok < 4096 else (1 if tok < 4352 else 2)


def scol(tok):
    return tok + 2 + 4 * seq_of(tok)


def pcol(tok):
    return tok + 8 + 16 * seq_of(tok)


class Builder:
    def __init__(self, debug=False, stop_after=None):
        self.k = KB()
        self.debug = debug
        self.stop_after = stop_after
        self.dbg_outs = []
        self.build()

    def din(self, name, shape, dtype=F32):
        return self.k.dram(name, shape, dtype, kind="ExternalInput")

    def dout(self, name, shape, dtype=F32):
        return self.k.dram(name, shape, dtype, kind="ExternalOutput")

    def scratch(self, name, shape, dtype):
        if self.debug:
            self.dbg_outs.append(name)
            return self.k.dram(name, shape, dtype, kind="ExternalOutput")
        return self.k.dram(name, shape, dtype)

    def dump(self, name, view, shape, dtype=F32):
        if not self.debug:
            return
        t = self.k.dram("dbg_" + name, list(shape), dtype, kind="ExternalOutput")
        self.dbg_outs.append("dbg_" + name)
        self.k.dma(t.k(), view, eng="sp", semkey=("dbg", name))

    def next_ps(self):
        self.ps_i = (self.ps_i + 1) % len(self.PS)
        return self.PS[self.ps_i]

    def build(self):
        k = self.k
        self.x_all = self.din("x_all", [NTOK, D])
        self.cc = self.din("cc", [128, 16])
        self.w_ada = self.din("w_ada", [2, 128, 8, 3072])
        self.b_ada_col = self.din("b_ada_col", [2, 128, 24])
        self.b_ada_row = self.din("b_ada_row", [2, 3072])
        self.ev_w_in = self.din("ev_w_in", [128, 8, 5152])
        self.conv_w = self.din("conv_w", [128, 16, 5])
        self.conv_b = self.din("conv_b", [128, 16])
        self.a_log = self.din("a_log", [1, 32])
        self.dt_bias = self.din("dt_bias", [1, 32])
        self.d_skip = self.din("d_skip", [1, 16])
        self.norm_g = self.din("norm_g", [128, 8])
        self.pool_scale = self.din("pool_scale", [128, 8])
        self.pool_w = self.din("pool_w", [4, 128, 2, 256])
        self.ev_w_out = self.din("ev_w_out", [128, 16, 1024])
        self.od_w_in = self.din("od_w_in", [128, 8, 6656])
        self.od_w_out = self.din("od_w_out", [128, 16, 1024])
        self.od_lambda = self.din("od_lambda", [1, 256])
        self.subln_g = self.din("subln_g", [128, 1])
        self.sink = self.din("sink", [1, 16])
        self.fnorm_g = self.din("fnorm_g", [1, 1024])
        self.hf0 = self.din("hf0", [128, 1024])
        self.hb0 = self.din("hb0", [128, 1024])
        self.dk_T = self.din("dk_T", [8, 128, 512])
        self.dv = self.din("dv", [8, 512, 128])
        self.wk_T = self.din("wk_T", [4, 64, 512])
        self.wv = self.din("wv", [4, 512, 64])
        self.c_ident = self.din("c_ident", [128, 128])
        self.c_triu = self.din("c_triu", [128, 128])
        self.c_tril = self.din("c_tril", [128, 128])
        self.c_nmf = self.din("c_nmf", [128, 128])
        self.c_nmb = self.din("c_nmb", [128, 128])
        self.c_sel = self.din("c_sel", [64, 32, 128])
        self.c_pm = self.din("c_pm", [128, 128])
        self.c_cos = self.din("c_cos", [128, 4096])
        self.c_sin = self.din("c_sin", [128, 4096])
        self.c_pedge = self.din("c_pedge", [128, 4, 16])
        self.y_all = self.dout("y_all", [NTOK, D])
        self.o_ssdf = self.dout("o_ssdf", [2, 1024, 128])
        self.o_ssdb = self.dout("o_ssdb", [2, 1024, 128])
        self.o_dk = self.dout("o_dk", [512, 1024])
        self.o_dv = self.dout("o_dv", [512, 1024])
        self.o_wk = self.dout("o_wk", [512, 256])
        self.o_wv = self.dout("o_wv", [512, 256])
        self.sza = self.scratch("sza", [NTOK, 1024], F32)
        self.xcT = self.scratch("xcT", [2048, NTOK], BF16)
        self.ypT = self.scratch("ypT", [1024, NTOK], BF16)
        self.x1 = self.scratch("x1", [NTOK, D], F32)
        self.yloc = self.scratch("yloc", [NTOK, 1024], F32)
        self.QcT = self.scratch("QcT", [1024, NTOK], BF16)
        self.KcT = self.scratch("KcT", [1024, NTOK], BF16)
        self.Vc = self.scratch("Vc", [NTOK, 1024], BF16)
        self.szcT = self.scratch("szcT", [1024, NTOK], F32)
        self.QdT = self.scratch("QdT", [1024, NTOK], BF16)
        self.KdT = self.scratch("KdT", [256, NTOK], BF16)
        self.Vd = self.scratch("Vd", [NTOK, 256], BF16)
        self.szdT = self.scratch("szdT", [1024, NTOK], F32)
        self.ocT = self.scratch("ocT", [1024, NTOK], BF16)
        self.odT = self.scratch("odT", [1024, NTOK], BF16)
        self.yoff = [self.scratch(f"yoff{d}", [NTOK, 1024], F32) for d in range(2)]
        self.xw_s = [self.scratch(f"xw{d}", [NTOK, 1024], BF16) for d in range(2)]
        self.btok_s = self.scratch("btok", [NTOK, 512], BF16)

        self.PS = [k.psum(f"ps{i}", [128, 512], F32) for i in range(8)]
        self.ps_i = -1

        self.ident = k.sb("ident", [128, 128], F32)
        self.identb = k.sb("identb", [128, 128], BF16)
        self.epsc = k.sb("epsc", [128, 1], F32)
        self.modT = [k.sb(f"modT{l}", [128, 24, 2], F32) for l in range(2)]
        self.gate_bc = [[k.sb(f"gate{l}{c}", [128, 1024], F32) for c in range(2)] for l in range(2)]
        self.dt_all = k.sb("dt_all", [128, NT, 32], F32)
        self.e_all = k.sb("e_all", [128, NT, 32], F32)
        self.dec_all = k.sb("dec_all", [128, NT, 32], F32)
        k.dma(self.ident.k(), self.c_ident.k())
        k.dma(self.identb.k(), self.c_ident.k(), eng="pool")
        k.I("dve", "memset", wr=("ap",), ap=self.epsc.k(), constant=EPS)

        self.phase0()
        for l in range(2):
            self.dump(f"modT{l}", self.modT[l].k(), [128, 24, 2])
            for c in range(2):
                self.dump(f"gate{l}{c}", self.gate_bc[l][c].k(), [128, 1024])
        if self.stop_after == "p0":
            return self.finish_debug()
        k.mark()
        self.hT = k.sb("hT", [128, 8, NTOK], BF16)
        self.build_hT(self.x_all, 0)
        self.dump("hT", self.hT.k(), [128, 8, NTOK], BF16)
        self.passA1()
        self.dump("dt_all", self.dt_all.k(), [128, NT, 32])
        if self.stop_after == "A1":
            return self.finish_debug()
        self.passA2()
        k.release()
        if self.stop_after == "A2":
            return self.finish_debug()
        self.passB()
        if self.stop_after == "B":
            return self.finish_debug()
        self.passC()
        if self.stop_after == "C":
            return self.finish_debug()
        self.passD()
        if self.stop_after == "D":
            return self.finish_debug()
        if self.stop_after == "L1only":
            pass
        k.mark()
        self.hT = k.sb("hT", [128, 8, NTOK], BF16)
        self.build_hT(self.x1, 1)
        if self.stop_after == "hT1":
            return self.finish_debug()
        self.passE()
        k.release()
        if self.stop_after in ("E", "E00", "E0", "E0a", "E0b", "E1", "E2"):
            return self.finish_debug()
        self.passF()
        if self.stop_after == "F":
            return self.finish_debug()
        self.passG()
        if self.stop_after in ("G",):
            return self.finish_debug()
        self.passH()
        self.nc = k.finish()

    def finish_debug(self):
        k = self.k
        self.nc = k.finish()

    def phase0(self):
        k = self.k
        k.mark()
        cc_sb = k.sb("cc_sb", [128, 16], F32)
        sc = k.sb("sc", [128, 16], F32)
        scb = k.sb("scb", [128, 16, 128], F32)
        wada = k.sb("wada", [128, 8, 3072], F32)
        bcol = k.sb("bcol", [128, 24], F32)
        brow = k.sb("brow", [128, 1024], F32)
        k.dma(cc_sb.k(), self.cc.k())
        k.I("act", "activation", out=sc.k(), in_=cc_sb.k(), func=AF.Silu)
        k.I("dve", "tensor_copy", out=scb.k(), in_=sc.k().m(lambda a: a.unsqueeze(2).to_broadcast([128, 16, 128])))
        for l in range(2):
            for q in range(6):
                k.dma(wada.k(("q", q))[:, :, q * 512:(q + 1) * 512],
                      self.w_ada.k()[l, :, :, q * 512:(q + 1) * 512])
            k.dma(bcol.k(), self.b_ada_col.k()[l])
            k.dma(brow.k(), self.b_ada_row.k()[l:l + 1, 2048:3072].m(lambda a: a.partition_broadcast(128)))
            ps = self.next_ps()
            for j in range(24):
                q = j // 4
                for kk in range(8):
                    k.mm(ps.k()[:, 2 * j:2 * j + 2], wada.k(("q", q))[:, kk, j * 128:(j + 1) * 128],
                         sc.k()[:, kk:16:8], start=(kk == 0), stop=(kk == 7))
            k.I("dve", "tensor_tensor", out=self.modT[l].k(),
                in0=ps.k()[:, 0:48].m(lambda a: a.rearrange("p (j c) -> p j c", c=2)),
                in1=bcol.k().m(lambda a: a.unsqueeze(2).to_broadcast([128, 24, 2])), op=ALU.add)
            k.I("dve", "tensor_scalar", out=self.modT[l].k()[:, 8:16, :], in0=self.modT[l].k()[:, 8:16, :],
                scalar1=1.0, scalar2=None, op0=ALU.add)
            for cond in range(2):
                for n in range(2):
                    ps2 = self.next_ps()
                    for kk in range(8):
                        k.mm(ps2.k(), scb.k()[:, cond * 8 + kk, :],
                             wada.k(("q", 4 + n))[:, kk, 2048 + n * 512:2048 + (n + 1) * 512],
                             start=(kk == 0), stop=(kk == 7))
                    k.I("dve", "tensor_tensor", out=self.gate_bc[l][cond].k()[:, n * 512:(n + 1) * 512],
                        in0=ps2.k(), in1=brow.k()[:, n * 512:(n + 1) * 512], op=ALU.add)
        k.release()

    def build_hT(self, xsrc, l):
        k = self.k
        k.mark()
        xr = [k.sb(f"xr{i}", [128, 1024], F32) for i in range(2)]
        xn = [k.sb(f"xn{i}", [128, 1024], F32) for i in range(2)]
        sq = k.sb("sq", [128, 1024], F32)
        ss = [k.sb(f"ss{i}", [128, 1], F32) for i in range(2)]
        rs = [k.sb(f"rs{i}", [128, 1], F32) for i in range(2)]
        rstd = [k.sb(f"rstd{i}", [128, 1], F32) for i in range(2)]
        xr.append(k.sb("xr2", [128, 1024], F32))
        pss = {}

        def stage_a(i):
            b = i % 2
            b3 = i % 3
            k.dma(xr[b3].k(), xsrc.k()[i * 128:(i + 1) * 128, :])
            k.I("act", "activation", out=sq.k(), in_=xr[b3].k(), func=AF.Square, accum_out=ss[b].k())
            k.I("act", "activation", out=rs[b].k(), in_=ss[b].k(), func=AF.Sqrt, scale=1.0 / D, bias=self.epsc.k())
            k.I("dve", "reciprocal", out=rstd[b].k(), in_=rs[b].k())
            k.I("dve", "tensor_scalar", out=xn[b].k(), in0=xr[b3].k(), scalar1=rstd[b].k(), scalar2=None, op0=ALU.mult)
            lst = []
            for half in range(2):
                ps = self.next_ps()
                for q in range(4):
                    c = half * 4 + q
                    k.tr(ps.k()[:, q * 128:(q + 1) * 128], xn[b].k()[:, c * 128:(c + 1) * 128], self.ident.k())
                lst.append(ps)
            pss[i] = lst

        def stage_b(i):
            cond = 0 if i < 32 else 1
            for half in range(2):
                ps = pss[i][half]
                for q in range(4):
                    c = half * 4 + q
                    o = self.hT.k(("t", i))[:, c, i * 128:(i + 1) * 128]
                    sc_ = self.modT[l].k()[:, 8 + c, cond:cond + 1]
                    sh_ = self.modT[l].k()[:, c, cond:cond + 1]
                    if half == 0:
                        k.I("act", "activation", out=o, in_=ps.k()[:, q * 128:(q + 1) * 128],
                            func=AF.Identity, scale=sc_, bias=sh_)
                    else:
                        k.I("dve", "tensor_scalar", out=o, in0=ps.k()[:, q * 128:(q + 1) * 128],
                            scalar1=sc_, scalar2=sh_, op0=ALU.mult, op1=ALU.add)
            del pss[i]

        stage_a(0)
        for i in range(NT):
            if i + 1 < NT:
                stage_a(i + 1)
            stage_b(i)
        k.release()

    def hT_cols(self, t0, n):
        return [self.hT.k(("t", i)) for i in range(t0 // 128, (t0 + n + 127) // 128)]

    def passA1(self):
        k = self.k
        k.mark()
        wfm = [k.sb(f"wfm{i}", [128, 8, 128], BF16) for i in range(4)]
        wtm = [k.sb(f"wtm{i}", [128, 8, 512], BF16) for i in range(2)]
        wdt = k.sb("wdt", [128, 8, 32], BF16)
        strips = [k.sb(f"strip{i}", [128, 4620], BF16) for i in range(2)]
        dg = k.sb("dg", [128, 16, 5, 128], BF16)
        cw = k.sb("cw", [128, 16, 5], F32)
        cb = k.sb("cb", [128, 16], F32)
        stg = [k.sb(f"stg{i}", [128, 512], F32) for i in range(3)]
        stgc = [k.sb(f"stgc{i}", [128, 512], BF16) for i in range(3)]
        dtraw = k.sb("dtraw", [128, NT, 32], F32)
        dtb = k.sb("dtb", [128, 32], F32)
        hT = self.hT

        k.dma(cw.k(), self.conv_w.k())
        k.dma(cb.k(), self.conv_b.k())
        k.dma(dtb.k(), self.dt_bias.k().m(lambda a: a.partition_broadcast(128)))
        for j in range(16):
            k.I("dve", "tensor_tensor", out=dg.k(("j", j))[:, j, :, :],
                in0=self.ident.k().m(lambda a: a.unsqueeze(1).to_broadcast([128, 5, 128])),
                in1=cw.k()[:, j, :].m(lambda a: a.unsqueeze(2).to_broadcast([128, 5, 128])), op=ALU.mult)
        for s in strips:
            k.I("pool", "memset", wr=("ap",), ap=s.k(), constant=0.0)

        k.dma(wdt.k(), self.ev_w_in.k()[:, :, 3072:3104], eng="pool")
        for i in range(NT):
            ps = self.next_ps()
            for kk in range(8):
                k.mm(ps.k()[:, 0:32], hT.k(("t", i))[:, kk, i * 128:(i + 1) * 128], wdt.k()[:, kk, :],
                     start=(kk == 0), stop=(kk == 7))
            k.I("dve", "tensor_tensor", out=dtraw.k(("t", i))[:, i, :], in0=ps.k()[:, 0:32], in1=dtb.k(), op=ALU.add)
        k.I("dve", "tensor_scalar", out=dtraw.k(), in0=dtraw.k(), scalar1=30.0, scalar2=None, op0=ALU.min)
        k.I("act", "activation", out=dtraw.k(), in_=dtraw.k(), func=AF.Exp)
        k.I("act", "activation", out=self.dt_all.k(), in_=dtraw.k(), func=AF.Ln, bias=1.0)

        n_st = 0
        for fb in range(2):
            W = wtm[fb % 2]
            k.dma(W.k(), self.ev_w_in.k()[:, :, fb * 512:(fb + 1) * 512], eng="pool")
            for i in range(NT):
                ps = self.next_ps()
                for kk in range(8):
                    k.mm(ps.k(), hT.k(("t", i))[:, kk, i * 128:(i + 1) * 128], W.k()[:, kk, :],
                         start=(kk == 0), stop=(kk == 7))
                st = stg[n_st % 3]
                n_st += 1
                k.I("act", "activation", out=st.k(), in_=ps.k(), func=AF.Silu)
                k.dma(self.sza.k(("t", i, fb))[i * 128:(i + 1) * 128, fb * 512:(fb + 1) * 512], st.k())

        n_sc = 0
        n_ev = 0
        for j in range(16):
            W = wfm[j % 4]
            k.dma(W.k(), self.ev_w_in.k()[:, :, 1024 + j * 128:1024 + (j + 1) * 128], eng="pool")
            strip = strips[j % 2]
            for si, (t0, n) in enumerate(SEGS):
                ps = self.next_ps()
                for kk in range(8):
                    k.mm(ps.k()[:, 0:n], W.k()[:, kk, :], hT.k()[:, kk, t0:t0 + n],
                         start=(kk == 0), stop=(kk == 7))
                c0 = scol(t0)
                if n_ev % 2 == 0:
                    k.I("act", "activation", out=strip.k(("s", si))[:, c0:c0 + n], in_=ps.k()[:, 0:n], func=AF.Copy)
                else:
                    k.I("dve", "tensor_copy", out=strip.k(("s", si))[:, c0:c0 + n], in_=ps.k()[:, 0:n])
                n_ev += 1
            for si, (t0, n) in enumerate(SEGS):
                ps = self.next_ps()
                c0 = scol(t0)
                nb = [strip.k(("s", s2)) for s2 in (si - 1, si + 1) if 0 <= s2 < len(SEGS)]
                for tap in range(5):
                    op = k.mm(ps.k()[:, 0:n], dg.k(("j", j))[:, j, tap, :],
                              strip.k(("s", si))[:, c0 + tap - 2:c0 + tap - 2 + n],
                              start=(tap == 0), stop=(tap == 4))
                    for v in nb:
                        v.res.read(op, v.key)
                st = stgc[n_sc % 3]
                n_sc += 1
                k.I("act", "activation", out=st.k()[:, 0:n], in_=ps.k()[:, 0:n], func=AF.Silu, bias=cb.k()[:, j:j + 1])
                k.dma(self.xcT.k(("j", j, si))[j * 128:(j + 1) * 128, t0:t0 + n], st.k()[:, 0:n])
        k.release()

    def passA2(self):
        k = self.k
        hT = self.hT
        k.mark()
        PW = NTOK + 48
        wfm = [k.sb(f"wfm{i}", [128, 8, 128], BF16) for i in range(4)]
        xb = [k.sb(f"xb{i}", [128, PW], F32) for i in range(2)]
        pl = [k.sb(f"pl{i}", [128, NTOK], BF16) for i in range(2)]
        szb = k.sb("szb", [128, NTOK], F32)
        tA = [k.sb(f"tA{i}", [128, 528], F32) for i in range(2)]
        tB = [k.sb(f"tB{i}", [128, 528], F32) for i in range(2)]
        te = [k.sb(f"te{i}", [128, 8], F32) for i in range(2)]
        pw = k.sb("pw", [128, 4, 2, 256], BF16)
        psc = k.sb("psc", [128, 8], F32)
        pedge = k.sb("pedge", [128, 4, 16], F32)
        stg = [k.sb(f"stgp{i}", [128, 512], BF16) for i in range(3)]
        k.dma(pw.k(), self.pool_w.k().m(lambda a: a.rearrange("g p c d -> p g c d")), eng="pool")
        k.dma(psc.k(), self.pool_scale.k())
        k.dma(pedge.k(), self.c_pedge.k())
        for t in xb:
            k.I("pool", "memset", wr=("ap",), ap=t.k(), constant=0.0)
        nw = 0
        nev = 0
        nst = 0
        seq_starts = {t0 for (t0, T, c) in SEQS}
        seq_ends = {t0 + T for (t0, T, c) in SEQS}
        for g in range(4):
            w = (2, 4, 8, 16)[g]
            levels = g + 1
            for cc in range(2):
                W = wfm[nw % 4]
                nw += 1
                f0 = 4128 + g * 256 + cc * 128
                k.dma(W.k(), self.ev_w_in.k()[:, :, f0:f0 + 128], eng="pool")
                for si, (t0, n) in enumerate(SEGS):
                    ps = self.next_ps()
                    for kk in range(8):
                        k.mm(ps.k()[:, 0:n], W.k()[:, kk, :], hT.k()[:, kk, t0:t0 + n], start=(kk == 0), stop=(kk == 7))
                    c0 = pcol(t0)
                    if nev % 2 == 0:
                        k.I("act", "activation", out=xb[cc].k(("s", si))[:, c0:c0 + n], in_=ps.k()[:, 0:n], func=AF.Copy)
                    else:
                        k.I("dve", "tensor_copy", out=xb[cc].k(("s", si))[:, c0:c0 + n], in_=ps.k()[:, 0:n])
                    nev += 1
            def zb_proj(ft):
                nonlocal nw
                W = wfm[nw % 4]
                nw += 1
                f0 = 3104 + ft * 128
                k.dma(W.k(), self.ev_w_in.k()[:, :, f0:f0 + 128], eng="pool")
                for si, (t0, n) in enumerate(SEGS):
                    ps = self.next_ps()
                    for kk in range(8):
                        k.mm(ps.k()[:, 0:n], W.k()[:, kk, :], hT.k()[:, kk, t0:t0 + n], start=(kk == 0), stop=(kk == 7))
                    k.I("act", "activation", out=szb.k(("s", si))[:, t0:t0 + n], in_=ps.k()[:, 0:n], func=AF.Silu)
            zb_proj(g * 2)
            for cc in range(2):
                X = xb[cc]
                for si, (t0, n) in enumerate(SEGS):
                    par = si % 2
                    eng = "dve" if par == 0 else "pool"
                    c0 = pcol(t0)
                    base = c0 - 8
                    nbr = [X.k(("s", s2)) for s2 in (si - 1, si + 1) if 0 <= s2 < len(SEGS)]
                    Xv = X.k(("s", si))
                    cur, oth = tA[par], tB[par]
                    lo, hi = c0 - 7, c0 + n + 7
                    k.I(eng, "tensor_tensor", out=cur.k()[:, lo - base:hi - base], in0=Xv[:, lo - 1:hi - 1],
                        in1=Xv[:, lo:hi], op=ALU.add, extra_r=nbr)
                    for lv, (m, sh) in enumerate(((6, 1), (4, 2), (0, 4))):
                        if levels < lv + 2:
                            break
                        lo, hi = c0 - m, c0 + n + m
                        k.I(eng, "tensor_tensor", out=oth.k()[:, lo - base:hi - base],
                            in0=cur.k()[:, lo - sh - base:hi - sh - base], in1=cur.k()[:, lo + sh - base:hi + sh - base],
                            op=ALU.add)
                        cur, oth = oth, cur
                    k.I("dve", "scalar_tensor_tensor", out=pl[cc].k(("s", si))[:, t0:t0 + n], in0=cur.k()[:, 8:8 + n],
                        scalar=1.0 / w, in1=Xv[:, c0:c0 + n], op0=ALU.mult, op1=ALU.subtract)
                    if t0 in seq_starts:
                        k.I("dve", "tensor_tensor", out=te[par].k(), in0=cur.k()[:, 8:16], in1=pedge.k()[:, g, 0:8], op=ALU.mult)
                        k.I("dve", "tensor_tensor", out=pl[cc].k(("s", si))[:, t0:t0 + 8], in0=te[par].k(),
                            in1=Xv[:, c0:c0 + 8], op=ALU.subtract)
                    if t0 + n in seq_ends:
                        k.I("dve", "tensor_tensor", out=te[par].k(), in0=cur.k()[:, n:n + 8], in1=pedge.k()[:, g, 8:16], op=ALU.mult)
                        k.I("dve", "tensor_tensor", out=pl[cc].k(("s", si))[:, t0 + n - 8:t0 + n], in0=te[par].k(),
                            in1=Xv[:, c0 + n - 8:c0 + n], op=ALU.subtract)
            for dd in range(2):
                ft = g * 2 + dd
                if dd == 1:
                    zb_proj(ft)
                for si, (t0, n) in enumerate(SEGS):
                    ps = self.next_ps()
                    for cc in range(2):
                        k.mm(ps.k()[:, 0:n], pw.k()[:, g, cc, dd * 128:(dd + 1) * 128], pl[cc].k(("s", si))[:, t0:t0 + n],
                             start=(cc == 0), stop=(cc == 1))
                    st = stg[nst % 3]
                    nst += 1
                    k.I("dve", "scalar_tensor_tensor", out=st.k()[:, 0:n], in0=ps.k()[:, 0:n], scalar=psc.k()[:, ft:ft + 1],
                        in1=szb.k(("s", si))[:, t0:t0 + n], op0=ALU.mult, op1=ALU.mult)
                    k.dma(self.ypT.k(("f", ft, si))[ft * 128:(ft + 1) * 128, t0:t0 + n], st.k()[:, 0:n])
        k.release()

    def psbf(self, ps):
        return V(ps.res, None, ps.h.bitcast(BF16).ap())

    def passB(self):
        k = self.k
        k.mark()
        PS = self.PS
        triu = k.sb("triu", [128, 128], BF16)
        tril = k.sb("tril", [128, 128], BF16)
        onesb = k.sb("onesb", [128, 128], BF16)
        nmf = k.sb("nmf", [128, 128], BF16)
        nmb = k.sb("nmb", [128, 128], BF16)
        A_bc = k.sb("A_bc", [128, 32], F32)
        D_bc = k.sb("D_bc", [128, 16], F32)
        k.dma(triu.k(), self.c_triu.k(), eng="pool")
        k.dma(tril.k(), self.c_tril.k(), eng="pool")
        k.dma(nmf.k(), self.c_nmf.k(), eng="pool")
        k.dma(nmb.k(), self.c_nmb.k(), eng="pool")
        k.I("dve", "memset", wr=("ap",), ap=onesb.k(), constant=1.0)
        k.dma(A_bc.k(), self.a_log.k().m(lambda a: a.partition_broadcast(128)))
        k.dma(D_bc.k(), self.d_skip.k().m(lambda a: a.partition_broadcast(128)))
        k.I("act", "activation", out=A_bc.k(), in_=A_bc.k(), func=AF.Exp)
        k.I("dve", "tensor_scalar", out=A_bc.k(), in0=A_bc.k(), scalar1=-1.0, scalar2=None, op0=ALU.mult)
        tri_d = (triu, tril)
        nm_d = (nmf, nmb)

        def mk(name, shape, dt_):
            return [k.sb(f"{name}{i}", shape, dt_) for i in range(2)]
        xc = mk("xc", [128, 16, 128], BF16)
        a32 = mk("a32", [128, 32], F32)
        ahl = mk("ahl", [128, 64], BF16)
        cst = mk("cst", [128, 64], F32)
        ncs = mk("ncs", [128, 32], F32)
        dte = mk("dte", [128, 32], F32)
        wgt = mk("wgt", [128, 32], F32)
        cbT = mk("cbT", [128, 512], F32)
        E = [k.sb(f"E{i}", [128, 512], F32) for i in range(4)]
        M = [[k.sb(f"M{d}{i}", [128, 512], BF16) for i in range(2)] for d in range(2)]
        xs_sb = mk("xs_sb", [128, 1024], BF16)
        xdt = [mk(f"xdt{d}", [128, 1024], BF16) for d in range(2)]
        xw = [mk(f"xwl{d}", [128, 1024], BF16) for d in range(2)]
        xsD = mk("xsD", [128, 1024], F32)
        btk = mk("btk", [128, 512], BF16)
        yst = mk("yst", [128, 1024], F32)
        xcv = self.xcT.k().m(lambda a: a.rearrange("(j p) t -> p j t", p=128))
        nE = 0
        k.dma(xc[0].k(), V(self.xcT.res, None, xcv.ap[:, :, slice(0, 128)]))
        for i in range(NT):
            b = i % 2
            sl = slice(i * 128, (i + 1) * 128)
            if i + 1 < NT:
                k.dma(xc[(i + 1) % 2].k(), V(self.xcT.res, None, xcv.ap[:, :, slice((i + 1) * 128, (i + 2) * 128)]))
            dt = self.dt_all.k(("t", i))[:, i, :]
            k.I("dve", "tensor_tensor", out=a32[b].k(), in0=dt, in1=A_bc.k(), op=ALU.mult)
            k.I("dve", "tensor_copy", out=ahl[b].k()[:, 0:32], in_=a32[b].k())
            k.I("dve", "tensor_tensor", out=ahl[b].k()[:, 32:64], in0=a32[b].k(), in1=ahl[b].k()[:, 0:32], op=ALU.subtract)
            pc = PS[6]
            k.mm(pc.k()[:, 0:16], triu.k(), ahl[b].k()[:, 0:16], start=True, stop=False)
            k.mm(pc.k()[:, 0:16], triu.k(), ahl[b].k()[:, 32:48], start=False, stop=True)
            k.mm(pc.k()[:, 16:32], tril.k(), ahl[b].k()[:, 16:32], start=True, stop=False)
            k.mm(pc.k()[:, 16:32], tril.k(), ahl[b].k()[:, 48:64], start=False, stop=True)
            k.mm(pc.k()[:, 32:64], onesb.k(), ahl[b].k()[:, 0:32], start=True, stop=False)
            k.mm(pc.k()[:, 32:64], onesb.k(), ahl[b].k()[:, 32:64], start=False, stop=True)
            k.I("dve", "tensor_copy", out=cst[b].k(), in_=pc.k()[:, 0:64])
            k.I("dve", "tensor_scalar", out=ncs[b].k(), in0=cst[b].k()[:, 0:32], scalar1=-1.0, scalar2=None, op0=ALU.mult)
            k.I("act", "activation", out=self.e_all.k(("t", i))[:, i, :], in_=cst[b].k()[:, 0:32], func=AF.Exp)
            k.I("act", "activation", out=self.dec_all.k(("t", i))[:, i, :], in_=cst[b].k()[:, 32:64], func=AF.Exp)
            k.I("dve", "tensor_tensor", out=dte[b].k(), in0=cst[b].k()[:, 32:64], in1=cst[b].k()[:, 0:32], op=ALU.subtract)
            k.I("act", "activation", out=dte[b].k(), in_=dte[b].k(), func=AF.Exp)
            k.I("dve", "tensor_tensor", out=wgt[b].k(), in0=dt, in1=dte[b].k(), op=ALU.mult)
            pcb = PS[6]
            for g in range(4):
                k.mm(pcb.k()[:, g * 128:(g + 1) * 128], xc[b].k()[:, 8 + g, :], xc[b].k()[:, 12 + g, :])
            k.I("act", "activation", out=cbT[b].k(), in_=pcb.k(), func=AF.Copy)
            px = self.psbf(PS[7])
            for j in range(8):
                k.tr(px[:, j * 128:(j + 1) * 128], xc[b].k()[:, j, :], self.identb.k())
            k.I("act", "activation", out=xs_sb[b].k(), in_=px, func=AF.Copy)
            pb = self.psbf(PS[7])
            for g in range(4):
                k.tr(pb[:, g * 128:(g + 1) * 128], xc[b].k()[:, 8 + g, :], self.identb.k())
            k.I("dve", "tensor_copy", out=btk[b].k(), in_=pb[:, 0:512])
            x3 = xs_sb[b].k().m(lambda a: a.rearrange("p (r q) -> p r q", q=64))

            def bc16(v):
                return v.m(lambda a: a.unsqueeze(2).to_broadcast([128, 16, 64]))
            for d in range(2):
                k.I("dve", "tensor_tensor", out=xdt[d][b].k().m(lambda a: a.rearrange("p (r q) -> p r q", q=64)),
                    in0=x3, in1=bc16(self.dt_all.k(("t", i))[:, i, d * 16:(d + 1) * 16]), op=ALU.mult)
                k.I("pool", "tensor_tensor", out=xw[d][b].k().m(lambda a: a.rearrange("p (r q) -> p r q", q=64)),
                    in0=x3, in1=bc16(wgt[b].k()[:, d * 16:(d + 1) * 16]), op=ALU.mult)
            k.I("pool", "tensor_tensor", out=xsD[b].k().m(lambda a: a.rearrange("p (r q) -> p r q", q=64)),
                in0=x3, in1=bc16(D_bc.k()), op=ALU.mult)
            py = (PS[0], PS[1])

            def emitE(g):
                nonlocal nE
                for d in range(2):
                    pE = PS[2 + (nE % 4)]
                    Et = E[nE % 4]
                    nE += 1
                    for rr in range(4):
                        col = d * 16 + g * 4 + rr
                        o = pE.k()[:, rr * 128:(rr + 1) * 128]
                        k.mm(o, ahl[b].k()[:, col:col + 1].m(lambda a: a.to_broadcast([128, 128])), tri_d[d].k(),
                             start=True, stop=False)
                        k.mm(o, ahl[b].k()[:, 32 + col:33 + col].m(lambda a: a.to_broadcast([128, 128])), tri_d[d].k(),
                             start=False, stop=False)
                        k.mm(o, self.identb.k(), nm_d[d].k(), start=False, stop=True)
                    for rr in range(4):
                        col = d * 16 + g * 4 + rr
                        k.I("act", "activation", out=Et.k()[:, rr * 128:(rr + 1) * 128], in_=pE.k()[:, rr * 128:(rr + 1) * 128],
                            func=AF.Exp, bias=ncs[b].k()[:, col:col + 1])
                    k.I("dve", "tensor_tensor", out=M[d][g % 2].k().m(lambda a: a.rearrange("p (r q) -> p r q", q=128)),
                        in0=Et.k().m(lambda a: a.rearrange("p (r q) -> p r q", q=128)),
                        in1=cbT[b].k()[:, g * 128:(g + 1) * 128].m(lambda a: a.unsqueeze(1).to_broadcast([128, 4, 128])),
                        op=ALU.mult)

            def emitY(g):
                for rr in range(4):
                    r = g * 4 + rr
                    o = py[r // 8].k()[:, (r % 8) * 64:(r % 8 + 1) * 64]
                    k.mm(o, M[0][g % 2].k()[:, rr * 128:(rr + 1) * 128], xdt[0][b].k()[:, r * 64:(r + 1) * 64], start=True, stop=False)
                    k.mm(o, M[1][g % 2].k()[:, rr * 128:(rr + 1) * 128], xdt[1][b].k()[:, r * 64:(r + 1) * 64], start=False, stop=True)

            emitE(0)
            for g in range(4):
                if g + 1 < 4:
                    emitE(g + 1)
                emitY(g)
            for hh in range(2):
                k.I("dve", "tensor_tensor", out=yst[b].k()[:, hh * 512:(hh + 1) * 512], in0=py[hh].k(),
                    in1=xsD[b].k()[:, hh * 512:(hh + 1) * 512], op=ALU.add)
            k.dma(self.yloc.k(("t", i))[sl, :], yst[b].k())
            for d in range(2):
                k.dma(self.xw_s[d].k(("t", i))[sl, :], xw[d][b].k())
            k.dma(self.btok_s.k(("t", i))[sl, :], btk[b].k())
        k.release()

    def passC(self):
        k = self.k
        k.mark()
        PS = self.PS
        H = [k.sb(f"H{d}", [128, 1024], F32) for d in range(2)]
        Hb = [k.sb(f"Hb{d}", [128, 1024], BF16) for d in range(2)]
        tmpH = [k.sb(f"tmpH{d}", [128, 1024], F32) for d in range(2)]
        CT = [[k.sb(f"CT{d}{i}", [128, 4, 128], BF16) for i in range(2)] for d in range(2)]
        Bt = [[k.sb(f"Bt{d}{i}", [128, 512], BF16) for i in range(2)] for d in range(2)]
        xwt = [[k.sb(f"xwt{d}{i}", [128, 1024], BF16) for i in range(2)] for d in range(2)]
        yo = [[k.sb(f"yo{d}{i}", [128, 1024], F32) for i in range(2)] for d in range(2)]
        hout = k.sb("hout", [128, 8, 128], F32)
        h0src = (self.hf0, self.hb0)
        oss = (self.o_ssdf, self.o_ssdb)
        xcv = self.xcT.k().m(lambda a: a.rearrange("(j p) t -> p j t", p=128))

        def bc16(v):
            return v.m(lambda a: a.unsqueeze(2).to_broadcast([128, 16, 64]))

        def v3(v):
            return v.m(lambda a: a.rearrange("p (r q) -> p r q", q=64))
        for sidx, (t0, T, cond) in enumerate(SEQS):
            nch = T // 128
            cb_ = t0 // 128
            for d in range(2):
                if sidx == 0:
                    k.dma(H[d].k(), h0src[d].k())
                else:
                    k.I("pool", "memset", wr=("ap",), ap=H[d].k(), constant=0.0)
                k.I("act", "activation", out=Hb[d].k(), in_=H[d].k(), func=AF.Copy)
            def c_loads(st):
                b = st % 2
                for d in range(2):
                    c = cb_ + (st if d == 0 else nch - 1 - st)
                    sl = slice(c * 128, (c + 1) * 128)
                    k.dma(CT[d][b].k(), V(self.xcT.res, None, xcv.ap[:, 12:16, sl]))
                    k.dma(Bt[d][b].k(), self.btok_s.k(("t", c))[sl, :])
                    k.dma(xwt[d][b].k(), self.xw_s[d].k(("t", c))[sl, :])
            c_loads(0)
            for st in range(nch):
                b = st % 2
                if st + 1 < nch:
                    c_loads(st + 1)
                for d in range(2):
                    c = cb_ + (st if d == 0 else nch - 1 - st)
                    sl = slice(c * 128, (c + 1) * 128)
                    pY = (PS[4 * d], PS[4 * d + 1])
                    pS = (PS[4 * d + 2], PS[4 * d + 3])
                    for g in range(4):
                        k.mm(pY[g // 2].k()[:, (g % 2) * 256:(g % 2 + 1) * 256], CT[d][b].k()[:, g, :],
                             Hb[d].k()[:, g * 256:(g + 1) * 256])
                    for hh in range(2):
                        k.I("dve", "tensor_tensor", out=v3(yo[d][b].k()[:, hh * 512:(hh + 1) * 512]), in0=v3(pY[hh].k()),
                            in1=self.e_all.k(("t", c))[:, c, d * 16 + hh * 8:d * 16 + hh * 8 + 8].m(
                                lambda a: a.unsqueeze(2).to_broadcast([128, 8, 64])), op=ALU.mult)
                    k.dma(self.yoff[d].k(("t", c))[sl, :], yo[d][b].k())
                    for g in range(4):
                        k.mm(pS[g // 2].k()[:, (g % 2) * 256:(g % 2 + 1) * 256], Bt[d][b].k()[:, g * 128:(g + 1) * 128],
                             xwt[d][b].k()[:, g * 256:(g + 1) * 256])
                    k.I("pool", "tensor_tensor", out=v3(tmpH[d].k()), in0=v3(H[d].k()),
                        in1=bc16(self.dec_all.k(("t", c))[:, c, d * 16:(d + 1) * 16]), op=ALU.mult)
                    for hh in range(2):
                        k.I("dve", "tensor_tensor", out=H[d].k()[:, hh * 512:(hh + 1) * 512], in0=pS[hh].k(),
                            in1=tmpH[d].k()[:, hh * 512:(hh + 1) * 512], op=ALU.add)
                    k.I("act", "activation", out=Hb[d].k(), in_=H[d].k(), func=AF.Copy)
            if sidx > 0:
                pi = sidx - 1
                for d in range(2):
                    for half in range(2):
                        ps = PS[4 * d + half]
                        for q in range(4):
                            j = half * 4 + q
                            k.tr(ps.k()[:, q * 128:(q + 1) * 128], H[d].k()[:, j * 128:(j + 1) * 128], self.ident.k())
                        k.I("act", "activation", out=hout.k()[:, half * 4:half * 4 + 4, :].m(
                            lambda a: a.rearrange("p j n -> p (j n)")), in_=ps.k(), func=AF.Copy)
                    k.dma(oss[d].k(("p", pi))[pi].m(lambda a: a.rearrange("(j p) n -> p j n", p=128)), hout.k(), final=True)
        k.release()

    def passD(self):
        k = self.k
        k.mark()
        PS = self.PS
        wout = k.sb("wout", [128, 16, 1024], BF16)
        ng = k.sb("ng", [128, 8], F32)
        for q in range(4):
            k.dma(wout.k(("q", q))[:, q * 4:(q + 1) * 4, :], self.ev_w_out.k()[:, q * 4:(q + 1) * 4, :], eng="pool")
        k.dma(ng.k(), self.norm_g.k())

        def mk(name, shape, dt_):
            return [k.sb(f"{name}{i}", shape, dt_) for i in range(2)]
        def mk3(name, shape, dt_):
            return [k.sb(f"{name}{i}", shape, dt_) for i in range(3)]
        y0 = mk3("dy0", [128, 1024], F32)
        y1 = mk3("dy1", [128, 1024], F32)
        y2 = mk3("dy2", [128, 1024], F32)
        za = mk3("dza", [128, 1024], F32)
        xr = mk3("dxr", [128, 1024], F32)
        sq = k.sb("dsq", [128, 1024], F32)
        gnb = mk("dgn", [128, 1024], BF16)
        ysT = mk("dysT", [128, 8, 128], BF16)
        ypt = mk3("dyp", [128, 8, 128], BF16)
        ss = mk("dss", [128, 1], F32)
        rs = mk("drs", [128, 1], F32)
        rstd = mk("drstd", [128, 1], F32)
        xo = mk("dxo", [128, 1024], F32)
        ypv = self.ypT.k().m(lambda a: a.rearrange("(j p) t -> p j t", p=128))

        def loads(i):
            b = i % 3
            sl = slice(i * 128, (i + 1) * 128)
            k.dma(y0[b].k(), self.yloc.k(("t", i))[sl, :])
            k.dma(y1[b].k(), self.yoff[0].k(("t", i))[sl, :])
            k.dma(y2[b].k(), self.yoff[1].k(("t", i))[sl, :])
            k.dma(za[b].k(), self.sza.k()[sl, :])
            k.dma(xr[b].k(), self.x_all.k()[sl, :])
            k.dma(ypt[b].k(), V(self.ypT.res, None, ypv.ap[:, :, sl]))

        def stage1(i):
            b = i % 2
            b3 = i % 3
            k.I("pool", "tensor_tensor", out=y1[b3].k(), in0=y1[b3].k(), in1=y2[b3].k(), op=ALU.add)
            k.I("dve", "tensor_tensor", out=y0[b3].k(), in0=y0[b3].k(), in1=y1[b3].k(), op=ALU.add)
            k.I("dve", "tensor_tensor", out=y0[b3].k(), in0=y0[b3].k(), in1=za[b3].k(), op=ALU.mult)
            k.I("act", "activation", out=sq.k(), in_=y0[b3].k(), func=AF.Square, accum_out=ss[b].k())
            k.I("act", "activation", out=rs[b].k(), in_=ss[b].k(), func=AF.Sqrt, scale=1.0 / 1024, bias=self.epsc.k())
            k.I("dve", "reciprocal", out=rstd[b].k(), in_=rs[b].k())
            k.I("dve", "tensor_scalar", out=gnb[b].k(), in0=y0[b3].k(), scalar1=rstd[b].k(), scalar2=None, op0=ALU.mult)
            pt = self.psbf(PS[4 + b])
            for j in range(8):
                k.tr(pt[:, j * 128:(j + 1) * 128], gnb[b].k()[:, j * 128:(j + 1) * 128], self.identb.k())
            k.I("dve", "tensor_tensor", out=ysT[b].k(), in0=pt.m(lambda a: a.rearrange("p (j t) -> p j t", t=128)),
                in1=ng.k().m(lambda a: a.unsqueeze(2).to_broadcast([128, 8, 128])), op=ALU.mult)

        def stage2(i):
            b = i % 2
            b3 = i % 3
            cond = 0 if i < 32 else 1
            sl = slice(i * 128, (i + 1) * 128)
            for nb in range(2):
                po = PS[2 * b + nb]
                for kk in range(16):
                    lhs = ysT[b].k()[:, kk, :] if kk < 8 else ypt[b3].k()[:, kk - 8, :]
                    k.mm(po.k(), lhs, wout.k()[:, kk, nb * 512:(nb + 1) * 512], start=(kk == 0), stop=(kk == 15))
                k.I("dve", "tensor_tensor", out=xo[b].k()[:, nb * 512:(nb + 1) * 512], in0=po.k(),
                    in1=self.gate_bc[0][cond].k()[:, nb * 512:(nb + 1) * 512], op=ALU.mult)
            k.I("pool", "tensor_tensor", out=xo[b].k(), in0=xo[b].k(), in1=xr[b3].k(), op=ALU.add)
            k.dma(self.x1.k(("t", i))[sl, :], xo[b].k())

        loads(0)
        loads(1)
        stage1(0)
        for i in range(NT):
            if i + 2 < NT:
                loads(i + 2)
            if i + 1 < NT:
                stage1(i + 1)
            stage2(i)
        k.release()

    def passE(self):
        k = self.k
        hT = self.hT
        k.mark()
        cosT = k.sb("cosT", [128, 4096], F32)
        sinT = k.sb("sinT", [128, 4096], F32)
        for q in range(4):
            k.dma(cosT.k(("q", q))[:, q * 1024:(q + 1) * 1024], self.c_cos.k()[:, q * 1024:(q + 1) * 1024])
            k.dma(sinT.k(("q", q))[:, q * 1024:(q + 1) * 1024], self.c_sin.k()[:, q * 1024:(q + 1) * 1024])
        wfm = [k.sb(f"wfm{i}", [128, 8, 128], BF16) for i in range(3)]
        qbf = [k.sb(f"eqb{i}", [128, 512], BF16) for i in range(2)]
        pmb = k.sb("pmb", [128, 128], BF16)
        k.dma(pmb.k(), self.c_pm.k(), eng="pool")
        wtm = [k.sb(f"wtm{i}", [128, 8, 512], BF16) for i in range(2)]
        t1 = [k.sb(f"et1{i}", [128, 512], F32) for i in range(2)]
        t2 = [k.sb(f"et2{i}", [128, 512], F32) for i in range(2)]
        sb16 = [k.sb(f"esb{i}", [128, 512], BF16) for i in range(3)]
        sf32 = [k.sb(f"esf{i}", [128, 512], F32) for i in range(3)]
        cnt = {"w": 0, "b": 0, "f": 0, "t": 0, "e": 0}

        pend = []

        def fm_rope(col0, rcol0, ntiles, dst):
            for j in range(ntiles):
                W = wfm[cnt["w"] % 3]
                cnt["w"] += 1
                k.dma(W.k(), self.od_w_in.k()[:, :, col0 + j * 128:col0 + (j + 1) * 128], eng="pool")
                for si, (t0, n) in enumerate(SEGS):
                    pa = self.next_ps()
                    for kk in range(8):
                        k.mm(pa.k()[:, 0:n], W.k()[:, kk, :], hT.k()[:, kk, t0:t0 + n], start=(kk == 0), stop=(kk == 7))
                    while pend:
                        pend.pop(0)()
                    st = sb16[cnt["b"] % 3]
                    cnt["b"] += 1
                    if si < 8:
                        qb16 = qbf[cnt["t"] % 2]
                        a = t1[cnt["t"] % 2]
                        b_ = t2[cnt["t"] % 2]
                        cnt["t"] += 1
                        k.I("act", "activation", out=qb16.k()[:, 0:n], in_=pa.k()[:, 0:n], func=AF.Copy)
                        k.I("dve", "tensor_tensor", out=a.k()[:, 0:n], in0=pa.k()[:, 0:n], in1=cosT.k()[:, t0:t0 + n], op=ALU.mult,
                            extra_r=[qb16.k()])

                        def tail(qb16=qb16, a=a, b_=b_, st=st, t0=t0, n=n, j=j, si=si):
                            pb = self.next_ps()
                            k.mm(pb.k()[:, 0:n], pmb.k(), qb16.k()[:, 0:n])
                            k.I("dve", "tensor_tensor", out=b_.k()[:, 0:n], in0=pb.k()[:, 0:n], in1=sinT.k()[:, t0:t0 + n], op=ALU.mult)
                            k.I("pool", "tensor_tensor", out=st.k()[:, 0:n], in0=a.k()[:, 0:n], in1=b_.k()[:, 0:n], op=ALU.add)
                            k.dma(dst.k(("f", j, si))[j * 128:(j + 1) * 128, t0:t0 + n], st.k()[:, 0:n])
                        pend.append(tail)
                    else:
                        k.I("act", "activation", out=st.k()[:, 0:n], in_=pa.k()[:, 0:n], func=AF.Copy)
                        k.dma(dst.k(("f", j, si))[j * 128:(j + 1) * 128, t0:t0 + n], st.k()[:, 0:n])
            while pend:
                pend.pop(0)()

        def fm_silu(col0, ntiles, dst):
            for j in range(ntiles):
                W = wfm[cnt["w"] % 3]
                cnt["w"] += 1
                k.dma(W.k(), self.od_w_in.k()[:, :, col0 + j * 128:col0 + (j + 1) * 128], eng="pool")
                for si, (t0, n) in enumerate(SEGS):
                    pa = self.next_ps()
                    for kk in range(8):
                        k.mm(pa.k()[:, 0:n], W.k()[:, kk, :], hT.k()[:, kk, t0:t0 + n], start=(kk == 0), stop=(kk == 7))
                    st = sf32[cnt["f"] % 3]
                    cnt["f"] += 1
                    k.I("act", "activation", out=st.k()[:, 0:n], in_=pa.k()[:, 0:n], func=AF.Silu)
                    k.dma(dst.k(("f", j, si))[j * 128:(j + 1) * 128, t0:t0 + n], st.k()[:, 0:n])

        def tm(col0, width, tiles, sinks):
            W = wtm[cnt["e"] % 2]
            cnt["e"] += 1
            k.dma(W.k()[:, :, 0:width], self.od_w_in.k()[:, :, col0:col0 + width], eng="pool")
            for i in tiles:
                pa = self.next_ps()
                for kk in range(8):
                    k.mm(pa.k()[:, 0:width], hT.k(("t", i))[:, kk, i * 128:(i + 1) * 128], W.k()[:, kk, 0:width],
                         start=(kk == 0), stop=(kk == 7))
                for (c0, c1, d16, dcol, d32, fcol, frow) in sinks:
                    if d16 is not None and self.stop_after != "E0b":
                        st = sb16[cnt["b"] % 3]
                        cnt["b"] += 1
                        k.I("act", "activation", out=st.k()[:, 0:c1 - c0], in_=pa.k()[:, c0:c1], func=AF.Copy)
                        k.dma(d16.k(("t", i, dcol))[i * 128:(i + 1) * 128, dcol:dcol + c1 - c0], st.k()[:, 0:c1 - c0])
                    if d32 is not None and i >= 32 and self.stop_after != "E0a":
                        st = sf32[cnt["f"] % 3]
                        cnt["f"] += 1
                        k.I("act", "activation", out=st.k()[:, 0:c1 - c0], in_=pa.k()[:, c0:c1], func=AF.Copy)
                        r0 = (i - 32) * 128
                        k.dma(d32.k(("t", i, fcol))[r0:r0 + 128, fcol:fcol + c1 - c0], st.k()[:, 0:c1 - c0], final=True)

        allt = list(range(NT))
        pt = list(range(32, NT))
        if self.stop_after == "E00":
            k.release()
            return
        for fb in range(2):
            tm(2048 + fb * 512, 512, allt, [(0, 512, self.Vc, fb * 512, self.o_dv, fb * 512, 0)])
        if self.stop_after in ("E0", "E0a", "E0b"):
            k.release()
            return
        tm(5376, 256, allt, [(0, 256, self.Vd, 0, self.o_wv, 0, 0)])
        for fb in range(2):
            tm(1024 + fb * 512, 512, pt, [(0, 512, None, 0, self.o_dk, fb * 512, 0)])
        tm(5120, 256, pt, [(0, 256, None, 0, self.o_wk, 0, 0)])
        if self.stop_after == "E1":
            k.release()
            return
        fm_rope(0, 0, 8, self.QcT)
        if self.stop_after == "E2":
            k.release()
            return
        fm_rope(1024, 1024, 8, self.KcT)
        fm_rope(4096, 2048, 8, self.QdT)
        fm_rope(5120, 3072, 2, self.KdT)
        fm_silu(3072, 8, self.szcT)
        fm_silu(5632, 8, self.szdT)
        k.release()

    def passF(self):
        k = self.k
        k.mark()
        PS = self.PS
        SC = 64 ** -0.5
        onesb = k.sb("f_ones", [128, 128], BF16)
        k.I("dve", "memset", wr=("ap",), ap=onesb.k(), constant=1.0)
        lam4 = k.sb("lam4", [128, 4, 64], F32)
        lp = k.sb("lamp", [128, 2, 64], F32)
        ls = k.sb("lams", [128, 2], F32)
        nlam = k.sb("nlam", [128, 1], F32)
        gsub = k.sb("gsub", [128, 1], F32)
        k.dma(lam4.k(), self.od_lambda.k().m(lambda a: a.partition_broadcast(128)).m(
            lambda a: a.rearrange("p o (a b) -> p (o a) b", b=64)))
        k.I("dve", "tensor_tensor", out=lp.k()[:, 0, :], in0=lam4.k()[:, 0, :], in1=lam4.k()[:, 1, :], op=ALU.mult)
        k.I("dve", "tensor_tensor", out=lp.k()[:, 1, :], in0=lam4.k()[:, 2, :], in1=lam4.k()[:, 3, :], op=ALU.mult)
        k.I("dve", "tensor_reduce", out=ls.k(), in_=lp.k(), axis=mybir.AxisListType.X, op=ALU.add)
        k.I("act", "activation", out=ls.k(), in_=ls.k(), func=AF.Exp)
        k.I("dve", "tensor_tensor", out=nlam.k(), in0=ls.k()[:, 1:2], in1=ls.k()[:, 0:1], op=ALU.subtract)
        k.I("dve", "tensor_scalar", out=nlam.k(), in0=nlam.k(), scalar1=-LAM_INIT, scalar2=None, op0=ALU.add)
        k.dma(gsub.k(), self.subln_g.k())
        k.I("dve", "tensor_scalar", out=gsub.k(), in0=gsub.k(), scalar1=1.0 - LAM_INIT, scalar2=None, op0=ALU.mult)

        KT = [k.sb(f"fKT{i}", [128, 4608], BF16) for i in range(2)]
        Vh = [k.sb(f"fVh{i}", [128, 36, 128], BF16) for i in range(2)]
        QT = [k.sb(f"fQT{i}", [128, 512], BF16) for i in range(2)]
        szc = [k.sb(f"fszc{i}", [128, 512], F32) for i in range(3)]
        P = [k.sb(f"fP{i}", [128, 512], BF16) for i in range(8)]
        ones32 = k.sb("f_ones32", [128, 32], BF16)
        k.I("dve", "memset", wr=("ap",), ap=ones32.k(), constant=1.0)
        c32 = k.sb("f_c32", [64, 128], F32)
        k.I("pool", "memset", wr=("ap",), ap=c32.k(), constant=1.0 / 32)
        zsb = k.sb("fzsb", [64, 512], F32)
        osb = [k.sb(f"fosb{i}", [128, 512], F32) for i in range(2)]
        rz = [k.sb(f"frz{i}", [128, 512], F32) for i in range(2)]
        ta = k.sb("fta", [128, 512], F32)
        tb = k.sb("ftb", [128, 512], F32)
        oc = k.sb("foc", [128, 512], F32)
        sq = k.sb("fsq", [128, 512], BF16)
        rs = k.sb("frs", [128, 512], F32)
        ost = [k.sb(f"fost{i}", [128, 512], BF16) for i in range(2)]
        cnt = {"P": 0, "S": 0}
        nq_ = 0
        nh = 0
        pO = (PS[0], PS[1])
        pZ = PS[2]
        pX = PS[3]
        heads = []
        qitems = []
        for sidx, (t0, T, cond) in enumerate(SEQS):
            nctx = 4 if sidx == 0 else 0
            qblocks = [(t0 + q * 512, 512) for q in range(T // 512)] if T >= 512 else [(t0, T)]
            for h in range(8):
                heads.append((sidx, t0, T, nctx, h))
                for qi, (q0, nq) in enumerate(qblocks):
                    qitems.append((len(heads) - 1, q0, nq, qi == 0))

        def head_loads(hidx):
            sidx, t0, T, nctx, h = heads[hidx]
            kt_, vh_ = KT[hidx % 2], Vh[hidx % 2]
            if nctx:
                k.dma(kt_.k(("c",))[:, 0:512], self.dk_T.k()[h], eng="pool")
                k.dma(vh_.k(("c",))[:, 0:4, :], self.dv.k()[h].m(lambda a: a.rearrange("(j p) v -> p j v", p=128)), eng="pool")
            for c0 in range(0, T, 2048):
                cw_ = min(2048, T - c0)
                k.dma(kt_.k(("o", c0))[:, nctx * 128 + c0:nctx * 128 + c0 + cw_],
                      self.KcT.k()[h * 128:(h + 1) * 128, t0 + c0:t0 + c0 + cw_])
            for j0 in range(0, T // 128, 8):
                jn = min(8, T // 128 - j0)
                k.dma(vh_.k(("o", j0))[:, nctx + j0:nctx + j0 + jn, :],
                      self.Vc.k()[t0 + j0 * 128:t0 + (j0 + jn) * 128, h * 128:(h + 1) * 128].m(
                          lambda a: a.rearrange("(j p) v -> p j v", p=128)))

        def q_loads(qidx):
            hidx, q0, nq, first = qitems[qidx]
            h = heads[hidx][4]
            k.dma(QT[qidx % 2].k()[:, 0:nq], self.QcT.k()[h * 128:(h + 1) * 128, q0:q0 + nq])
            k.dma(szc[qidx % 3].k()[:, 0:nq], self.szcT.k()[h * 128:(h + 1) * 128, q0:q0 + nq])

        head_loads(0)
        q_loads(0)
        deferred = []
        if True:
            if True:
                for qidx, (hidx, q0, nq, first) in enumerate(qitems):
                    sidx, t0, T, nctx, h = heads[hidx]
                    nkt = nctx + T // 128
                    kt_, vh_ = KT[hidx % 2], Vh[hidx % 2]
                    qb = qidx % 2
                    if first and hidx + 1 < len(heads):
                        head_loads(hidx + 1)
                    if qidx + 1 < len(qitems):
                        q_loads(qidx + 1)
                    Ps = {}

                    def emit_qk(kt):
                        kkey = ("c",) if kt < nctx else ("o", ((kt - nctx) * 128) // 2048 * 2048)
                        lst = []
                        for i in range(2):
                            pS = PS[4 + cnt["S"] % 4]
                            cnt["S"] += 1
                            k.mm(pS.k()[:, 0:nq], kt_.k(kkey)[i * 64:(i + 1) * 64, kt * 128:(kt + 1) * 128],
                                 QT[qb].k()[i * 64:(i + 1) * 64, 0:nq])
                            Pt = P[cnt["P"] % 8]
                            cnt["P"] += 1
                            k.I("act", "activation", out=Pt.k()[:, 0:nq], in_=pS.k()[:, 0:nq], func=AF.Exp, scale=SC)
                            lst.append(Pt)
                        Ps[kt] = lst
                    emit_qk(0)
                    for kt in range(nkt):
                        if kt + 1 < nkt:
                            emit_qk(kt + 1)
                        vkey = ("c",) if kt < nctx else ("o", (kt - nctx) // 8 * 8)
                        for i in range(2):
                            k.mm(pO[i].k()[:, 0:nq], vh_.k(vkey)[:, kt, :], Ps[kt][i].k()[:, 0:nq],
                                 start=(kt == 0), stop=(kt == nkt - 1))
                        for i in range(2):
                            k.mm(pZ.k()[32 * i:32 * i + 32, 0:nq], ones32.k(), Ps[kt][i].k()[:, 0:nq],
                                 start=(kt == 0), stop=(kt == nkt - 1), tile_position=(0, 32 * i))
                        del Ps[kt]
                        while deferred and deferred[0][0] <= kt:
                            deferred.pop(0)[1]()
                    while deferred:
                        deferred.pop(0)[1]()
                    k.I("dve", "tensor_copy", out=osb[0].k()[:, 0:nq], in_=pO[0].k()[:, 0:nq])
                    k.I("dve", "tensor_copy", out=osb[1].k()[:, 0:nq], in_=pO[1].k()[:, 0:nq])
                    k.I("dve", "tensor_copy", out=zsb.k()[:, 0:nq], in_=pZ.k()[0:64, 0:nq])

                    def mk_tail(h=h, q0=q0, nq=nq, qidx=qidx):
                        szc_ = szc[qidx % 3]
                        ot = ost[qidx % 2]

                        def d1():
                            k.I("dve", "reciprocal", out=zsb.k()[:, 0:nq], in_=zsb.k()[:, 0:nq])

                        def d2():
                            k.mm(pX.k()[:, 0:nq], c32.k()[0:32, :], zsb.k()[0:32, 0:nq])
                            k.I("dve", "tensor_tensor", out=ta.k()[:, 0:nq], in0=osb[0].k()[:, 0:nq], in1=pX.k()[:, 0:nq], op=ALU.mult)

                        def d3():
                            k.mm(pX.k()[:, 0:nq], c32.k()[32:64, :], zsb.k()[32:64, 0:nq])
                            k.I("dve", "tensor_tensor", out=tb.k()[:, 0:nq], in0=osb[1].k()[:, 0:nq], in1=pX.k()[:, 0:nq], op=ALU.mult)
                            k.I("dve", "scalar_tensor_tensor", out=oc.k()[:, 0:nq], in0=tb.k()[:, 0:nq], scalar=nlam.k(),
                                in1=ta.k()[:, 0:nq], op0=ALU.mult, op1=ALU.add)

                        def d4():
                            k.I("act", "activation", out=sq.k()[:, 0:nq], in_=oc.k()[:, 0:nq], func=AF.Square)

                        def d5():
                            k.mm(pX.k()[:, 0:nq], onesb.k(), sq.k()[:, 0:nq])

                        def d6():
                            k.I("act", "activation", out=rs.k()[:, 0:nq], in_=pX.k()[:, 0:nq], func=AF.Sqrt, scale=1.0 / 128,
                                bias=self.epsc.k())
                            k.I("dve", "reciprocal", out=rs.k()[:, 0:nq], in_=rs.k()[:, 0:nq])
                            k.I("dve", "tensor_tensor", out=oc.k()[:, 0:nq], in0=oc.k()[:, 0:nq], in1=rs.k()[:, 0:nq], op=ALU.mult)
                            k.I("dve", "scalar_tensor_tensor", out=ot.k()[:, 0:nq], in0=oc.k()[:, 0:nq], scalar=gsub.k(),
                                in1=szc_.k()[:, 0:nq], op0=ALU.mult, op1=ALU.mult)
                            k.dma(self.ocT.k(("h", h, q0))[h * 128:(h + 1) * 128, q0:q0 + nq], ot.k()[:, 0:nq])
                        return [(1, d1), (3, d2), (6, d3), (9, d4), (12, d5), (15, d6)]
                    deferred.extend(mk_tail())
        while deferred:
            deferred.pop(0)[1]()
        k.release()

    def passG(self):
        k = self.k
        k.mark()
        PS = self.PS
        SC = 64 ** -0.5
        onesb = k.sb("g_ones", [128, 64], BF16)
        k.I("dve", "memset", wr=("ap",), ap=onesb.k(), constant=1.0)
        nmp = k.sb("g_nmp", [128, 128], BF16)
        nmn = k.sb("g_nmn", [128, 128], BF16)
        k.dma(nmp.k(), self.c_nmb.k(), eng="pool")
        k.dma(nmn.k(), self.c_nmf.k(), eng="pool")
        esink = k.sb("esink", [128, 16], F32)
        k.dma(esink.k(), self.sink.k().m(lambda a: a.partition_broadcast(128)))
        k.I("act", "activation", out=esink.k(), in_=esink.k(), func=AF.Exp)
        KT = [k.sb(f"gKT{i}", [128, 4608], BF16) for i in range(2)]
        Vh = [k.sb(f"gVh{i}", [128, 36, 128], BF16) for i in range(2)]
        for t in Vh:
            k.I("pool", "memset", wr=("ap",), ap=t.k(), constant=1.0)
        c64 = k.sb("g_c64", [128, 64], F32)
        k.I("pool", "memset", wr=("ap",), ap=c64.k(), constant=1.0 / 64)
        QA = [k.sb(f"gQA{i}", [128, 4096], BF16) for i in range(2)]
        QB = [k.sb(f"gQB{i}", [128, 4096], BF16) for i in range(2)]
        P = [k.sb(f"gP{i}", [128, 512], BF16) for i in range(4)]
        szd = [k.sb(f"gszd{i}", [64, 4, 128], F32) for i in range(3)]
        osb = k.sb("gosb", [128, 512], F32)
        zmv = k.sb("gzmv", [64, 512], F32)
        zt = k.sb("gzt", [64, 512], F32)
        od = k.sb("god", [64, 512], F32)
        ost = [k.sb(f"gost{i}", [64, 4, 128], BF16) for i in range(2)]
        gcnt = {"P": 0, "S": 0}

        def pv(v):
            return v.m(lambda a: a.rearrange("p (r h q) -> p r h q", r=2, h=2))

        def gv(v):
            return v.m(lambda a: a.rearrange("p (h r) q -> p r h q", r=2))

        heads = []
        items = []
        for sidx, (t0, T, cond) in enumerate(SEQS):
            nctx = 4 if sidx == 0 else 0
            nblk = T // 128
            for kv in range(4):
                heads.append((sidx, t0, T, nctx, nblk, kv))
                for blk in range(nblk):
                    items.append((len(heads) - 1, blk))

        def head_loads(hidx):
            sidx, t0, T, nctx, nblk, kv = heads[hidx]
            hb = hidx % 2
            kt_, vh_, qa, qb_ = KT[hb], Vh[hb], QA[hb], QB[hb]
            for half in range(2):
                rows = slice(half * 64, (half + 1) * 64)
                if nctx:
                    k.dma(kt_.k(("c", half))[rows, 0:512], self.wk_T.k()[kv], eng="pool")
                for c0 in range(0, T, 2048):
                    cw_ = min(2048, T - c0)
                    k.dma(kt_.k(("o", half, c0))[rows, nctx * 128 + c0:nctx * 128 + c0 + cw_],
                          self.KdT.k()[kv * 64:(kv + 1) * 64, t0 + c0:t0 + c0 + cw_])
            if nctx:
                k.dma(vh_.k(("c",))[:, 0:4, 0:64], self.wv.k()[kv].m(lambda a: a.rearrange("(j p) v -> p j v", p=128)), eng="pool")
            for j0 in range(0, nblk, 8):
                jn = min(8, nblk - j0)
                k.dma(vh_.k(("o", j0))[:, nctx + j0:nctx + j0 + jn, 0:64],
                      self.Vd.k()[t0 + j0 * 128:t0 + (j0 + jn) * 128, kv * 64:(kv + 1) * 64].m(
                          lambda a: a.rearrange("(j p) v -> p j v", p=128)))
            for c0 in range(0, T, 2048):
                cw_ = min(2048, T - c0)
                k.dma(qa.k(("q", c0))[:, c0:c0 + cw_], self.QdT.k()[kv * 256:kv * 256 + 128, t0 + c0:t0 + c0 + cw_])
                k.dma(qb_.k(("q", c0))[:, c0:c0 + cw_], self.QdT.k()[kv * 256 + 128:kv * 256 + 256, t0 + c0:t0 + c0 + cw_])

        def blk_loads(idx):
            hidx, blk = items[idx]
            sidx, t0, T, nctx, nblk, kv = heads[hidx]
            q0 = t0 + blk * 128
            k.dma(szd[idx % 3].k(), self.szdT.k()[kv * 256:(kv + 1) * 256, q0:q0 + 128].m(
                lambda a: a.rearrange("(g d) q -> d g q", d=64)))

        head_loads(0)
        blk_loads(0)
        deferred = []
        ctxs = []
        for idx, (hidx, blk) in enumerate(items):
            sidx, t0, T, nctx, nblk, kv = heads[hidx]
            if sidx == 0:
                kts = [(c, ("c",), None) for c in range(4)]
                for off, msk in ((-1, nmp), (0, None), (1, nmn)):
                    if 0 <= blk + off < nblk:
                        kts.append((nctx + blk + off, ("o",), msk))
            else:
                kts = [(c, ("o",), None) for c in range(nblk)]
            ctxs.append(kts)
        steps = [(idx, n_) for idx in range(len(items)) for n_ in range(len(ctxs[idx]))]
        Pq = {}

        def emit_qk(si):
            idx, n_ = steps[si]
            hidx, blk = items[idx]
            sidx, t0, T, nctx, nblk, kv = heads[hidx]
            hb = hidx % 2
            kt_, qtile = KT[hb], (QA[hb], QB[hb])
            kt, key, msk = ctxs[idx][n_]
            pSA = PS[4 + 2 * (gcnt["S"] % 2)]
            pSB = PS[5 + 2 * (gcnt["S"] % 2)]
            gcnt["S"] += 1
            Pt = P[gcnt["P"] % 4]
            gcnt["P"] += 1
            for g in range(4):
                rows = slice((g % 2) * 64, (g % 2 + 1) * 64)
                pS = pSA if g % 2 == 0 else pSB
                o = pS.k()[:, (g // 2) * 128:(g // 2 + 1) * 128]
                kk_ = ("c", g % 2) if key[0] == "c" else ("o", g % 2, ((kt - nctx) * 128) // 2048 * 2048)
                if msk is not None:
                    k.mm(o, self.identb.k(), msk.k(), start=True, stop=False)
                k.mm(o, kt_.k(kk_)[rows, kt * 128:(kt + 1) * 128],
                     qtile[g // 2].k(("q", (blk * 128) // 2048 * 2048))[rows, blk * 128:(blk + 1) * 128],
                     start=(msk is None), stop=True)
            k.I("act", "activation", out=Pt.k()[:, 0:256], in_=pSA.k()[:, 0:256], func=AF.Exp, scale=SC)
            k.I("act", "activation", out=Pt.k()[:, 256:512], in_=pSB.k()[:, 0:256], func=AF.Exp, scale=SC)
            Pq[si] = Pt

        emit_qk(0)
        for si, (idx, n_) in enumerate(steps):
            hidx, blk = items[idx]
            sidx, t0, T, nctx, nblk, kv = heads[hidx]
            hb = hidx % 2
            vh_ = Vh[hb]
            kts = ctxs[idx]
            if n_ == 0:
                if blk == 0 and hidx + 1 < len(heads):
                    head_loads(hidx + 1)
                if idx + 1 < len(items):
                    blk_loads(idx + 1)
            q0 = t0 + blk * 128
            frow = slice(kv * 256, (kv + 1) * 256)
            pO = PS[idx % 2]
            if si + 1 < len(steps):
                emit_qk(si + 1)
            kt, key, msk = kts[n_]
            Pt = Pq.pop(si)
            vkey = ("c",) if key[0] == "c" else ("o", (kt - nctx) // 8 * 8)
            k.mm(pO.k(), vh_.k(vkey)[:, kt, :], Pt.k(), start=(n_ == 0), stop=(n_ == len(kts) - 1))
            while deferred and deferred[0][0] <= n_:
                deferred.pop(0)[1]()
            if n_ < len(kts) - 1:
                continue
            while deferred:
                deferred.pop(0)[1]()

            def mk_tail(pO=pO, kv=kv, q0=q0, frow=frow, idx=idx):
                szd_ = szd[idx % 3]
                ost_ = ost[idx % 2]

                def d0():
                    k.I("dve", "tensor_copy", out=osb.k(), in_=pO.k())
                    k.dma(zmv.k(), osb.k()[64:128, :], semkey=("zmv",))

                def d1():
                    k.I("dve", "tensor_tensor", out=pv(zt.k()), in0=pv(zmv.k()),
                        in1=esink.k()[0:64, kv * 4:kv * 4 + 4].m(
                            lambda a: a.rearrange("p (h r) -> p r h", r=2).unsqueeze(3).to_broadcast([64, 2, 2, 128])), op=ALU.add)
                    k.I("dve", "reciprocal", out=zt.k(), in_=zt.k())
                    k.I("dve", "tensor_tensor", out=od.k(), in0=osb.k()[0:64, :], in1=zt.k(), op=ALU.mult)

                def d2():
                    k.I("pool", "tensor_tensor", out=gv(ost_.k()), in0=pv(od.k()), in1=gv(szd_.k()), op=ALU.mult)
                    k.dma(self.odT.k(("kv", kv, q0))[frow, q0:q0 + 128].m(lambda a: a.rearrange("(g d) q -> d g q", d=64)), ost_.k())
                return [(-1, d0), (1, d1), (3, d2)]
            tl = mk_tail()
            tl.pop(0)[1]()
            deferred.extend(tl)
        while deferred:
            deferred.pop(0)[1]()
        k.release()

    def passH(self):
        k = self.k
        k.mark()
        PS = self.PS
        wout = k.sb("wout1", [128, 16, 1024], BF16)
        for q in range(4):
            k.dma(wout.k(("q", q))[:, q * 4:(q + 1) * 4, :], self.od_w_out.k()[:, q * 4:(q + 1) * 4, :], eng="pool")
        fg = k.sb("fg", [128, 1024], F32)
        k.dma(fg.k(), self.fnorm_g.k().m(lambda a: a.partition_broadcast(128)))

        def mk(name, shape, dt_):
            return [k.sb(f"{name}{i}", shape, dt_) for i in range(2)]
        oct_ = [k.sb(f"hoc{i}", [128, 8, 128], BF16) for i in range(3)]
        odt_ = [k.sb(f"hod{i}", [128, 8, 128], BF16) for i in range(3)]
        xr = [k.sb(f"hxr{i}", [128, 1024], F32) for i in range(3)]
        xo = mk("hxo", [128, 1024], F32)
        sq = k.sb("hsq", [128, 1024], F32)
        ss = mk("hss", [128, 1], F32)
        rs = mk("hrs", [128, 1], F32)
        rstd = mk("hrstd", [128, 1], F32)
        yo = mk("hyo", [128, 1024], F32)
        ocv = self.ocT.k().m(lambda a: a.rearrange("(j p) t -> p j t", p=128))
        odv = self.odT.k().m(lambda a: a.rearrange("(j p) t -> p j t", p=128))
        def loads(i):
            b = i % 3
            sl = slice(i * 128, (i + 1) * 128)
            k.dma(oct_[b].k(), V(self.ocT.res, None, ocv.ap[:, :, sl]))
            k.dma(odt_[b].k(), V(self.odT.res, None, odv.ap[:, :, sl]))
            k.dma(xr[b].k(), self.x1.k()[sl, :])

        def compute(i):
            b3 = i % 3
            b = i % 2
            cond = 0 if i < 32 else 1
            sl = slice(i * 128, (i + 1) * 128)
            for nb in range(2):
                po = PS[2 * b + nb]
                for kk in range(16):
                    lhs = oct_[b3].k()[:, kk, :] if kk < 8 else odt_[b3].k()[:, kk - 8, :]
                    k.mm(po.k(), lhs, wout.k()[:, kk, nb * 512:(nb + 1) * 512], start=(kk == 0), stop=(kk == 15))
                k.I("dve", "tensor_tensor", out=xo[b].k()[:, nb * 512:(nb + 1) * 512], in0=po.k(),
                    in1=self.gate_bc[1][cond].k()[:, nb * 512:(nb + 1) * 512], op=ALU.mult)
            k.I("pool", "tensor_tensor", out=xo[b].k(), in0=xo[b].k(), in1=xr[b3].k(), op=ALU.add)
            k.I("act", "activation", out=sq.k(), in_=xo[b].k(), func=AF.Square, accum_out=ss[b].k())
            k.I("act", "activation", out=rs[b].k(), in_=ss[b].k(), func=AF.Sqrt, scale=1.0 / D, bias=self.epsc.k())
            k.I("dve", "reciprocal", out=rstd[b].k(), in_=rs[b].k())
            k.I("dve", "scalar_tensor_tensor", out=yo[b].k(), in0=xo[b].k(), scalar=rstd[b].k(), in1=fg.k(), op0=ALU.mult, op1=ALU.mult)
            k.dma(self.y_all.k(("t", i))[sl, :], yo[b].k(), final=True)

        loads(0)
        loads(1)
        for i in range(NT):
            if i + 2 < NT:
                loads(i + 2)
            compute(i)
        k.release()

def make_consts():
    c = {}
    c["c_ident"] = np.eye(128, dtype=np.float32)
    t = np.arange(128)
    c["c_triu"] = (t[:, None] <= t[None, :]).astype(np.float32)
    c["c_tril"] = (t[:, None] >= t[None, :]).astype(np.float32)
    c["c_nmf"] = np.where(t[None, :] < t[:, None], -30000.0, 0.0).astype(np.float32)
    c["c_nmb"] = np.where(t[None, :] > t[:, None], -30000.0, 0.0).astype(np.float32)
    sel = np.zeros((64, 32, 128), np.float32)
    c["c_sel"] = sel
    T = 4096
    row = np.repeat(np.arange(T // 64), 64).astype(np.float32)
    col = np.tile(np.arange(64), T // 64).astype(np.float32)
    nf = 16
    inv = (10000.0 ** (-np.arange(nf, dtype=np.float32) / nf)).astype(np.float32)
    ang = np.stack([row[:, None] * inv, col[:, None] * inv], axis=1)
    cos = np.cos(ang).astype(np.float32)
    sin = np.sin(ang).astype(np.float32)
    ct = np.zeros((64, T), np.float32)
    st = np.zeros((64, T), np.float32)
    for ax in range(2):
        for half in range(2):
            for f in range(nf):
                d = ax * 32 + half * 16 + f
                ct[d] = cos[:, ax, f]
                st[d] = sin[:, ax, f] * (-1.0 if half == 0 else 1.0)
    pm = np.zeros((128, 128), np.float32)
    for fo in range(128):
        d = fo % 32
        fi = fo + 16 if d < 16 else fo - 16
        pm[fi, fo] = 1.0
    c["c_pm"] = pm
    c["c_cos"] = np.concatenate([ct, ct], 0)
    c["c_sin"] = np.concatenate([st, st], 0)
    pe = np.zeros((128, 4, 16), np.float32)
    for g, w in enumerate((2, 4, 8, 16)):
        left = w // 2
        right = w - 1 - left
        for j in range(8):
            pe[:, g, j] = 1.0 / ((j + right + 1) - max(j - left, 0))
            pe[:, g, 8 + j] = 1.0 / (min(right + 1, 8 - j) + left)
    c["c_pedge"] = pe
    return c


def prep_core_inputs(inp, core, consts):
    f = np.float32
    m = {}
    xs = np.asarray(inp["x_sample"][core], f)
    xp = np.asarray(inp["x_prompt"][2 * core:2 * core + 2], f).reshape(512, D)
    m["x_all"] = np.ascontiguousarray(np.concatenate([xs, xp], 0))
    cc = np.concatenate([np.asarray(inp["c"][core], f).reshape(8, 128).T,
                         np.asarray(inp["c_ctx"], f).reshape(8, 128).T], 1)
    m["cc"] = np.ascontiguousarray(cc)
    m.update(consts)
    return m


def prep_shared_inputs(inp):
    f = np.float32
    m = {}
    wa = np.asarray(inp["w_ada"], f)
    m["w_ada"] = np.ascontiguousarray(wa.reshape(2, 8, 128, 3072).transpose(0, 2, 1, 3))
    ba = np.asarray(inp["b_ada"], f)
    m["b_ada_col"] = np.ascontiguousarray(ba.reshape(2, 24, 128).transpose(0, 2, 1))
    m["b_ada_row"] = np.ascontiguousarray(ba)
    m["ev_w_in"] = np.ascontiguousarray(np.asarray(inp["ev_w_in"], f)[0].reshape(8, 128, 5152).transpose(1, 0, 2))
    cw = np.asarray(inp["ev_conv_w"], f)[0]
    m["conv_w"] = np.ascontiguousarray(cw.reshape(5, 16, 128).transpose(2, 1, 0))
    m["conv_b"] = np.ascontiguousarray(np.asarray(inp["ev_conv_b"], f)[0].reshape(16, 128).T)
    m["a_log"] = np.ascontiguousarray(np.asarray(inp["ev_A_log"], f)[0].reshape(1, 32))
    m["dt_bias"] = np.ascontiguousarray(np.asarray(inp["ev_dt_bias"], f)[0].reshape(1, 32))
    m["d_skip"] = np.ascontiguousarray(np.asarray(inp["ev_D"], f).reshape(1, 16))
    m["norm_g"] = np.ascontiguousarray(np.asarray(inp["ev_norm_g"], f)[0].reshape(8, 128).T)
    m["pool_scale"] = np.ascontiguousarray(np.asarray(inp["ev_pool_scale"], f)[0].reshape(8, 128).T)
    pw = np.asarray(inp["ev_pool_w"], f)[0]
    m["pool_w"] = np.ascontiguousarray(pw.reshape(4, 2, 128, 256).transpose(0, 2, 1, 3))
    m["ev_w_out"] = np.ascontiguousarray(np.asarray(inp["ev_w_out"], f)[0].reshape(16, 128, 1024).transpose(1, 0, 2))
    m["od_w_in"] = np.ascontiguousarray(np.asarray(inp["od_w_in"], f)[0].reshape(8, 128, 6656).transpose(1, 0, 2))
    m["od_w_out"] = np.ascontiguousarray(np.asarray(inp["od_w_out"], f)[0].reshape(16, 128, 1024).transpose(1, 0, 2))
    m["od_lambda"] = np.ascontiguousarray(np.asarray(inp["od_lambda"], f)[0].reshape(1, 256))
    m["subln_g"] = np.ascontiguousarray(np.asarray(inp["od_subln_g"], f)[0].reshape(128, 1))
    m["sink"] = np.ascontiguousarray(np.asarray(inp["od_sink"], f)[0].reshape(1, 16))
    m["fnorm_g"] = np.ascontiguousarray(np.asarray(inp["final_norm_g"], f).reshape(1, 1024))
    return m


def prep_core_caches(inp, core, m):
    f = np.float32
    m["hf0"] = np.ascontiguousarray(np.asarray(inp["state_ssd_fwd"], f)[core, 0].reshape(1024, 128).T)
    m["hb0"] = np.ascontiguousarray(np.asarray(inp["state_ssd_bwd"], f)[core, 0].reshape(1024, 128).T)
    m["dk_T"] = np.ascontiguousarray(np.asarray(inp["cache_diff_k"], f)[core, 0].transpose(0, 2, 1))
    m["dv"] = np.ascontiguousarray(np.asarray(inp["cache_diff_v"], f)[core, 0])
    m["wk_T"] = np.ascontiguousarray(np.asarray(inp["cache_win_k"], f)[core, 0].transpose(0, 2, 1))
    m["wv"] = np.ascontiguousarray(np.asarray(inp["cache_win_v"], f)[core, 0])
    return m


_PROGRAM = {}


def _get_program():
    if "nc" not in _PROGRAM:
        b = Builder(debug=False)
        _PROGRAM["nc"] = b.nc
    return _PROGRAM["nc"]


def kernel(**inputs):
    n = 8
    nc = _get_program()
    consts = make_consts()
    shared = prep_shared_inputs(inputs)
    in_maps = []
    for core in range(n):
        m = prep_core_inputs(inputs, core, consts)
        m.update(shared)
        prep_core_caches(inputs, core, m)
        in_maps.append(m)
    res = run_bass_kernel_spmd(nc, in_maps, core_ids=list(range(n)))
    R = res.results
    f = np.float32
    y_prompt = np.zeros((16, 256, D), f)
    y_sample = np.zeros((8, 4096, D), f)
    ssd_f = np.zeros((16, 1, 16, 64, 128), f)
    ssd_b = np.zeros((16, 1, 16, 64, 128), f)
    dk = np.zeros((16, 1, 8, 256, 128), f)
    dv = np.zeros((16, 1, 8, 256, 128), f)
    wk = np.zeros((16, 1, 4, 256, 64), f)
    wv = np.zeros((16, 1, 4, 256, 64), f)
    for c in range(n):
        r = R[c]
        ya = np.asarray(r["y_all"], f)
        y_sample[c] = ya[0:4096]
        y_prompt[2 * c:2 * c + 2] = ya[4096:4608].reshape(2, 256, D)
        ssd_f[2 * c:2 * c + 2, 0] = np.asarray(r["o_ssdf"], f).reshape(2, 16, 64, 128)
        ssd_b[2 * c:2 * c + 2, 0] = np.asarray(r["o_ssdb"], f).reshape(2, 16, 64, 128)
        dk[2 * c:2 * c + 2, 0] = np.asarray(r["o_dk"], f).reshape(2, 256, 8, 128).transpose(0, 2, 1, 3)
        dv[2 * c:2 * c + 2, 0] = np.asarray(r["o_dv"], f).reshape(2, 256, 8, 128).transpose(0, 2, 1, 3)
        wk[2 * c:2 * c + 2, 0] = np.asarray(r["o_wk"], f).reshape(2, 256, 4, 64).transpose(0, 2, 1, 3)
        wv[2 * c:2 * c + 2, 0] = np.asarray(r["o_wv"], f).reshape(2, 256, 4, 64).transpose(0, 2, 1, 3)
    return (y_prompt, y_sample, ssd_f, ssd_b, dk, dv, wk, wv)
```

```python
import numpy as np
import concourse.bass as bass
import concourse.mybir as mybir

F32 = mybir.dt.float32
BF16 = mybir.dt.bfloat16
AF = mybir.ActivationFunctionType
ALU = mybir.AluOpType

ENGS = ("pe", "act", "dve", "pool", "sp")


class Op:
    __slots__ = ("eng", "fn", "deps", "signal", "sigval", "is_dma", "sem", "seq", "is_barrier")

    def __init__(self, eng, fn, is_dma=False):
        self.eng = eng
        self.fn = fn
        self.deps = set()
        self.signal = False
        self.sigval = 0
        self.is_dma = is_dma
        self.sem = None
        self.seq = 0
        self.is_barrier = False


class Res:
    _uid = 0

    def __init__(self, name):
        self.name = name
        self.lastw = {}
        self.readers = {}
        Res._uid += 1
        self.uid = Res._uid

    def _writers(self, key):
        if key is None:
            return list(self.lastw.values())
        out = []
        if key in self.lastw:
            out.append(self.lastw[key])
        if None in self.lastw:
            out.append(self.lastw[None])
        return out

    def read(self, op, key):
        for w in self._writers(key):
            op.deps.add(w)
        self.readers.setdefault(key, []).append(op)

    @staticmethod
    def _add_readers(op, lst):
        last = {}
        for r in lst:
            if r.is_dma:
                op.deps.add(r)
            else:
                p = last.get(r.eng)
                if p is None or r.seq > p.seq:
                    last[r.eng] = r
        for r in last.values():
            op.deps.add(r)

    def write(self, op, key):
        for w in self._writers(key):
            op.deps.add(w)
        if key is None:
            for lst in self.readers.values():
                self._add_readers(op, lst)
            self.lastw = {None: op}
            self.readers = {}
        else:
            self._add_readers(op, self.readers.get(key, ()))
            self._add_readers(op, self.readers.get(None, ()))
            self.lastw[key] = op
            self.readers[key] = []


class V:
    __slots__ = ("res", "key", "ap")

    def __init__(self, res, key, ap):
        self.res = res
        self.key = key
        self.ap = ap

    def __getitem__(self, idx):
        return V(self.res, self.key, self.ap[idx])

    def m(self, f):
        return V(self.res, self.key, f(self.ap))


class T:
    def __init__(self, handle, name, space="sb"):
        self.h = handle
        self.res = Res(name)
        self.res.space = space
        self.name = name

    def k(self, key=None):
        return V(self.res, key, self.h.ap())

    def __getitem__(self, idx):
        return V(self.res, None, self.h.ap()[idx])


class KB:
    def __init__(self):
        self.nc = bass.Bass("TRN2", target_bir_lowering=False)
        self.ops = []
        self.sb_lo = 16512
        self.sb_hi = 229344
        self.sb_ptr = self.sb_lo
        self.sb_stack = []
        self.nid = 0
        self.dma_sems = {}
        self.dma_last = {}
        self.dma_cnt = {}
        self.final_deps = []
        self.last_op = {}

    def sb(self, name, shape, dtype):
        nbytes = int(np.prod(shape[1:])) * (4 if dtype == F32 else 2)
        off = (self.sb_ptr + 31) // 32 * 32
        assert off + nbytes <= self.sb_hi, f"SBUF overflow allocating {name}: {off}+{nbytes}"
        self.sb_ptr = off + nbytes
        self.nid += 1
        h = self.nc.alloc_sbuf_tensor_at(f"{name}_{self.nid}", list(shape), dtype, offset=off)
        return T(h, name)

    def mark(self):
        self.sb_stack.append(self.sb_ptr)

    def release(self):
        self.barrier()
        self.sb_ptr = self.sb_stack.pop()

    def psum(self, name, shape, dtype=F32):
        h = self.nc.alloc_psum_tensor(name, list(shape), dtype)
        return T(h, name, "ps")

    def dram(self, name, shape, dtype, kind=None):
        if kind is None:
            h = self.nc.dram_tensor(name, list(shape), dtype)
        else:
            h = self.nc.dram_tensor(name, list(shape), dtype, kind=kind)
        return T(h, name, "dram")

    def _reg(self, op, reads, writes):
        for v in reads:
            v.res.read(op, v.key)
        for v in writes:
            v.res.write(op, v.key)
        op.deps.discard(op)
        op.seq = len(self.ops)
        self.ops.append(op)
        self.last_op[op.eng] = op

    def I(self, eng, meth, wr=("out",), extra_r=(), extra_w=(), **kw):
        reads, writes = list(extra_r), list(extra_w)
        args = {}
        for k_, v in kw.items():
            if isinstance(v, V):
                args[k_] = v.ap
                if k_ in wr or k_ == "accum_out":
                    writes.append(v)
                else:
                    reads.append(v)
            else:
                args[k_] = v

        def fn(e, meth=meth, args=args):
            return getattr(e, meth)(**args)

        op = Op(eng, fn)
        self._reg(op, reads, writes)
        return op

    def mm(self, out, lhsT, rhs, start=True, stop=True, **kw):
        args = dict(start=start, stop=stop, **kw)

        def fn(e, o=out.ap, l=lhsT.ap, r=rhs.ap, args=args):
            return e.matmul(o, l, r, **args)

        op = Op("pe", fn)
        self._reg(op, [lhsT, rhs], [out])
        return op

    def tr(self, out, in_, ident):
        def fn(e, o=out.ap, i=in_.ap, d=ident.ap):
            return e.transpose(o, i, d)

        op = Op("pe", fn)
        self._reg(op, [in_, ident], [out])
        return op

    def dma(self, out, in_, eng="sp", semkey=None, final=False):
        def fn(e, o=out.ap, i=in_.ap):
            return e.dma_start(out=o, in_=i)

        op = Op(eng, fn, is_dma=True)
        if semkey is None:
            side = out if out.res.space != "dram" else in_
            semkey = ("t", side.res.uid, side.key)
        op.sem = semkey
        prev = self.dma_last.get(semkey)
        if prev is not None:
            op.deps.add(prev)
        self.dma_last[semkey] = op
        self._reg(op, [in_], [out])
        if final:
            self.final_deps.append(op)
        return op

    def barrier(self):
        lasts = [o for o in self.last_op.values()] + list(self.dma_last.values())
        for e in ENGS:
            op = Op(e, None)
            op.is_barrier = True
            for l in lasts:
                op.deps.add(l)
            op.seq = len(self.ops)
            self.ops.append(op)
            self.last_op[e] = op

    def finish(self, same_eng_sync=True):
        nc = self.nc
        fin = Op("sp", None)
        for d in self.final_deps:
            fin.deps.add(d)
        fin.seq = len(self.ops)
        self.ops.append(fin)
        def needs_sig(op, d):
            if d.fn is None:
                return False
            if d.is_dma:
                return True
            if d.eng == op.eng and not op.is_dma and (op.eng == "pe" or not same_eng_sync):
                return False
            return True
        for op in self.ops:
            op.deps.discard(op)
            for d in op.deps:
                if needs_sig(op, d):
                    d.signal = True
        def expand(op):
            out = set()
            stack = list(op.deps)
            while stack:
                d = stack.pop()
                if d.fn is None:
                    stack.extend(d.deps)
                else:
                    out.add(d)
            return out
        for op in self.ops:
            if any(d.fn is None for d in op.deps):
                op.deps = expand(op)
                for d in op.deps:
                    if needs_sig(op, d):
                        d.signal = True
        esem = {e: nc.alloc_semaphore(f"s_{e}") for e in ("pe", "act", "dve", "pool")}
        ecnt = {e: 0 for e in esem}
        dsem = {}
        active = {}
        free = {False: [], True: []}
        nslots = [0]
        for op in self.ops:
            if op.fn is None:
                if op.is_barrier and op.eng == "pe":
                    for key, slot in active.items():
                        free[slot[2]].append(slot)
                    active = {}
                continue
            if op.is_dma:
                sw = (op.eng == "pool")
                if op.sem not in active:
                    if free[sw]:
                        slot = free[sw].pop()
                    else:
                        slot = [nc.alloc_semaphore(f"d{nslots[0]}"), 0, sw]
                        nslots[0] += 1
                    active[op.sem] = slot
                slot = active[op.sem]
                assert slot[2] == sw, f"semkey {op.sem} mixes SW and HW DGE"
                slot[1] += 16
                op.sigval = slot[1]
                op.signal = True
                op.sem = ("slot", id(slot), op.seq)
                dsem[op.sem] = slot[0]
            elif op.signal:
                ecnt[op.eng] += 1
                op.sigval = ecnt[op.eng]
        self._slots_keepalive = (active, free)
        self.sem_stats = (dict(ecnt), nslots[0], max([sl[1] for sl in free[False] + free[True] + list(active.values())] + [0]))
        self.n_dma_sems = nslots[0]
        streams = {e: [o for o in self.ops if o.eng == e] for e in ENGS}
        engobj = {"pe": "tensor", "act": "scalar", "dve": "vector", "pool": "gpsimd", "sp": "sync"}

        def emit(e, eng):
            known = {}
            for op in streams[e]:
                for d in sorted(op.deps, key=lambda o: o.seq):
                    if d.is_dma:
                        sem, val = dsem[d.sem], d.sigval
                    else:
                        if d.eng == e and not op.is_dma:
                            if e == "pe" or not same_eng_sync:
                                continue
                        sem, val = esem[d.eng], d.sigval
                    kk = id(sem)
                    if known.get(kk, 0) >= val:
                        continue
                    known[kk] = val
                    eng.wait_ge(sem, val)
                if op.fn is None:
                    continue
                inst = op.fn(eng)
                if op.is_dma:
                    inst.then_inc(dsem[op.sem], 16)
                elif op.signal:
                    inst.then_inc(esem[op.eng], 1)

        with nc.Block() as block:
            @block.tensor
            def _(eng):
                emit("pe", eng)

            @block.scalar
            def _(eng):
                emit("act", eng)

            @block.vector
            def _(eng):
                emit("dve", eng)

            @block.gpsimd
            def _(eng):
                emit("pool", eng)

            @block.sync
            def _(eng):
                emit("sp", eng)
        return nc
from concourse.bass_utils import run_bass_kernel_spmd
import math
import ml_dtypes

NTOK = 4608
NT = 36
D = 1024
EPS = 1e-6
SEQS = [(0, 4096, 0), (4096, 256, 1), (4352, 256, 1)]
SEGS = [(b * 512, 512) for b in range(8)] + [(4096, 256), (4352, 256)]
LAM_INIT = 0.8 - 0.6 * math.exp(-0.3 * 1)


def seq_of(tok):
    return 0 if tok < 4096 else (1 if tok < 4352 else 2)


def scol(tok):
    return tok + 2 + 4 * seq_of(tok)


def pcol(tok):
    return tok + 8 + 16 * seq_of(tok)


class Builder:
    def __init__(self, debug=False, stop_after=None):
        self.k = KB()
        self.debug = debug
        self.stop_after = stop_after
        self.dbg_outs = []
        self.build()

    def din(self, name, shape, dtype=F32):
        return self.k.dram(name, shape, dtype, kind="ExternalInput")

    def dout(self, name, shape, dtype=F32):
        return self.k.dram(name, shape, dtype, kind="ExternalOutput")

    def scratch(self, name, shape, dtype):
        if self.debug:
            self.dbg_outs.append(name)
            return self.k.dram(name, shape, dtype, kind="ExternalOutput")
        return self.k.dram(name, shape, dtype)

    def dump(self, name, view, shape, dtype=F32):
        if not self.debug:
            return
        t = self.k.dram("dbg_" + name, list(shape), dtype, kind="ExternalOutput")
        self.dbg_outs.append("dbg_" + name)
        self.k.dma(t.k(), view, eng="sp", semkey=("dbg", name))

    def next_ps(self):
        self.ps_i = (self.ps_i + 1) % len(self.PS)
        return self.PS[self.ps_i]

    def build(self):
        k = self.k
        self.x_all = self.din("x_all", [NTOK, D])
        self.cc = self.din("cc", [128, 16])
        self.w_ada = self.din("w_ada", [2, 128, 8, 3072])
        self.b_ada_col = self.din("b_ada_col", [2, 128, 24])
        self.b_ada_row = self.din("b_ada_row", [2, 3072])
        self.ev_w_in = self.din("ev_w_in", [128, 8, 5152])
        self.conv_w = self.din("conv_w", [128, 16, 5])
        self.conv_b = self.din("conv_b", [128, 16])
        self.a_log = self.din("a_log", [1, 32])
        self.dt_bias = self.din("dt_bias", [1, 32])
        self.d_skip = self.din("d_skip", [1, 16])
        self.norm_g = self.din("norm_g", [128, 8])
        self.pool_scale = self.din("pool_scale", [128, 8])
        self.pool_w = self.din("pool_w", [4, 128, 2, 256])
        self.ev_w_out = self.din("ev_w_out", [128, 16, 1024])
        self.od_w_in = self.din("od_w_in", [128, 8, 6656])
        self.od_w_out = self.din("od_w_out", [128, 16, 1024])
        self.od_lambda = self.din("od_lambda", [1, 256])
        self.subln_g = self.din("subln_g", [128, 1])
        self.sink = self.din("sink", [1, 16])
        self.fnorm_g = self.din("fnorm_g", [1, 1024])
        self.hf0 = self.din("hf0", [128, 1024])
        self.hb0 = self.din("hb0", [128, 1024])
        self.dk_T = self.din("dk_T", [8, 128, 512])
        self.dv = self.din("dv", [8, 512, 128])
        self.wk_T = self.din("wk_T", [4, 64, 512])
        self.wv = self.din("wv", [4, 512, 64])
        self.c_ident = self.din("c_ident", [128, 128])
        self.c_triu = self.din("c_triu", [128, 128])
        self.c_tril = self.din("c_tril", [128, 128])
        self.c_nmf = self.din("c_nmf", [128, 128])
        self.c_nmb = self.din("c_nmb", [128, 128])
        self.c_sel = self.din("c_sel", [64, 32, 128])
        self.c_pm = self.din("c_pm", [128, 128])
        self.c_cos = self.din("c_cos", [128, 4096])
        self.c_sin = self.din("c_sin", [128, 4096])
        self.c_pedge = self.din("c_pedge", [128, 4, 16])
        self.y_all = self.dout("y_all", [NTOK, D])
        self.o_ssdf = self.dout("o_ssdf", [2, 1024, 128])
        self.o_ssdb = self.dout("o_ssdb", [2, 1024, 128])
        self.o_dk = self.dout("o_dk", [512, 1024])
        self.o_dv = self.dout("o_dv", [512, 1024])
        self.o_wk = self.dout("o_wk", [512, 256])
        self.o_wv = self.dout("o_wv", [512, 256])
        self.sza = self.scratch("sza", [NTOK, 1024], F32)
        self.xcT = self.scratch("xcT", [2048, NTOK], BF16)
        self.ypT = self.scratch("ypT", [1024, NTOK], BF16)
        self.x1 = self.scratch("x1", [NTOK, D], F32)
        self.yloc = self.scratch("yloc", [NTOK, 1024], F32)
        self.QcT = self.scratch("QcT", [1024, NTOK], BF16)
        self.KcT = self.scratch("KcT", [1024, NTOK], BF16)
        self.Vc = self.scratch("Vc", [NTOK, 1024], BF16)
        self.szcT = self.scratch("szcT", [1024, NTOK], F32)
        self.QdT = self.scratch("QdT", [1024, NTOK], BF16)
        self.KdT = self.scratch("KdT", [256, NTOK], BF16)
        self.Vd = self.scratch("Vd", [NTOK, 256], BF16)
        self.szdT = self.scratch("szdT", [1024, NTOK], F32)
        self.ocT = self.scratch("ocT", [1024, NTOK], BF16)
        self.odT = self.scratch("odT", [1024, NTOK], BF16)
        self.yoff = [self.scratch(f"yoff{d}", [NTOK, 1024], F32) for d in range(2)]
        self.xw_s = [self.scratch(f"xw{d}", [NTOK, 1024], BF16) for d in range(2)]
        self.btok_s = self.scratch("btok", [NTOK, 512], BF16)

        self.PS = [k.psum(f"ps{i}", [128, 512], F32) for i in range(8)]
        self.ps_i = -1

        self.ident = k.sb("ident", [128, 128], F32)
        self.identb = k.sb("identb", [128, 128], BF16)
        self.epsc = k.sb("epsc", [128, 1], F32)
        self.modT = [k.sb(f"modT{l}", [128, 24, 2], F32) for l in range(2)]
        self.gate_bc = [[k.sb(f"gate{l}{c}", [128, 1024], F32) for c in range(2)] for l in range(2)]
        self.dt_all = k.sb("dt_all", [128, NT, 32], F32)
        self.e_all = k.sb("e_all", [128, NT, 32], F32)
        self.dec_all = k.sb("dec_all", [128, NT, 32], F32)
        k.dma(self.ident.k(), self.c_ident.k())
        k.dma(self.identb.k(), self.c_ident.k(), eng="pool")
        k.I("dve", "memset", wr=("ap",), ap=self.epsc.k(), constant=EPS)

        self.phase0()
        for l in range(2):
            self.dump(f"modT{l}", self.modT[l].k(), [128, 24, 2])
            for c in range(2):
                self.dump(f"gate{l}{c}", self.gate_bc[l][c].k(), [128, 1024])
        if self.stop_after == "p0":
            return self.finish_debug()
        k.mark()
        self.hT = k.sb("hT", [128, 8, NTOK], BF16)
        self.build_hT(self.x_all, 0)
        self.dump("hT", self.hT.k(), [128, 8, NTOK], BF16)
        self.passA1()
        self.dump("dt_all", self.dt_all.k(), [128, NT, 32])
        if self.stop_after == "A1":
            return self.finish_debug()
        self.passA2()
        k.release()
        if self.stop_after == "A2":
            return self.finish_debug()
        self.passB()
        if self.stop_after == "B":
            return self.finish_debug()
        self.passC()
        if self.stop_after == "C":
            return self.finish_debug()
        self.passD()
        if self.stop_after == "D":
            return self.finish_debug()
        if self.stop_after == "L1only":
            pass
        k.mark()
        self.hT = k.sb("hT", [128, 8, NTOK], BF16)
        self.build_hT(self.x1, 1)
        if self.stop_after == "hT1":
            return self.finish_debug()
        self.passE()
        k.release()
        if self.stop_after in ("E", "E00", "E0", "E0a", "E0b", "E1", "E2"):
            return self.finish_debug()
        self.passF()
        if self.stop_after == "F":
            return self.finish_debug()
        self.passG()
        if self.stop_after in ("G",):
            return self.finish_debug()
        self.passH()
        self.nc = k.finish()

    def finish_debug(self):
        k = self.k
        self.nc = k.finish()

    def phase0(self):
        k = self.k
        k.mark()
        cc_sb = k.sb("cc_sb", [128, 16], F32)
        sc = k.sb("sc", [128, 16], F32)
        scb = k.sb("scb", [128, 16, 128], F32)
        wada = k.sb("wada", [128, 8, 3072], F32)
        bcol = k.sb("bcol", [128, 24], F32)
        brow = k.sb("brow", [128, 1024], F32)
        k.dma(cc_sb.k(), self.cc.k())
        k.I("act", "activation", out=sc.k(), in_=cc_sb.k(), func=AF.Silu)
        k.I("dve", "tensor_copy", out=scb.k(), in_=sc.k().m(lambda a: a.unsqueeze(2).to_broadcast([128, 16, 128])))
        for l in range(2):
            for q in range(6):
                k.dma(wada.k(("q", q))[:, :, q * 512:(q + 1) * 512],
                      self.w_ada.k()[l, :, :, q * 512:(q + 1) * 512])
            k.dma(bcol.k(), self.b_ada_col.k()[l])
            k.dma(brow.k(), self.b_ada_row.k()[l:l + 1, 2048:3072].m(lambda a: a.partition_broadcast(128)))
            ps = self.next_ps()
            for j in range(24):
                q = j // 4
                for kk in range(8):
                    k.mm(ps.k()[:, 2 * j:2 * j + 2], wada.k(("q", q))[:, kk, j * 128:(j + 1) * 128],
                         sc.k()[:, kk:16:8], start=(kk == 0), stop=(kk == 7))
            k.I("dve", "tensor_tensor", out=self.modT[l].k(),
                in0=ps.k()[:, 0:48].m(lambda a: a.rearrange("p (j c) -> p j c", c=2)),
                in1=bcol.k().m(lambda a: a.unsqueeze(2).to_broadcast([128, 24, 2])), op=ALU.add)
            k.I("dve", "tensor_scalar", out=self.modT[l].k()[:, 8:16, :], in0=self.modT[l].k()[:, 8:16, :],
                scalar1=1.0, scalar2=None, op0=ALU.add)
            for cond in range(2):
                for n in range(2):
                    ps2 = self.next_ps()
                    for kk in range(8):
                        k.mm(ps2.k(), scb.k()[:, cond * 8 + kk, :],
                             wada.k(("q", 4 + n))[:, kk, 2048 + n * 512:2048 + (n + 1) * 512],
                             start=(kk == 0), stop=(kk == 7))
                    k.I("dve", "tensor_tensor", out=self.gate_bc[l][cond].k()[:, n * 512:(n + 1) * 512],
                        in0=ps2.k(), in1=brow.k()[:, n * 512:(n + 1) * 512], op=ALU.add)
        k.release()

    def build_hT(self, xsrc, l):
        k = self.k
        k.mark()
        xr = [k.sb(f"xr{i}", [128, 1024], F32) for i in range(2)]
        xn = [k.sb(f"xn{i}", [128, 1024], F32) for i in range(2)]
        sq = k.sb("sq", [128, 1024], F32)
        ss = [k.sb(f"ss{i}", [128, 1], F32) for i in range(2)]
        rs = [k.sb(f"rs{i}", [128, 1], F32) for i in range(2)]
        rstd = [k.sb(f"rstd{i}", [128, 1], F32) for i in range(2)]
        xr.append(k.sb("xr2", [128, 1024], F32))
        pss = {}

        def stage_a(i):
            b = i % 2
            b3 = i % 3
            k.dma(xr[b3].k(), xsrc.k()[i * 128:(i + 1) * 128, :])
            k.I("act", "activation", out=sq.k(), in_=xr[b3].k(), func=AF.Square, accum_out=ss[b].k())
            k.I("act", "activation", out=rs[b].k(), in_=ss[b].k(), func=AF.Sqrt, scale=1.0 / D, bias=self.epsc.k())
            k.I("dve", "reciprocal", out=rstd[b].k(), in_=rs[b].k())
            k.I("dve", "tensor_scalar", out=xn[b].k(), in0=xr[b3].k(), scalar1=rstd[b].k(), scalar2=None, op0=ALU.mult)
            lst = []
            for half in range(2):
                ps = self.next_ps()
                for q in range(4):
                    c = half * 4 + q
                    k.tr(ps.k()[:, q * 128:(q + 1) * 128], xn[b].k()[:, c * 128:(c + 1) * 128], self.ident.k())
                lst.append(ps)
            pss[i] = lst

        def stage_b(i):
            cond = 0 if i < 32 else 1
            for half in range(2):
                ps = pss[i][half]
                for q in range(4):
                    c = half * 4 + q
                    o = self.hT.k(("t", i))[:, c, i * 128:(i + 1) * 128]
                    sc_ = self.modT[l].k()[:, 8 + c, cond:cond + 1]
                    sh_ = self.modT[l].k()[:, c, cond:cond + 1]
                    if half == 0:
                        k.I("act", "activation", out=o, in_=ps.k()[:, q * 128:(q + 1) * 128],
                            func=AF.Identity, scale=sc_, bias=sh_)
                    else:
                        k.I("dve", "tensor_scalar", out=o, in0=ps.k()[:, q * 128:(q + 1) * 128],
                            scalar1=sc_, scalar2=sh_, op0=ALU.mult, op1=ALU.add)
            del pss[i]

        stage_a(0)
        for i in range(NT):
            if i + 1 < NT:
                stage_a(i + 1)
            stage_b(i)
        k.release()

    def hT_cols(self, t0, n):
        return [self.hT.k(("t", i)) for i in range(t0 // 128, (t0 + n + 127) // 128)]

    def passA1(self):
        k = self.k
        k.mark()
        wfm = [k.sb(f"wfm{i}", [128, 8, 128], BF16) for i in range(4)]
        wtm = [k.sb(f"wtm{i}", [128, 8, 512], BF16) for i in range(2)]
        wdt = k.sb("wdt", [128, 8, 32], BF16)
        strips = [k.sb(f"strip{i}", [128, 4620], BF16) for i in range(2)]
        dg = k.sb("dg", [128, 16, 5, 128], BF16)
        cw = k.sb("cw", [128, 16, 5], F32)
        cb = k.sb("cb", [128, 16], F32)
        stg = [k.sb(f"stg{i}", [128, 512], F32) for i in range(3)]
        stgc = [k.sb(f"stgc{i}", [128, 512], BF16) for i in range(3)]
        dtraw = k.sb("dtraw", [128, NT, 32], F32)
        dtb = k.sb("dtb", [128, 32], F32)
        hT = self.hT

        k.dma(cw.k(), self.conv_w.k())
        k.dma(cb.k(), self.conv_b.k())
        k.dma(dtb.k(), self.dt_bias.k().m(lambda a: a.partition_broadcast(128)))
        for j in range(16):
            k.I("dve", "tensor_tensor", out=dg.k(("j", j))[:, j, :, :],
                in0=self.ident.k().m(lambda a: a.unsqueeze(1).to_broadcast([128, 5, 128])),
                in1=cw.k()[:, j, :].m(lambda a: a.unsqueeze(2).to_broadcast([128, 5, 128])), op=ALU.mult)
        for s in strips:
            k.I("pool", "memset", wr=("ap",), ap=s.k(), constant=0.0)

        k.dma(wdt.k(), self.ev_w_in.k()[:, :, 3072:3104], eng="pool")
        for i in range(NT):
            ps = self.next_ps()
            for kk in range(8):
                k.mm(ps.k()[:, 0:32], hT.k(("t", i))[:, kk, i * 128:(i + 1) * 128], wdt.k()[:, kk, :],
                     start=(kk == 0), stop=(kk == 7))
            k.I("dve", "tensor_tensor", out=dtraw.k(("t", i))[:, i, :], in0=ps.k()[:, 0:32], in1=dtb.k(), op=ALU.add)
        k.I("dve", "tensor_scalar", out=dtraw.k(), in0=dtraw.k(), scalar1=30.0, scalar2=None, op0=ALU.min)
        k.I("act", "activation", out=dtraw.k(), in_=dtraw.k(), func=AF.Exp)
        k.I("act", "activation", out=self.dt_all.k(), in_=dtraw.k(), func=AF.Ln, bias=1.0)

        n_st = 0
        for fb in range(2):
            W = wtm[fb % 2]
            k.dma(W.k(), self.ev_w_in.k()[:, :, fb * 512:(fb + 1) * 512], eng="pool")
            for i in range(NT):
                ps = self.next_ps()
                for kk in range(8):
                    k.mm(ps.k(), hT.k(("t", i))[:, kk, i * 128:(i + 1) * 128], W.k()[:, kk, :],
                         start=(kk == 0), stop=(kk == 7))
                st = stg[n_st % 3]
                n_st += 1
                k.I("act", "activation", out=st.k(), in_=ps.k(), func=AF.Silu)
                k.dma(self.sza.k(("t", i, fb))[i * 128:(i + 1) * 128, fb * 512:(fb + 1) * 512], st.k())

        n_sc = 0
        n_ev = 0
        for j in range(16):
            W = wfm[j % 4]
            k.dma(W.k(), self.ev_w_in.k()[:, :, 1024 + j * 128:1024 + (j + 1) * 128], eng="pool")
            strip = strips[j % 2]
            for si, (t0, n) in enumerate(SEGS):
                ps = self.next_ps()
                for kk in range(8):
                    k.mm(ps.k()[:, 0:n], W.k()[:, kk, :], hT.k()[:, kk, t0:t0 + n],
                         start=(kk == 0), stop=(kk == 7))
                c0 = scol(t0)
                if n_ev % 2 == 0:
                    k.I("act", "activation", out=strip.k(("s", si))[:, c0:c0 + n], in_=ps.k()[:, 0:n], func=AF.Copy)
                else:
                    k.I("dve", "tensor_copy", out=strip.k(("s", si))[:, c0:c0 + n], in_=ps.k()[:, 0:n])
                n_ev += 1
            for si, (t0, n) in enumerate(SEGS):
                ps = self.next_ps()
                c0 = scol(t0)
                nb = [strip.k(("s", s2)) for s2 in (si - 1, si + 1) if 0 <= s2 < len(SEGS)]
                for tap in range(5):
                    op = k.mm(ps.k()[:, 0:n], dg.k(("j", j))[:, j, tap, :],
                              strip.k(("s", si))[:, c0 + tap - 2:c0 + tap - 2 + n],
                              start=(tap == 0), stop=(tap == 4))
                    for v in nb:
                        v.res.read(op, v.key)
                st = stgc[n_sc % 3]
                n_sc += 1
                k.I("act", "activation", out=st.k()[:, 0:n], in_=ps.k()[:, 0:n], func=AF.Silu, bias=cb.k()[:, j:j + 1])
                k.dma(self.xcT.k(("j", j, si))[j * 128:(j + 1) * 128, t0:t0 + n], st.k()[:, 0:n])
        k.release()

    def passA2(self):
        k = self.k
        hT = self.hT
        k.mark()
        PW = NTOK + 48
        wfm = [k.sb(f"wfm{i}", [128, 8, 128], BF16) for i in range(4)]
        xb = [k.sb(f"xb{i}", [128, PW], F32) for i in range(2)]
        pl = [k.sb(f"pl{i}", [128, NTOK], BF16) for i in range(2)]
        szb = k.sb("szb", [128, NTOK], F32)
        tA = [k.sb(f"tA{i}", [128, 528], F32) for i in range(2)]
        tB = [k.sb(f"tB{i}", [128, 528], F32) for i in range(2)]
        te = [k.sb(f"te{i}", [128, 8], F32) for i in range(2)]
        pw = k.sb("pw", [128, 4, 2, 256], BF16)
        psc = k.sb("psc", [128, 8], F32)
        pedge = k.sb("pedge", [128, 4, 16], F32)
        stg = [k.sb(f"stgp{i}", [128, 512], BF16) for i in range(3)]
        k.dma(pw.k(), self.pool_w.k().m(lambda a: a.rearrange("g p c d -> p g c d")), eng="pool")
        k.dma(psc.k(), self.pool_scale.k())
        k.dma(pedge.k(), self.c_pedge.k())
        for t in xb:
            k.I("pool", "memset", wr=("ap",), ap=t.k(), constant=0.0)
        nw = 0
        nev = 0
        nst = 0
        seq_starts = {t0 for (t0, T, c) in SEQS}
        seq_ends = {t0 + T for (t0, T, c) in SEQS}
        for g in range(4):
            w = (2, 4, 8, 16)[g]
            levels = g + 1
            for cc in range(2):
                W = wfm[nw % 4]
                nw += 1
                f0 = 4128 + g * 256 + cc * 128
                k.dma(W.k(), self.ev_w_in.k()[:, :, f0:f0 + 128], eng="pool")
                for si, (t0, n) in enumerate(SEGS):
                    ps = self.next_ps()
                    for kk in range(8):
                        k.mm(ps.k()[:, 0:n], W.k()[:, kk, :], hT.k()[:, kk, t0:t0 + n], start=(kk == 0), stop=(kk == 7))
                    c0 = pcol(t0)
                    if nev % 2 == 0:
                        k.I("act", "activation", out=xb[cc].k(("s", si))[:, c0:c0 + n], in_=ps.k()[:, 0:n], func=AF.Copy)
                    else:
                        k.I("dve", "tensor_copy", out=xb[cc].k(("s", si))[:, c0:c0 + n], in_=ps.k()[:, 0:n])
                    nev += 1
            def zb_proj(ft):
                nonlocal nw
                W = wfm[nw % 4]
                nw += 1
                f0 = 3104 + ft * 128
                k.dma(W.k(), self.ev_w_in.k()[:, :, f0:f0 + 128], eng="pool")
                for si, (t0, n) in enumerate(SEGS):
                    ps = self.next_ps()
                    for kk in range(8):
                        k.mm(ps.k()[:, 0:n], W.k()[:, kk, :], hT.k()[:, kk, t0:t0 + n], start=(kk == 0), stop=(kk == 7))
                    k.I("act", "activation", out=szb.k(("s", si))[:, t0:t0 + n], in_=ps.k()[:, 0:n], func=AF.Silu)
            zb_proj(g * 2)
            for cc in range(2):
                X = xb[cc]
                for si, (t0, n) in enumerate(SEGS):
                    par = si % 2
                    eng = "dve" if par == 0 else "pool"
                    c0 = pcol(t0)
                    base = c0 - 8
                    nbr = [X.k(("s", s2)) for s2 in (si - 1, si + 1) if 0 <= s2 < len(SEGS)]
                    Xv = X.k(("s", si))
                    cur, oth = tA[par], tB[par]
                    lo, hi = c0 - 7, c0 + n + 7
                    k.I(eng, "tensor_tensor", out=cur.k()[:, lo - base:hi - base], in0=Xv[:, lo - 1:hi - 1],
                        in1=Xv[:, lo:hi], op=ALU.add, extra_r=nbr)
                    for lv, (m, sh) in enumerate(((6, 1), (4, 2), (0, 4))):
                        if levels < lv + 2:
                            break
                        lo, hi = c0 - m, c0 + n + m
                        k.I(eng, "tensor_tensor", out=oth.k()[:, lo - base:hi - base],
                            in0=cur.k()[:, lo - sh - base:hi - sh - base], in1=cur.k()[:, lo + sh - base:hi + sh - base],
                            op=ALU.add)
                        cur, oth = oth, cur
                    k.I("dve", "scalar_tensor_tensor", out=pl[cc].k(("s", si))[:, t0:t0 + n], in0=cur.k()[:, 8:8 + n],
                        scalar=1.0 / w, in1=Xv[:, c0:c0 + n], op0=ALU.mult, op1=ALU.subtract)
                    if t0 in seq_starts:
                        k.I("dve", "tensor_tensor", out=te[par].k(), in0=cur.k()[:, 8:16], in1=pedge.k()[:, g, 0:8], op=ALU.mult)
                        k.I("dve", "tensor_tensor", out=pl[cc].k(("s", si))[:, t0:t0 + 8], in0=te[par].k(),
                            in1=Xv[:, c0:c0 + 8], op=ALU.subtract)
                    if t0 + n in seq_ends:
                        k.I("dve", "tensor_tensor", out=te[par].k(), in0=cur.k()[:, n:n + 8], in1=pedge.k()[:, g, 8:16], op=ALU.mult)
                        k.I("dve", "tensor_tensor", out=pl[cc].k(("s", si))[:, t0 + n - 8:t0 + n], in0=te[par].k(),
                            in1=Xv[:, c0 + n - 8:c0 + n], op=ALU.subtract)
            for dd in range(2):
                ft = g * 2 + dd
                if dd == 1:
                    zb_proj(ft)
                for si, (t0, n) in enumerate(SEGS):
                    ps = self.next_ps()
                    for cc in range(2):
                        k.mm(ps.k()[:, 0:n], pw.k()[:, g, cc, dd * 128:(dd + 1) * 128], pl[cc].k(("s", si))[:, t0:t0 + n],
                             start=(cc == 0), stop=(cc == 1))
                    st = stg[nst % 3]
                    nst += 1
                    k.I("dve", "scalar_tensor_tensor", out=st.k()[:, 0:n], in0=ps.k()[:, 0:n], scalar=psc.k()[:, ft:ft + 1],
                        in1=szb.k(("s", si))[:, t0:t0 + n], op0=ALU.mult, op1=ALU.mult)
                    k.dma(self.ypT.k(("f", ft, si))[ft * 128:(ft + 1) * 128, t0:t0 + n], st.k()[:, 0:n])
        k.release()

    def psbf(self, ps):
        return V(ps.res, None, ps.h.bitcast(BF16).ap())

    def passB(self):
        k = self.k
        k.mark()
        PS = self.PS
        triu = k.sb("triu", [128, 128], BF16)
        tril = k.sb("tril", [128, 128], BF16)
        onesb = k.sb("onesb", [128, 128], BF16)
        nmf = k.sb("nmf", [128, 128], BF16)
        nmb = k.sb("nmb", [128, 128], BF16)
        A_bc = k.sb("A_bc", [128, 32], F32)
        D_bc = k.sb("D_bc", [128, 16], F32)
        k.dma(triu.k(), self.c_triu.k(), eng="pool")
        k.dma(tril.k(), self.c_tril.k(), eng="pool")
        k.dma(nmf.k(), self.c_nmf.k(), eng="pool")
        k.dma(nmb.k(), self.c_nmb.k(), eng="pool")
        k.I("dve", "memset", wr=("ap",), ap=onesb.k(), constant=1.0)
        k.dma(A_bc.k(), self.a_log.k().m(lambda a: a.partition_broadcast(128)))
        k.dma(D_bc.k(), self.d_skip.k().m(lambda a: a.partition_broadcast(128)))
        k.I("act", "activation", out=A_bc.k(), in_=A_bc.k(), func=AF.Exp)
        k.I("dve", "tensor_scalar", out=A_bc.k(), in0=A_bc.k(), scalar1=-1.0, scalar2=None, op0=ALU.mult)
        tri_d = (triu, tril)
        nm_d = (nmf, nmb)

        def mk(name, shape, dt_):
            return [k.sb(f"{name}{i}", shape, dt_) for i in range(2)]
        xc = mk("xc", [128, 16, 128], BF16)
        a32 = mk("a32", [128, 32], F32)
        ahl = mk("ahl", [128, 64], BF16)
        cst = mk("cst", [128, 64], F32)
        ncs = mk("ncs", [128, 32], F32)
        dte = mk("dte", [128, 32], F32)
        wgt = mk("wgt", [128, 32], F32)
        cbT = mk("cbT", [128, 512], F32)
        E = [k.sb(f"E{i}", [128, 512], F32) for i in range(4)]
        M = [[k.sb(f"M{d}{i}", [128, 512], BF16) for i in range(2)] for d in range(2)]
        xs_sb = mk("xs_sb", [128, 1024], BF16)
        xdt = [mk(f"xdt{d}", [128, 1024], BF16) for d in range(2)]
        xw = [mk(f"xwl{d}", [128, 1024], BF16) for d in range(2)]
        xsD = mk("xsD", [128, 1024], F32)
        btk = mk("btk", [128, 512], BF16)
        yst = mk("yst", [128, 1024], F32)
        xcv = self.xcT.k().m(lambda a: a.rearrange("(j p) t -> p j t", p=128))
        nE = 0
        k.dma(xc[0].k(), V(self.xcT.res, None, xcv.ap[:, :, slice(0, 128)]))
        for i in range(NT):
            b = i % 2
            sl = slice(i * 128, (i + 1) * 128)
            if i + 1 < NT:
                k.dma(xc[(i + 1) % 2].k(), V(self.xcT.res, None, xcv.ap[:, :, slice((i + 1) * 128, (i + 2) * 128)]))
            dt = self.dt_all.k(("t", i))[:, i, :]
            k.I("dve", "tensor_tensor", out=a32[b].k(), in0=dt, in1=A_bc.k(), op=ALU.mult)
            k.I("dve", "tensor_copy", out=ahl[b].k()[:, 0:32], in_=a32[b].k())
            k.I("dve", "tensor_tensor", out=ahl[b].k()[:, 32:64], in0=a32[b].k(), in1=ahl[b].k()[:, 0:32], op=ALU.subtract)
            pc = PS[6]
            k.mm(pc.k()[:, 0:16], triu.k(), ahl[b].k()[:, 0:16], start=True, stop=False)
            k.mm(pc.k()[:, 0:16], triu.k(), ahl[b].k()[:, 32:48], start=False, stop=True)
            k.mm(pc.k()[:, 16:32], tril.k(), ahl[b].k()[:, 16:32], start=True, stop=False)
            k.mm(pc.k()[:, 16:32], tril.k(), ahl[b].k()[:, 48:64], start=False, stop=True)
            k.mm(pc.k()[:, 32:64], onesb.k(), ahl[b].k()[:, 0:32], start=True, stop=False)
            k.mm(pc.k()[:, 32:64], onesb.k(), ahl[b].k()[:, 32:64], start=False, stop=True)
            k.I("dve", "tensor_copy", out=cst[b].k(), in_=pc.k()[:, 0:64])
            k.I("dve", "tensor_scalar", out=ncs[b].k(), in0=cst[b].k()[:, 0:32], scalar1=-1.0, scalar2=None, op0=ALU.mult)
            k.I("act", "activation", out=self.e_all.k(("t", i))[:, i, :], in_=cst[b].k()[:, 0:32], func=AF.Exp)
            k.I("act", "activation", out=self.dec_all.k(("t", i))[:, i, :], in_=cst[b].k()[:, 32:64], func=AF.Exp)
            k.I("dve", "tensor_tensor", out=dte[b].k(), in0=cst[b].k()[:, 32:64], in1=cst[b].k()[:, 0:32], op=ALU.subtract)
            k.I("act", "activation", out=dte[b].k(), in_=dte[b].k(), func=AF.Exp)
            k.I("dve", "tensor_tensor", out=wgt[b].k(), in0=dt, in1=dte[b].k(), op=ALU.mult)
            pcb = PS[6]
            for g in range(4):
                k.mm(pcb.k()[:, g * 128:(g + 1) * 128], xc[b].k()[:, 8 + g, :], xc[b].k()[:, 12 + g, :])
            k.I("act", "activation", out=cbT[b].k(), in_=pcb.k(), func=AF.Copy)
            px = self.psbf(PS[7])
            for j in range(8):
                k.tr(px[:, j * 128:(j + 1) * 128], xc[b].k()[:, j, :], self.identb.k())
            k.I("act", "activation", out=xs_sb[b].k(), in_=px, func=AF.Copy)
            pb = self.psbf(PS[7])
            for g in range(4):
                k.tr(pb[:, g * 128:(g + 1) * 128], xc[b].k()[:, 8 + g, :], self.identb.k())
            k.I("dve", "tensor_copy", out=btk[b].k(), in_=pb[:, 0:512])
            x3 = xs_sb[b].k().m(lambda a: a.rearrange("p (r q) -> p r q", q=64))

            def bc16(v):
                return v.m(lambda a: a.unsqueeze(2).to_broadcast([128, 16, 64]))
            for d in range(2):
                k.I("dve", "tensor_tensor", out=xdt[d][b].k().m(lambda a: a.rearrange("p (r q) -> p r q", q=64)),
                    in0=x3, in1=bc16(self.dt_all.k(("t", i))[:, i, d * 16:(d + 1) * 16]), op=ALU.mult)
                k.I("pool", "tensor_tensor", out=xw[d][b].k().m(lambda a: a.rearrange("p (r q) -> p r q", q=64)),
                    in0=x3, in1=bc16(wgt[b].k()[:, d * 16:(d + 1) * 16]), op=ALU.mult)
            k.I("pool", "tensor_tensor", out=xsD[b].k().m(lambda a: a.rearrange("p (r q) -> p r q", q=64)),
                in0=x3, in1=bc16(D_bc.k()), op=ALU.mult)
            py = (PS[0], PS[1])

            def emitE(g):
                nonlocal nE
                for d in range(2):
                    pE = PS[2 + (nE % 4)]
                    Et = E[nE % 4]
                    nE += 1
                    for rr in range(4):
                        col = d * 16 + g * 4 + rr
                        o = pE.k()[:, rr * 128:(rr + 1) * 128]
                        k.mm(o, ahl[b].k()[:, col:col + 1].m(lambda a: a.to_broadcast([128, 128])), tri_d[d].k(),
                             start=True, stop=False)
                        k.mm(o, ahl[b].k()[:, 32 + col:33 + col].m(lambda a: a.to_broadcast([128, 128])), tri_d[d].k(),
                             start=False, stop=False)
                        k.mm(o, self.identb.k(), nm_d[d].k(), start=False, stop=True)
                    for rr in range(4):
                        col = d * 16 + g * 4 + rr
                        k.I("act", "activation", out=Et.k()[:, rr * 128:(rr + 1) * 128], in_=pE.k()[:, rr * 128:(rr + 1) * 128],
                            func=AF.Exp, bias=ncs[b].k()[:, col:col + 1])
                    k.I("dve", "tensor_tensor", out=M[d][g % 2].k().m(lambda a: a.rearrange("p (r q) -> p r q", q=128)),
                        in0=Et.k().m(lambda a: a.rearrange("p (r q) -> p r q", q=128)),
                        in1=cbT[b].k()[:, g * 128:(g + 1) * 128].m(lambda a: a.unsqueeze(1).to_broadcast([128, 4, 128])),
                        op=ALU.mult)

            def emitY(g):
                for rr in range(4):
                    r = g * 4 + rr
                    o = py[r // 8].k()[:, (r % 8) * 64:(r % 8 + 1) * 64]
                    k.mm(o, M[0][g % 2].k()[:, rr * 128:(rr + 1) * 128], xdt[0][b].k()[:, r * 64:(r + 1) * 64], start=True, stop=False)
                    k.mm(o, M[1][g % 2].k()[:, rr * 128:(rr + 1) * 128], xdt[1][b].k()[:, r * 64:(r + 1) * 64], start=False, stop=True)

            emitE(0)
            for g in range(4):
                if g + 1 < 4:
                    emitE(g + 1)
                emitY(g)
            for hh in range(2):
                k.I("dve", "tensor_tensor", out=yst[b].k()[:, hh * 512:(hh + 1) * 512], in0=py[hh].k(),
                    in1=xsD[b].k()[:, hh * 512:(hh + 1) * 512], op=ALU.add)
            k.dma(self.yloc.k(("t", i))[sl, :], yst[b].k())
            for d in range(2):
                k.dma(self.xw_s[d].k(("t", i))[sl, :], xw[d][b].k())
            k.dma(self.btok_s.k(("t", i))[sl, :], btk[b].k())
        k.release()

    def passC(self):
        k = self.k
        k.mark()
        PS = self.PS
        H = [k.sb(f"H{d}", [128, 1024], F32) for d in range(2)]
        Hb = [k.sb(f"Hb{d}", [128, 1024], BF16) for d in range(2)]
        tmpH = [k.sb(f"tmpH{d}", [128, 1024], F32) for d in range(2)]
        CT = [[k.sb(f"CT{d}{i}", [128, 4, 128], BF16) for i in range(2)] for d in range(2)]
        Bt = [[k.sb(f"Bt{d}{i}", [128, 512], BF16) for i in range(2)] for d in range(2)]
        xwt = [[k.sb(f"xwt{d}{i}", [128, 1024], BF16) for i in range(2)] for d in range(2)]
        yo = [[k.sb(f"yo{d}{i}", [128, 1024], F32) for i in range(2)] for d in range(2)]
        hout = k.sb("hout", [128, 8, 128], F32)
        h0src = (self.hf0, self.hb0)
        oss = (self.o_ssdf, self.o_ssdb)
        xcv = self.xcT.k().m(lambda a: a.rearrange("(j p) t -> p j t", p=128))

        def bc16(v):
            return v.m(lambda a: a.unsqueeze(2).to_broadcast([128, 16, 64]))

        def v3(v):
            return v.m(lambda a: a.rearrange("p (r q) -> p r q", q=64))
        for sidx, (t0, T, cond) in enumerate(SEQS):
            nch = T // 128
            cb_ = t0 // 128
            for d in range(2):
                if sidx == 0:
                    k.dma(H[d].k(), h0src[d].k())
                else:
                    k.I("pool", "memset", wr=("ap",), ap=H[d].k(), constant=0.0)
                k.I("act", "activation", out=Hb[d].k(), in_=H[d].k(), func=AF.Copy)
            def c_loads(st):
                b = st % 2
                for d in range(2):
                    c = cb_ + (st if d == 0 else nch - 1 - st)
                    sl = slice(c * 128, (c + 1) * 128)
                    k.dma(CT[d][b].k(), V(self.xcT.res, None, xcv.ap[:, 12:16, sl]))
                    k.dma(Bt[d][b].k(), self.btok_s.k(("t", c))[sl, :])
                    k.dma(xwt[d][b].k(), self.xw_s[d].k(("t", c))[sl, :])
            c_loads(0)
            for st in range(nch):
                b = st % 2
                if st + 1 < nch:
                    c_loads(st + 1)
                for d in range(2):
                    c = cb_ + (st if d == 0 else nch - 1 - st)
                    sl = slice(c * 128, (c + 1) * 128)
                    pY = (PS[4 * d], PS[4 * d + 1])
                    pS = (PS[4 * d + 2], PS[4 * d + 3])
                    for g in range(4):
                        k.mm(pY[g // 2].k()[:, (g % 2) * 256:(g % 2 + 1) * 256], CT[d][b].k()[:, g, :],
                             Hb[d].k()[:, g * 256:(g + 1) * 256])
                    for hh in range(2):
                        k.I("dve", "tensor_tensor", out=v3(yo[d][b].k()[:, hh * 512:(hh + 1) * 512]), in0=v3(pY[hh].k()),
                            in1=self.e_all.k(("t", c))[:, c, d * 16 + hh * 8:d * 16 + hh * 8 + 8].m(
                                lambda a: a.unsqueeze(2).to_broadcast([128, 8, 64])), op=ALU.mult)
                    k.dma(self.yoff[d].k(("t", c))[sl, :], yo[d][b].k())
                    for g in range(4):
                        k.mm(pS[g // 2].k()[:, (g % 2) * 256:(g % 2 + 1) * 256], Bt[d][b].k()[:, g * 128:(g + 1) * 128],
                             xwt[d][b].k()[:, g * 256:(g + 1) * 256])
                    k.I("pool", "tensor_tensor", out=v3(tmpH[d].k()), in0=v3(H[d].k()),
                        in1=bc16(self.dec_all.k(("t", c))[:, c, d * 16:(d + 1) * 16]), op=ALU.mult)
                    for hh in range(2):
                        k.I("dve", "tensor_tensor", out=H[d].k()[:, hh * 512:(hh + 1) * 512], in0=pS[hh].k(),
                            in1=tmpH[d].k()[:, hh * 512:(hh + 1) * 512], op=ALU.add)
                    k.I("act", "activation", out=Hb[d].k(), in_=H[d].k(), func=AF.Copy)
            if sidx > 0:
                pi = sidx - 1
                for d in range(2):
                    for half in range(2):
                        ps = PS[4 * d + half]
                        for q in range(4):
                            j = half * 4 + q
                            k.tr(ps.k()[:, q * 128:(q + 1) * 128], H[d].k()[:, j * 128:(j + 1) * 128], self.ident.k())
                        k.I("act", "activation", out=hout.k()[:, half * 4:half * 4 + 4, :].m(
                            lambda a: a.rearrange("p j n -> p (j n)")), in_=ps.k(), func=AF.Copy)
                    k.dma(oss[d].k(("p", pi))[pi].m(lambda a: a.rearrange("(j p) n -> p j n", p=128)), hout.k(), final=True)
        k.release()

    def passD(self):
        k = self.k
        k.mark()
        PS = self.PS
        wout = k.sb("wout", [128, 16, 1024], BF16)
        ng = k.sb("ng", [128, 8], F32)
        for q in range(4):
            k.dma(wout.k(("q", q))[:, q * 4:(q + 1) * 4, :], self.ev_w_out.k()[:, q * 4:(q + 1) * 4, :], eng="pool")
        k.dma(ng.k(), self.norm_g.k())

        def mk(name, shape, dt_):
            return [k.sb(f"{name}{i}", shape, dt_) for i in range(2)]
        def mk3(name, shape, dt_):
            return [k.sb(f"{name}{i}", shape, dt_) for i in range(3)]
        y0 = mk3("dy0", [128, 1024], F32)
        y1 = mk3("dy1", [128, 1024], F32)
        y2 = mk3("dy2", [128, 1024], F32)
        za = mk3("dza", [128, 1024], F32)
        xr = mk3("dxr", [128, 1024], F32)
        sq = k.sb("dsq", [128, 1024], F32)
        gnb = mk("dgn", [128, 1024], BF16)
        ysT = mk("dysT", [128, 8, 128], BF16)
        ypt = mk3("dyp", [128, 8, 128], BF16)
        ss = mk("dss", [128, 1], F32)
        rs = mk("drs", [128, 1], F32)
        rstd = mk("drstd", [128, 1], F32)
        xo = mk("dxo", [128, 1024], F32)
        ypv = self.ypT.k().m(lambda a: a.rearrange("(j p) t -> p j t", p=128))

        def loads(i):
            b = i % 3
            sl = slice(i * 128, (i + 1) * 128)
            k.dma(y0[b].k(), self.yloc.k(("t", i))[sl, :])
            k.dma(y1[b].k(), self.yoff[0].k(("t", i))[sl, :])
            k.dma(y2[b].k(), self.yoff[1].k(("t", i))[sl, :])
            k.dma(za[b].k(), self.sza.k()[sl, :])
            k.dma(xr[b].k(), self.x_all.k()[sl, :])
            k.dma(ypt[b].k(), V(self.ypT.res, None, ypv.ap[:, :, sl]))

        def stage1(i):
            b = i % 2
            b3 = i % 3
            k.I("pool", "tensor_tensor", out=y1[b3].k(), in0=y1[b3].k(), in1=y2[b3].k(), op=ALU.add)
            k.I("dve", "tensor_tensor", out=y0[b3].k(), in0=y0[b3].k(), in1=y1[b3].k(), op=ALU.add)
            k.I("dve", "tensor_tensor", out=y0[b3].k(), in0=y0[b3].k(), in1=za[b3].k(), op=ALU.mult)
            k.I("act", "activation", out=sq.k(), in_=y0[b3].k(), func=AF.Square, accum_out=ss[b].k())
            k.I("act", "activation", out=rs[b].k(), in_=ss[b].k(), func=AF.Sqrt, scale=1.0 / 1024, bias=self.epsc.k())
            k.I("dve", "reciprocal", out=rstd[b].k(), in_=rs[b].k())
            k.I("dve", "tensor_scalar", out=gnb[b].k(), in0=y0[b3].k(), scalar1=rstd[b].k(), scalar2=None, op0=ALU.mult)
            pt = self.psbf(PS[4 + b])
            for j in range(8):
                k.tr(pt[:, j * 128:(j + 1) * 128], gnb[b].k()[:, j * 128:(j + 1) * 128], self.identb.k())
            k.I("dve", "tensor_tensor", out=ysT[b].k(), in0=pt.m(lambda a: a.rearrange("p (j t) -> p j t", t=128)),
                in1=ng.k().m(lambda a: a.unsqueeze(2).to_broadcast([128, 8, 128])), op=ALU.mult)

        def stage2(i):
            b = i % 2
            b3 = i % 3
            cond = 0 if i < 32 else 1
            sl = slice(i * 128, (i + 1) * 128)
            for nb in range(2):
                po = PS[2 * b + nb]
                for kk in range(16):
                    lhs = ysT[b].k()[:, kk, :] if kk < 8 else ypt[b3].k()[:, kk - 8, :]
                    k.mm(po.k(), lhs, wout.k()[:, kk, nb * 512:(nb + 1) * 512], start=(kk == 0), stop=(kk == 15))
                k.I("dve", "tensor_tensor", out=xo[b].k()[:, nb * 512:(nb + 1) * 512], in0=po.k(),
                    in1=self.gate_bc[0][cond].k()[:, nb * 512:(nb + 1) * 512], op=ALU.mult)
            k.I("pool", "tensor_tensor", out=xo[b].k(), in0=xo[b].k(), in1=xr[b3].k(), op=ALU.add)
            k.dma(self.x1.k(("t", i))[sl, :], xo[b].k())

        loads(0)
        loads(1)
        stage1(0)
        for i in range(NT):
            if i + 2 < NT:
                loads(i + 2)
            if i + 1 < NT:
                stage1(i + 1)
            stage2(i)
        k.release()

    def passE(self):
        k = self.k
        hT = self.hT
        k.mark()
        cosT = k.sb("cosT", [128, 4096], F32)
        sinT = k.sb("sinT", [128, 4096], F32)
        for q in range(4):
            k.dma(cosT.k(("q", q))[:, q * 1024:(q + 1) * 1024], self.c_cos.k()[:, q * 1024:(q + 1) * 1024])
            k.dma(sinT.k(("q", q))[:, q * 1024:(q + 1) * 1024], self.c_sin.k()[:, q * 1024:(q + 1) * 1024])
        wfm = [k.sb(f"wfm{i}", [128, 8, 128], BF16) for i in range(3)]
        qbf = [k.sb(f"eqb{i}", [128, 512], BF16) for i in range(2)]
        pmb = k.sb("pmb", [128, 128], BF16)
        k.dma(pmb.k(), self.c_pm.k(), eng="pool")
        wtm = [k.sb(f"wtm{i}", [128, 8, 512], BF16) for i in range(2)]
        t1 = [k.sb(f"et1{i}", [128, 512], F32) for i in range(2)]
        t2 = [k.sb(f"et2{i}", [128, 512], F32) for i in range(2)]
        sb16 = [k.sb(f"esb{i}", [128, 512], BF16) for i in range(3)]
        sf32 = [k.sb(f"esf{i}", [128, 512], F32) for i in range(3)]
        cnt = {"w": 0, "b": 0, "f": 0, "t": 0, "e": 0}

        pend = []

        def fm_rope(col0, rcol0, ntiles, dst):
            for j in range(ntiles):
                W = wfm[cnt["w"] % 3]
                cnt["w"] += 1
                k.dma(W.k(), self.od_w_in.k()[:, :, col0 + j * 128:col0 + (j + 1) * 128], eng="pool")
                for si, (t0, n) in enumerate(SEGS):
                    pa = self.next_ps()
                    for kk in range(8):
                        k.mm(pa.k()[:, 0:n], W.k()[:, kk, :], hT.k()[:, kk, t0:t0 + n], start=(kk == 0), stop=(kk == 7))
                    while pend:
                        pend.pop(0)()
                    st = sb16[cnt["b"] % 3]
                    cnt["b"] += 1
                    if si < 8:
                        qb16 = qbf[cnt["t"] % 2]
                        a = t1[cnt["t"] % 2]
                        b_ = t2[cnt["t"] % 2]
                        cnt["t"] += 1
                        k.I("act", "activation", out=qb16.k()[:, 0:n], in_=pa.k()[:, 0:n], func=AF.Copy)
                        k.I("dve", "tensor_tensor", out=a.k()[:, 0:n], in0=pa.k()[:, 0:n], in1=cosT.k()[:, t0:t0 + n], op=ALU.mult,
                            extra_r=[qb16.k()])

                        def tail(qb16=qb16, a=a, b_=b_, st=st, t0=t0, n=n, j=j, si=si):
                            pb = self.next_ps()
                            k.mm(pb.k()[:, 0:n], pmb.k(), qb16.k()[:, 0:n])
                            k.I("dve", "tensor_tensor", out=b_.k()[:, 0:n], in0=pb.k()[:, 0:n], in1=sinT.k()[:, t0:t0 + n], op=ALU.mult)
                            k.I("pool", "tensor_tensor", out=st.k()[:, 0:n], in0=a.k()[:, 0:n], in1=b_.k()[:, 0:n], op=ALU.add)
                            k.dma(dst.k(("f", j, si))[j * 128:(j + 1) * 128, t0:t0 + n], st.k()[:, 0:n])
                        pend.append(tail)
                    else:
                        k.I("act", "activation", out=st.k()[:, 0:n], in_=pa.k()[:, 0:n], func=AF.Copy)
                        k.dma(dst.k(("f", j, si))[j * 128:(j + 1) * 128, t0:t0 + n], st.k()[:, 0:n])
            while pend:
                pend.pop(0)()

        def fm_silu(col0, ntiles, dst):
            for j in range(ntiles):
                W = wfm[cnt["w"] % 3]
                cnt["w"] += 1
                k.dma(W.k(), self.od_w_in.k()[:, :, col0 + j * 128:col0 + (j + 1) * 128], eng="pool")
                for si, (t0, n) in enumerate(SEGS):
                    pa = self.next_ps()
                    for kk in range(8):
                        k.mm(pa.k()[:, 0:n], W.k()[:, kk, :], hT.k()[:, kk, t0:t0 + n], start=(kk == 0), stop=(kk == 7))
                    st = sf32[cnt["f"] % 3]
                    cnt["f"] += 1
                    k.I("act", "activation", out=st.k()[:, 0:n], in_=pa.k()[:, 0:n], func=AF.Silu)
                    k.dma(dst.k(("f", j, si))[j * 128:(j + 1) * 128, t0:t0 + n], st.k()[:, 0:n])

        def tm(col0, width, tiles, sinks):
            W = wtm[cnt["e"] % 2]
            cnt["e"] += 1
            k.dma(W.k()[:, :, 0:width], self.od_w_in.k()[:, :, col0:col0 + width], eng="pool")
            for i in tiles:
                pa = self.next_ps()
                for kk in range(8):
                    k.mm(pa.k()[:, 0:width], hT.k(("t", i))[:, kk, i * 128:(i + 1) * 128], W.k()[:, kk, 0:width],
                         start=(kk == 0), stop=(kk == 7))
                for (c0, c1, d16, dcol, d32, fcol, frow) in sinks:
                    if d16 is not None and self.stop_after != "E0b":
                        st = sb16[cnt["b"] % 3]
                        cnt["b"] += 1
                        k.I("act", "activation", out=st.k()[:, 0:c1 - c0], in_=pa.k()[:, c0:c1], func=AF.Copy)
                        k.dma(d16.k(("t", i, dcol))[i * 128:(i + 1) * 128, dcol:dcol + c1 - c0], st.k()[:, 0:c1 - c0])
                    if d32 is not None and i >= 32 and self.stop_after != "E0a":
                        st = sf32[cnt["f"] % 3]
                        cnt["f"] += 1
                        k.I("act", "activation", out=st.k()[:, 0:c1 - c0], in_=pa.k()[:, c0:c1], func=AF.Copy)
                        r0 = (i - 32) * 128
                        k.dma(d32.k(("t", i, fcol))[r0:r0 + 128, fcol:fcol + c1 - c0], st.k()[:, 0:c1 - c0], final=True)

        allt = list(range(NT))
        pt = list(range(32, NT))
        if self.stop_after == "E00":
            k.release()
            return
        for fb in range(2):
            tm(2048 + fb * 512, 512, allt, [(0, 512, self.Vc, fb * 512, self.o_dv, fb * 512, 0)])
        if self.stop_after in ("E0", "E0a", "E0b"):
            k.release()
            return
        tm(5376, 256, allt, [(0, 256, self.Vd, 0, self.o_wv, 0, 0)])
        for fb in range(2):
            tm(1024 + fb * 512, 512, pt, [(0, 512, None, 0, self.o_dk, fb * 512, 0)])
        tm(5120, 256, pt, [(0, 256, None, 0, self.o_wk, 0, 0)])
        if self.stop_after == "E1":
            k.release()
            return
        fm_rope(0, 0, 8, self.QcT)
        if self.stop_after == "E2":
            k.release()
            return
        fm_rope(1024, 1024, 8, self.KcT)
        fm_rope(4096, 2048, 8, self.QdT)
        fm_rope(5120, 3072, 2, self.KdT)
        fm_silu(3072, 8, self.szcT)
        fm_silu(5632, 8, self.szdT)
        k.release()

    def passF(self):
        k = self.k
        k.mark()
        PS = self.PS
        SC = 64 ** -0.5
        onesb = k.sb("f_ones", [128, 128], BF16)
        k.I("dve", "memset", wr=("ap",), ap=onesb.k(), constant=1.0)
        lam4 = k.sb("lam4", [128, 4, 64], F32)
        lp = k.sb("lamp", [128, 2, 64], F32)
        ls = k.sb("lams", [128, 2], F32)
        nlam = k.sb("nlam", [128, 1], F32)
        gsub = k.sb("gsub", [128, 1], F32)
        k.dma(lam4.k(), self.od_lambda.k().m(lambda a: a.partition_broadcast(128)).m(
            lambda a: a.rearrange("p o (a b) -> p (o a) b", b=64)))
        k.I("dve", "tensor_tensor", out=lp.k()[:, 0, :], in0=lam4.k()[:, 0, :], in1=lam4.k()[:, 1, :], op=ALU.mult)
        k.I("dve", "tensor_tensor", out=lp.k()[:, 1, :], in0=lam4.k()[:, 2, :], in1=lam4.k()[:, 3, :], op=ALU.mult)
        k.I("dve", "tensor_reduce", out=ls.k(), in_=lp.k(), axis=mybir.AxisListType.X, op=ALU.add)
        k.I("act", "activation", out=ls.k(), in_=ls.k(), func=AF.Exp)
        k.I("dve", "tensor_tensor", out=nlam.k(), in0=ls.k()[:, 1:2], in1=ls.k()[:, 0:1], op=ALU.subtract)
        k.I("dve", "tensor_scalar", out=nlam.k(), in0=nlam.k(), scalar1=-LAM_INIT, scalar2=None, op0=ALU.add)
        k.dma(gsub.k(), self.subln_g.k())
        k.I("dve", "tensor_scalar", out=gsub.k(), in0=gsub.k(), scalar1=1.0 - LAM_INIT, scalar2=None, op0=ALU.mult)

        KT = [k.sb(f"fKT{i}", [128, 4608], BF16) for i in range(2)]
        Vh = [k.sb(f"fVh{i}", [128, 36, 128], BF16) for i in range(2)]
        QT = [k.sb(f"fQT{i}", [128, 512], BF16) for i in range(2)]
        szc = [k.sb(f"fszc{i}", [128, 512], F32) for i in range(3)]
        P = [k.sb(f"fP{i}", [128, 512], BF16) for i in range(8)]
        ones32 = k.sb("f_ones32", [128, 32], BF16)
        k.I("dve", "memset", wr=("ap",), ap=ones32.k(), constant=1.0)
        c32 = k.sb("f_c32", [64, 128], F32)
        k.I("pool", "memset", wr=("ap",), ap=c32.k(), constant=1.0 / 32)
        zsb = k.sb("fzsb", [64, 512], F32)
        osb = [k.sb(f"fosb{i}", [128, 512], F32) for i in range(2)]
        rz = [k.sb(f"frz{i}", [128, 512], F32) for i in range(2)]
        ta = k.sb("fta", [128, 512], F32)
        tb = k.sb("ftb", [128, 512], F32)
        oc = k.sb("foc", [128, 512], F32)
        sq = k.sb("fsq", [128, 512], BF16)
        rs = k.sb("frs", [128, 512], F32)
        ost = [k.sb(f"fost{i}", [128, 512], BF16) for i in range(2)]
        cnt = {"P": 0, "S": 0}
        nq_ = 0
        nh = 0
        pO = (PS[0], PS[1])
        pZ = PS[2]
        pX = PS[3]
        heads = []
        qitems = []
        for sidx, (t0, T, cond) in enumerate(SEQS):
            nctx = 4 if sidx == 0 else 0
            qblocks = [(t0 + q * 512, 512) for q in range(T // 512)] if T >= 512 else [(t0, T)]
            for h in range(8):
                heads.append((sidx, t0, T, nctx, h))
                for qi, (q0, nq) in enumerate(qblocks):
                    qitems.append((len(heads) - 1, q0, nq, qi == 0))

        def head_loads(hidx):
            sidx, t0, T, nctx, h = heads[hidx]
            kt_, vh_ = KT[hidx % 2], Vh[hidx % 2]
            if nctx:
                k.dma(kt_.k(("c",))[:, 0:512], self.dk_T.k()[h], eng="pool")
                k.dma(vh_.k(("c",))[:, 0:4, :], self.dv.k()[h].m(lambda a: a.rearrange("(j p) v -> p j v", p=128)), eng="pool")
            for c0 in range(0, T, 2048):
                cw_ = min(2048, T - c0)
                k.dma(kt_.k(("o", c0))[:, nctx * 128 + c0:nctx * 128 + c0 + cw_],
                      self.KcT.k()[h * 128:(h + 1) * 128, t0 + c0:t0 + c0 + cw_])
            for j0 in range(0, T // 128, 8):
                jn = min(8, T // 128 - j0)
                k.dma(vh_.k(("o", j0))[:, nctx + j0:nctx + j0 + jn, :],
                      self.Vc.k()[t0 + j0 * 128:t0 + (j0 + jn) * 128, h * 128:(h + 1) * 128].m(
                          lambda a: a.rearrange("(j p) v -> p j v", p=128)))

        def q_loads(qidx):
            hidx, q0, nq, first = qitems[qidx]
            h = heads[hidx][4]
            k.dma(QT[qidx % 2].k()[:, 0:nq], self.QcT.k()[h * 128:(h + 1) * 128, q0:q0 + nq])
            k.dma(szc[qidx % 3].k()[:, 0:nq], self.szcT.k()[h * 128:(h + 1) * 128, q0:q0 + nq])

        head_loads(0)
        q_loads(0)
        deferred = []
        PsAll = {}
        if True:
            if True:
                for qidx, (hidx, q0, nq, first) in enumerate(qitems):
                    sidx, t0, T, nctx, h = heads[hidx]
                    nkt = nctx + T // 128
                    kt_, vh_ = KT[hidx % 2], Vh[hidx % 2]
                    qb = qidx % 2
                    if first and hidx + 1 < len(heads):
                        head_loads(hidx + 1)
                    if qidx + 1 < len(qitems):
                        q_loads(qidx + 1)
                    def emit_qk(qi2, kt):
                        hidx2, q02, nq2, first2 = qitems[qi2]
                        nctx2 = heads[hidx2][3]
                        ktile2 = KT[hidx2 % 2]
                        kkey = ("c",) if kt < nctx2 else ("o", ((kt - nctx2) * 128) // 2048 * 2048)
                        lst = []
                        for i in range(2):
                            pS = PS[4 + cnt["S"] % 4]
                            cnt["S"] += 1
                            k.mm(pS.k()[:, 0:nq2], ktile2.k(kkey)[i * 64:(i + 1) * 64, kt * 128:(kt + 1) * 128],
                                 QT[qi2 % 2].k()[i * 64:(i + 1) * 64, 0:nq2])
                            Pt = P[cnt["P"] % 8]
                            cnt["P"] += 1
                            k.I("act", "activation", out=Pt.k()[:, 0:nq2], in_=pS.k()[:, 0:nq2], func=AF.Exp, scale=SC)
                            lst.append(Pt)
                        PsAll[(qi2, kt)] = lst
                    if qidx == 0:
                        emit_qk(0, 0)
                    Ps = {}
                    for kt in range(nkt):
                        if kt + 1 < nkt:
                            emit_qk(qidx, kt + 1)
                        elif qidx + 1 < len(qitems):
                            emit_qk(qidx + 1, 0)
                        Ps[kt] = PsAll.pop((qidx, kt))
                        vkey = ("c",) if kt < nctx else ("o", (kt - nctx) // 8 * 8)
                        for i in range(2):
                            k.mm(pO[i].k()[:, 0:nq], vh_.k(vkey)[:, kt, :], Ps[kt][i].k()[:, 0:nq],
                                 start=(kt == 0), stop=(kt == nkt - 1))
                        for i in range(2):
                            k.mm(pZ.k()[32 * i:32 * i + 32, 0:nq], ones32.k(), Ps[kt][i].k()[:, 0:nq],
                                 start=(kt == 0), stop=(kt == nkt - 1), tile_position=(0, 32 * i))
                        del Ps[kt]
                        while deferred and deferred[0][0] <= kt:
                            deferred.pop(0)[1]()
                    while deferred:
                        deferred.pop(0)[1]()
                    k.I("dve", "tensor_copy", out=osb[0].k()[:, 0:nq], in_=pO[0].k()[:, 0:nq])
                    k.I("act", "activation", out=osb[1].k()[:, 0:nq], in_=pO[1].k()[:, 0:nq], func=AF.Copy)
                    k.I("dve", "tensor_copy", out=zsb.k()[:, 0:nq], in_=pZ.k()[0:64, 0:nq])

                    def mk_tail(h=h, q0=q0, nq=nq, qidx=qidx):
                        szc_ = szc[qidx % 3]
                        ot = ost[qidx % 2]

                        def d1():
                            k.I("dve", "reciprocal", out=zsb.k()[:, 0:nq], in_=zsb.k()[:, 0:nq])

                        def d2():
                            k.mm(pX.k()[:, 0:nq], c32.k()[0:32, :], zsb.k()[0:32, 0:nq])
                            k.I("dve", "tensor_tensor", out=ta.k()[:, 0:nq], in0=osb[0].k()[:, 0:nq], in1=pX.k()[:, 0:nq], op=ALU.mult)

                        def d3():
                            k.mm(pX.k()[:, 0:nq], c32.k()[32:64, :], zsb.k()[32:64, 0:nq])
                            k.I("dve", "tensor_tensor", out=tb.k()[:, 0:nq], in0=osb[1].k()[:, 0:nq], in1=pX.k()[:, 0:nq], op=ALU.mult)
                            k.I("dve", "scalar_tensor_tensor", out=oc.k()[:, 0:nq], in0=tb.k()[:, 0:nq], scalar=nlam.k(),
                                in1=ta.k()[:, 0:nq], op0=ALU.mult, op1=ALU.add)

                        def d4():
                            k.I("act", "activation", out=sq.k()[:, 0:nq], in_=oc.k()[:, 0:nq], func=AF.Square)

                        def d5():
                            k.mm(pX.k()[:, 0:nq], onesb.k(), sq.k()[:, 0:nq])

                        def d6():
                            k.I("act", "activation", out=rs.k()[:, 0:nq], in_=pX.k()[:, 0:nq], func=AF.Sqrt, scale=1.0 / 128,
                                bias=self.epsc.k())
                            k.I("dve", "reciprocal", out=rs.k()[:, 0:nq], in_=rs.k()[:, 0:nq])
                            k.I("dve", "tensor_tensor", out=oc.k()[:, 0:nq], in0=oc.k()[:, 0:nq], in1=rs.k()[:, 0:nq], op=ALU.mult)
                            k.I("dve", "scalar_tensor_tensor", out=ot.k()[:, 0:nq], in0=oc.k()[:, 0:nq], scalar=gsub.k(),
                                in1=szc_.k()[:, 0:nq], op0=ALU.mult, op1=ALU.mult)
                            k.dma(self.ocT.k(("h", h, q0))[h * 128:(h + 1) * 128, q0:q0 + nq], ot.k()[:, 0:nq])
                        return [(1, d1), (3, d2), (6, d3), (9, d4), (12, d5), (15, d6)]
                    deferred.extend(mk_tail())
        while deferred:
            deferred.pop(0)[1]()
        k.release()

    def passG(self):
        k = self.k
        k.mark()
        PS = self.PS
        SC = 64 ** -0.5
        onesb = k.sb("g_ones", [128, 64], BF16)
        k.I("dve", "memset", wr=("ap",), ap=onesb.k(), constant=1.0)
        nmp = k.sb("g_nmp", [128, 128], BF16)
        nmn = k.sb("g_nmn", [128, 128], BF16)
        k.dma(nmp.k(), self.c_nmb.k(), eng="pool")
        k.dma(nmn.k(), self.c_nmf.k(), eng="pool")
        esink = k.sb("esink", [128, 16], F32)
        k.dma(esink.k(), self.sink.k().m(lambda a: a.partition_broadcast(128)))
        k.I("act", "activation", out=esink.k(), in_=esink.k(), func=AF.Exp)
        KT = [k.sb(f"gKT{i}", [128, 4608], BF16) for i in range(2)]
        Vh = [k.sb(f"gVh{i}", [128, 36, 128], BF16) for i in range(2)]
        for t in Vh:
            k.I("pool", "memset", wr=("ap",), ap=t.k(), constant=1.0)
        c64 = k.sb("g_c64", [128, 64], F32)
        k.I("pool", "memset", wr=("ap",), ap=c64.k(), constant=1.0 / 64)
        QA = [k.sb(f"gQA{i}", [128, 4096], BF16) for i in range(2)]
        QB = [k.sb(f"gQB{i}", [128, 4096], BF16) for i in range(2)]
        P = [k.sb(f"gP{i}", [128, 512], BF16) for i in range(4)]
        szd = [k.sb(f"gszd{i}", [64, 4, 128], F32) for i in range(3)]
        osb = k.sb("gosb", [128, 512], F32)
        zmv = k.sb("gzmv", [64, 512], F32)
        zt = k.sb("gzt", [64, 512], F32)
        od = k.sb("god", [64, 512], F32)
        ost = [k.sb(f"gost{i}", [64, 4, 128], BF16) for i in range(2)]
        gcnt = {"P": 0, "S": 0}

        def pv(v):
            return v.m(lambda a: a.rearrange("p (r h q) -> p r h q", r=2, h=2))

        def gv(v):
            return v.m(lambda a: a.rearrange("p (h r) q -> p r h q", r=2))

        heads = []
        items = []
        for sidx, (t0, T, cond) in enumerate(SEQS):
            nctx = 4 if sidx == 0 else 0
            nblk = T // 128
            for kv in range(4):
                heads.append((sidx, t0, T, nctx, nblk, kv))
                for blk in range(nblk):
                    items.append((len(heads) - 1, blk))

        def head_loads(hidx):
            sidx, t0, T, nctx, nblk, kv = heads[hidx]
            hb = hidx % 2
            kt_, vh_, qa, qb_ = KT[hb], Vh[hb], QA[hb], QB[hb]
            for half in range(2):
                rows = slice(half * 64, (half + 1) * 64)
                if nctx:
                    k.dma(kt_.k(("c", half))[rows, 0:512], self.wk_T.k()[kv], eng="pool")
                for c0 in range(0, T, 2048):
                    cw_ = min(2048, T - c0)
                    k.dma(kt_.k(("o", half, c0))[rows, nctx * 128 + c0:nctx * 128 + c0 + cw_],
                          self.KdT.k()[kv * 64:(kv + 1) * 64, t0 + c0:t0 + c0 + cw_])
            if nctx:
                k.dma(vh_.k(("c",))[:, 0:4, 0:64], self.wv.k()[kv].m(lambda a: a.rearrange("(j p) v -> p j v", p=128)), eng="pool")
            for j0 in range(0, nblk, 8):
                jn = min(8, nblk - j0)
                k.dma(vh_.k(("o", j0))[:, nctx + j0:nctx + j0 + jn, 0:64],
                      self.Vd.k()[t0 + j0 * 128:t0 + (j0 + jn) * 128, kv * 64:(kv + 1) * 64].m(
                          lambda a: a.rearrange("(j p) v -> p j v", p=128)))
            for c0 in range(0, T, 2048):
                cw_ = min(2048, T - c0)
                k.dma(qa.k(("q", c0))[:, c0:c0 + cw_], self.QdT.k()[kv * 256:kv * 256 + 128, t0 + c0:t0 + c0 + cw_])
                k.dma(qb_.k(("q", c0))[:, c0:c0 + cw_], self.QdT.k()[kv * 256 + 128:kv * 256 + 256, t0 + c0:t0 + c0 + cw_])

        def blk_loads(idx):
            hidx, blk = items[idx]
            sidx, t0, T, nctx, nblk, kv = heads[hidx]
            q0 = t0 + blk * 128
            k.dma(szd[idx % 3].k(), self.szdT.k()[kv * 256:(kv + 1) * 256, q0:q0 + 128].m(
                lambda a: a.rearrange("(g d) q -> d g q", d=64)))

        head_loads(0)
        blk_loads(0)
        deferred = []
        ctxs = []
        for idx, (hidx, blk) in enumerate(items):
            sidx, t0, T, nctx, nblk, kv = heads[hidx]
            if sidx == 0:
                kts = [(c, ("c",), None) for c in range(4)]
                for off, msk in ((-1, nmp), (0, None), (1, nmn)):
                    if 0 <= blk + off < nblk:
                        kts.append((nctx + blk + off, ("o",), msk))
            else:
                kts = [(c, ("o",), None) for c in range(nblk)]
            ctxs.append(kts)
        steps = [(idx, n_) for idx in range(len(items)) for n_ in range(len(ctxs[idx]))]
        Pq = {}

        def emit_qk(si):
            idx, n_ = steps[si]
            hidx, blk = items[idx]
            sidx, t0, T, nctx, nblk, kv = heads[hidx]
            hb = hidx % 2
            kt_, qtile = KT[hb], (QA[hb], QB[hb])
            kt, key, msk = ctxs[idx][n_]
            pSA = PS[4 + 2 * (gcnt["S"] % 2)]
            pSB = PS[5 + 2 * (gcnt["S"] % 2)]
            gcnt["S"] += 1
            Pt = P[gcnt["P"] % 4]
            gcnt["P"] += 1
            for g in range(4):
                rows = slice((g % 2) * 64, (g % 2 + 1) * 64)
                pS = pSA if g % 2 == 0 else pSB
                o = pS.k()[:, (g // 2) * 128:(g // 2 + 1) * 128]
                kk_ = ("c", g % 2) if key[0] == "c" else ("o", g % 2, ((kt - nctx) * 128) // 2048 * 2048)
                if msk is not None:
                    k.mm(o, self.identb.k(), msk.k(), start=True, stop=False)
                k.mm(o, kt_.k(kk_)[rows, kt * 128:(kt + 1) * 128],
                     qtile[g // 2].k(("q", (blk * 128) // 2048 * 2048))[rows, blk * 128:(blk + 1) * 128],
                     start=(msk is None), stop=True)
            k.I("act", "activation", out=Pt.k()[:, 0:256], in_=pSA.k()[:, 0:256], func=AF.Exp, scale=SC)
            k.I("act", "activation", out=Pt.k()[:, 256:512], in_=pSB.k()[:, 0:256], func=AF.Exp, scale=SC)
            Pq[si] = Pt

        emit_qk(0)
        for si, (idx, n_) in enumerate(steps):
            hidx, blk = items[idx]
            sidx, t0, T, nctx, nblk, kv = heads[hidx]
            hb = hidx % 2
            vh_ = Vh[hb]
            kts = ctxs[idx]
            if n_ == 0:
                if blk == 0 and hidx + 1 < len(heads):
                    head_loads(hidx + 1)
                if idx + 1 < len(items):
                    blk_loads(idx + 1)
            q0 = t0 + blk * 128
            frow = slice(kv * 256, (kv + 1) * 256)
            pO = PS[idx % 2]
            if si + 1 < len(steps):
                emit_qk(si + 1)
            kt, key, msk = kts[n_]
            Pt = Pq.pop(si)
            vkey = ("c",) if key[0] == "c" else ("o", (kt - nctx) // 8 * 8)
            k.mm(pO.k(), vh_.k(vkey)[:, kt, :], Pt.k(), start=(n_ == 0), stop=(n_ == len(kts) - 1))
            while deferred and deferred[0][0] <= n_:
                deferred.pop(0)[1]()
            if n_ < len(kts) - 1:
                continue
            while deferred:
                deferred.pop(0)[1]()

            def mk_tail(pO=pO, kv=kv, q0=q0, frow=frow, idx=idx):
                szd_ = szd[idx % 3]
                ost_ = ost[idx % 2]

                def d0():
                    k.I("dve", "tensor_copy", out=osb.k(), in_=pO.k())
                    k.dma(zmv.k(), osb.k()[64:128, :], semkey=("zmv",))

                def d1():
                    k.I("dve", "tensor_tensor", out=pv(zt.k()), in0=pv(zmv.k()),
                        in1=esink.k()[0:64, kv * 4:kv * 4 + 4].m(
                            lambda a: a.rearrange("p (h r) -> p r h", r=2).unsqueeze(3).to_broadcast([64, 2, 2, 128])), op=ALU.add)
                    k.I("dve", "reciprocal", out=zt.k(), in_=zt.k())
                    k.I("dve", "tensor_tensor", out=od.k(), in0=osb.k()[0:64, :], in1=zt.k(), op=ALU.mult)

                def d2():
                    k.I("pool", "tensor_tensor", out=gv(ost_.k()), in0=pv(od.k()), in1=gv(szd_.k()), op=ALU.mult)
                    k.dma(self.odT.k(("kv", kv, q0))[frow, q0:q0 + 128].m(lambda a: a.rearrange("(g d) q -> d g q", d=64)), ost_.k())
                return [(-1, d0), (1, d1), (3, d2)]
            tl = mk_tail()
            tl.pop(0)[1]()
            deferred.extend(tl)
        while deferred:
            deferred.pop(0)[1]()
        k.release()

    def passH(self):
        k = self.k
        k.mark()
        PS = self.PS
        wout = k.sb("wout1", [128, 16, 1024], BF16)
        for q in range(4):
            k.dma(wout.k(("q", q))[:, q * 4:(q + 1) * 4, :], self.od_w_out.k()[:, q * 4:(q + 1) * 4, :], eng="pool")
        fg = k.sb("fg", [128, 1024], F32)
        k.dma(fg.k(), self.fnorm_g.k().m(lambda a: a.partition_broadcast(128)))

        def mk(name, shape, dt_):
            return [k.sb(f"{name}{i}", shape, dt_) for i in range(2)]
        oct_ = [k.sb(f"hoc{i}", [128, 8, 128], BF16) for i in range(3)]
        odt_ = [k.sb(f"hod{i}", [128, 8, 128], BF16) for i in range(3)]
        xr = [k.sb(f"hxr{i}", [128, 1024], F32) for i in range(3)]
        xo = mk("hxo", [128, 1024], F32)
        sq = k.sb("hsq", [128, 1024], F32)
        ss = mk("hss", [128, 1], F32)
        rs = mk("hrs", [128, 1], F32)
        rstd = mk("hrstd", [128, 1], F32)
        yo = mk("hyo", [128, 1024], F32)
        ocv = self.ocT.k().m(lambda a: a.rearrange("(j p) t -> p j t", p=128))
        odv = self.odT.k().m(lambda a: a.rearrange("(j p) t -> p j t", p=128))
        def loads(i):
            b = i % 3
            sl = slice(i * 128, (i + 1) * 128)
            k.dma(oct_[b].k(), V(self.ocT.res, None, ocv.ap[:, :, sl]))
            k.dma(odt_[b].k(), V(self.odT.res, None, odv.ap[:, :, sl]))
            k.dma(xr[b].k(), self.x1.k()[sl, :])

        def compute(i):
            b3 = i % 3
            b = i % 2
            cond = 0 if i < 32 else 1
            sl = slice(i * 128, (i + 1) * 128)
            for nb in range(2):
                po = PS[2 * b + nb]
                for kk in range(16):
                    lhs = oct_[b3].k()[:, kk, :] if kk < 8 else odt_[b3].k()[:, kk - 8, :]
                    k.mm(po.k(), lhs, wout.k()[:, kk, nb * 512:(nb + 1) * 512], start=(kk == 0), stop=(kk == 15))
                k.I("dve", "tensor_tensor", out=xo[b].k()[:, nb * 512:(nb + 1) * 512], in0=po.k(),
                    in1=self.gate_bc[1][cond].k()[:, nb * 512:(nb + 1) * 512], op=ALU.mult)
            k.I("pool", "tensor_tensor", out=xo[b].k(), in0=xo[b].k(), in1=xr[b3].k(), op=ALU.add)
            k.I("act", "activation", out=sq.k(), in_=xo[b].k(), func=AF.Square, accum_out=ss[b].k())
            k.I("act", "activation", out=rs[b].k(), in_=ss[b].k(), func=AF.Sqrt, scale=1.0 / D, bias=self.epsc.k())
            k.I("dve", "reciprocal", out=rstd[b].k(), in_=rs[b].k())
            k.I("dve", "scalar_tensor_tensor", out=yo[b].k(), in0=xo[b].k(), scalar=rstd[b].k(), in1=fg.k(), op0=ALU.mult, op1=ALU.mult)
            k.dma(self.y_all.k(("t", i))[sl, :], yo[b].k(), final=True)

        loads(0)
        loads(1)
        for i in range(NT):
            if i + 2 < NT:
                loads(i + 2)
            compute(i)
        k.release()

def make_consts():
    c = {}
    c["c_ident"] = np.eye(128, dtype=np.float32)
    t = np.arange(128)
    c["c_triu"] = (t[:, None] <= t[None, :]).astype(np.float32)
    c["c_tril"] = (t[:, None] >= t[None, :]).astype(np.float32)
    c["c_nmf"] = np.where(t[None, :] < t[:, None], -30000.0, 0.0).astype(np.float32)
    c["c_nmb"] = np.where(t[None, :] > t[:, None], -30000.0, 0.0).astype(np.float32)
    sel = np.zeros((64, 32, 128), np.float32)
    c["c_sel"] = sel
    T = 4096
    row = np.repeat(np.arange(T // 64), 64).astype(np.float32)
    col = np.tile(np.arange(64), T // 64).astype(np.float32)
    nf = 16
    inv = (10000.0 ** (-np.arange(nf, dtype=np.float32) / nf)).astype(np.float32)
    ang = np.stack([row[:, None] * inv, col[:, None] * inv], axis=1)
    cos = np.cos(ang).astype(np.float32)
    sin = np.sin(ang).astype(np.float32)
    ct = np.zeros((64, T), np.float32)
    st = np.zeros((64, T), np.float32)
    for ax in range(2):
        for half in range(2):
            for f in range(nf):
                d = ax * 32 + half * 16 + f
                ct[d] = cos[:, ax, f]
                st[d] = sin[:, ax, f] * (-1.0 if half == 0 else 1.0)
    pm = np.zeros((128, 128), np.float32)
    for fo in range(128):
        d = fo % 32
        fi = fo + 16 if d < 16 else fo - 16
        pm[fi, fo] = 1.0
    c["c_pm"] = pm
    c["c_cos"] = np.concatenate([ct, ct], 0)
    c["c_sin"] = np.concatenate([st, st], 0)
    pe = np.zeros((128, 4, 16), np.float32)
    for g, w in enumerate((2, 4, 8, 16)):
        left = w // 2
        right = w - 1 - left
        for j in range(8):
            pe[:, g, j] = 1.0 / ((j + right + 1) - max(j - left, 0))
            pe[:, g, 8 + j] = 1.0 / (min(right + 1, 8 - j) + left)
    c["c_pedge"] = pe
    return c


def prep_core_inputs(inp, core, consts):
    f = np.float32
    m = {}
    xs = np.asarray(inp["x_sample"][core], f)
    xp = np.asarray(inp["x_prompt"][2 * core:2 * core + 2], f).reshape(512, D)
    m["x_all"] = np.ascontiguousarray(np.concatenate([xs, xp], 0))
    cc = np.concatenate([np.asarray(inp["c"][core], f).reshape(8, 128).T,
                         np.asarray(inp["c_ctx"], f).reshape(8, 128).T], 1)
    m["cc"] = np.ascontiguousarray(cc)
    m.update(consts)
    return m


def prep_shared_inputs(inp):
    f = np.float32
    m = {}
    wa = np.asarray(inp["w_ada"], f)
    m["w_ada"] = np.ascontiguousarray(wa.reshape(2, 8, 128, 3072).transpose(0, 2, 1, 3))
    ba = np.asarray(inp["b_ada"], f)
    m["b_ada_col"] = np.ascontiguousarray(ba.reshape(2, 24, 128).transpose(0, 2, 1))
    m["b_ada_row"] = np.ascontiguousarray(ba)
    m["ev_w_in"] = np.ascontiguousarray(np.asarray(inp["ev_w_in"], f)[0].reshape(8, 128, 5152).transpose(1, 0, 2))
    cw = np.asarray(inp["ev_conv_w"], f)[0]
    m["conv_w"] = np.ascontiguousarray(cw.reshape(5, 16, 128).transpose(2, 1, 0))
    m["conv_b"] = np.ascontiguousarray(np.asarray(inp["ev_conv_b"], f)[0].reshape(16, 128).T)
    m["a_log"] = np.ascontiguousarray(np.asarray(inp["ev_A_log"], f)[0].reshape(1, 32))
    m["dt_bias"] = np.ascontiguousarray(np.asarray(inp["ev_dt_bias"], f)[0].reshape(1, 32))
    m["d_skip"] = np.ascontiguousarray(np.asarray(inp["ev_D"], f).reshape(1, 16))
    m["norm_g"] = np.ascontiguousarray(np.asarray(inp["ev_norm_g"], f)[0].reshape(8, 128).T)
    m["pool_scale"] = np.ascontiguousarray(np.asarray(inp["ev_pool_scale"], f)[0].reshape(8, 128).T)
    pw = np.asarray(inp["ev_pool_w"], f)[0]
    m["pool_w"] = np.ascontiguousarray(pw.reshape(4, 2, 128, 256).transpose(0, 2, 1, 3))
    m["ev_w_out"] = np.ascontiguousarray(np.asarray(inp["ev_w_out"], f)[0].reshape(16, 128, 1024).transpose(1, 0, 2))
    m["od_w_in"] = np.ascontiguousarray(np.asarray(inp["od_w_in"], f)[0].reshape(8, 128, 6656).transpose(1, 0, 2))
    m["od_w_out"] = np.ascontiguousarray(np.asarray(inp["od_w_out"], f)[0].reshape(16, 128, 1024).transpose(1, 0, 2))
    m["od_lambda"] = np.ascontiguousarray(np.asarray(inp["od_lambda"], f)[0].reshape(1, 256))
    m["subln_g"] = np.ascontiguousarray(np.asarray(inp["od_subln_g"], f)[0].reshape(128, 1))
    m["sink"] = np.ascontiguousarray(np.asarray(inp["od_sink"], f)[0].reshape(1, 16))
    m["fnorm_g"] = np.ascontiguousarray(np.asarray(inp["final_norm_g"], f).reshape(1, 1024))
    return m


def prep_core_caches(inp, core, m):
    f = np.float32
    m["hf0"] = np.ascontiguousarray(np.asarray(inp["state_ssd_fwd"], f)[core, 0].reshape(1024, 128).T)
    m["hb0"] = np.ascontiguousarray(np.asarray(inp["state_ssd_bwd"], f)[core, 0].reshape(1024, 128).T)
    m["dk_T"] = np.ascontiguousarray(np.asarray(inp["cache_diff_k"], f)[core, 0].transpose(0, 2, 1))
    m["dv"] = np.ascontiguousarray(np.asarray(inp["cache_diff_v"], f)[core, 0])
    m["wk_T"] = np.ascontiguousarray(np.asarray(inp["cache_win_k"], f)[core, 0].transpose(0, 2, 1))
    m["wv"] = np.ascontiguousarray(np.asarray(inp["cache_win_v"], f)[core, 0])
    return m


_PROGRAM = {}


def _get_program():
    if "nc" not in _PROGRAM:
        b = Builder(debug=False)
        _PROGRAM["nc"] = b.nc
    return _PROGRAM["nc"]


def kernel(**inputs):
    n = 8
    nc = _get_program()
    consts = make_consts()
    shared = prep_shared_inputs(inputs)
    in_maps = []
    for core in range(n):
        m = prep_core_inputs(inputs, core, consts)
        m.update(shared)
        prep_core_caches(inputs, core, m)
        in_maps.append(m)
    res = run_bass_kernel_spmd(nc, in_maps, core_ids=list(range(n)))
    R = res.results
    f = np.float32
    y_prompt = np.zeros((16, 256, D), f)
    y_sample = np.zeros((8, 4096, D), f)
    ssd_f = np.zeros((16, 1, 16, 64, 128), f)
    ssd_b = np.zeros((16, 1, 16, 64, 128), f)
    dk = np.zeros((16, 1, 8, 256, 128), f)
    dv = np.zeros((16, 1, 8, 256, 128), f)
    wk = np.zeros((16, 1, 4, 256, 64), f)
    wv = np.zeros((16, 1, 4, 256, 64), f)
    for c in range(n):
        r = R[c]
        ya = np.asarray(r["y_all"], f)
        y_sample[c] = ya[0:4096]
        y_prompt[2 * c:2 * c + 2] = ya[4096:4608].reshape(2, 256, D)
        ssd_f[2 * c:2 * c + 2, 0] = np.asarray(r["o_ssdf"], f).reshape(2, 16, 64, 128)
        ssd_b[2 * c:2 * c + 2, 0] = np.asarray(r["o_ssdb"], f).reshape(2, 16, 64, 128)
        dk[2 * c:2 * c + 2, 0] = np.asarray(r["o_dk"], f).reshape(2, 256, 8, 128).transpose(0, 2, 1, 3)
        dv[2 * c:2 * c + 2, 0] = np.asarray(r["o_dv"], f).reshape(2, 256, 8, 128).transpose(0, 2, 1, 3)
        wk[2 * c:2 * c + 2, 0] = np.asarray(r["o_wk"], f).reshape(2, 256, 4, 64).transpose(0, 2, 1, 3)
        wv[2 * c:2 * c + 2, 0] = np.asarray(r["o_wv"], f).reshape(2, 256, 4, 64).transpose(0, 2, 1, 3)
    return (y_prompt, y_sample, ssd_f, ssd_b, dk, dv, wk, wv)
```
